# Optimizing a Trainium2 kernel written in Bass

```python
import math
import jax, jax.numpy as jnp
from jax import lax
import numpy as np

D_MODEL = 1024
BATCH = 2
SEQ = 8192
DEPTH = 2

N_MIXERS = 2
N_ATTN_LAYERS = (DEPTH + 1) // 2
N_POOL_LAYERS = DEPTH // 2

HEAD_DIM = 64
N_Q_HEADS = D_MODEL // HEAD_DIM
N_KV_HEADS = 4
GROUP = N_Q_HEADS // N_KV_HEADS
QKV_DIM = (N_Q_HEADS + 2 * N_KV_HEADS) * HEAD_DIM
WINDOW = 128
BLOCK = 128
ROPE_DIM = HEAD_DIM // 4
ROPE_THETA = 500000.0

POOL_WINDOWS = (2, 4, 8, 16)
N_POOL_GROUPS = len(POOL_WINDOWS)
POOL_GROUP_DIM = D_MODEL // N_POOL_GROUPS

D_FF = 2816
CONV_WIDTH = 3

NORM_EPS = 1e-6
NEG_INF = -1e30

kernel_name = "hybrid_swa_sink_pool_convffn_adaln"


def rmsnorm(x, gain):
    xf = x.astype(jnp.float32)
    y = xf * lax.rsqrt(jnp.mean(xf * xf, axis=-1, keepdims=True) + NORM_EPS)
    return (y * gain.astype(jnp.float32)).astype(x.dtype)


def rope_tables(seq_len):
    half = ROPE_DIM // 2
    inv_freq = ROPE_THETA ** (-jnp.arange(0, half, dtype=jnp.float32) * 2.0 / ROPE_DIM)
    pos = jnp.arange(seq_len, dtype=jnp.float32)
    ang = pos[:, None] * inv_freq[None, :]
    return jnp.cos(ang), jnp.sin(ang)


def apply_partial_rope(x, cos, sin):
    half = ROPE_DIM // 2
    xr = x[..., :ROPE_DIM].astype(jnp.float32)
    x1, x2 = xr[..., :half], xr[..., half:]
    rot = jnp.concatenate([x1 * cos - x2 * sin, x2 * cos + x1 * sin], axis=-1)
    return jnp.concatenate([rot.astype(x.dtype), x[..., ROPE_DIM:]], axis=-1)


def swa_sink_attention(h, w_qkv, q_gain, k_gain, sinks, w_o):
    B, S, _ = h.shape
    nb = S // BLOCK
    qkv = h @ w_qkv
    q = qkv[..., :N_Q_HEADS * HEAD_DIM].reshape(B, S, N_KV_HEADS, GROUP, HEAD_DIM)
    k = qkv[..., N_Q_HEADS * HEAD_DIM:(N_Q_HEADS + N_KV_HEADS) * HEAD_DIM].reshape(B, S, N_KV_HEADS, HEAD_DIM)
    v = qkv[..., (N_Q_HEADS + N_KV_HEADS) * HEAD_DIM:].reshape(B, S, N_KV_HEADS, HEAD_DIM)
    q = rmsnorm(q, q_gain)
    k = rmsnorm(k, k_gain)
    cos, sin = rope_tables(S)
    q = apply_partial_rope(q, cos[None, :, None, None, :], sin[None, :, None, None, :])
    k = apply_partial_rope(k, cos[None, :, None, :], sin[None, :, None, :])

    qb = q.reshape(B, nb, BLOCK, N_KV_HEADS, GROUP, HEAD_DIM)
    kb = k.reshape(B, nb, BLOCK, N_KV_HEADS, HEAD_DIM)
    vb = v.reshape(B, nb, BLOCK, N_KV_HEADS, HEAD_DIM)
    k_prev = jnp.concatenate([jnp.zeros_like(kb[:, :1]), kb[:, :-1]], axis=1)
    v_prev = jnp.concatenate([jnp.zeros_like(vb[:, :1]), vb[:, :-1]], axis=1)
    k_band = jnp.concatenate([k_prev, kb], axis=2)
    v_band = jnp.concatenate([v_prev, vb], axis=2)

    scale = 1.0 / math.sqrt(HEAD_DIM)
    scores = jnp.einsum('bnqhgd,bnkhd->bnhgqk', qb, k_band).astype(jnp.float32) * scale

    qi = jnp.arange(BLOCK)[:, None]
    kj = jnp.arange(2 * BLOCK)[None, :]
    dist = BLOCK + qi - kj
    in_window = (dist >= 0) & (dist < WINDOW)
    blk = jnp.arange(nb)[:, None, None]
    key_pos = blk * BLOCK + kj[None] - BLOCK
    valid = in_window[None] & (key_pos >= 0)
    scores = jnp.where(valid[None, :, None, None], scores, NEG_INF)

    sink = sinks.astype(jnp.float32).reshape(1, 1, N_KV_HEADS, GROUP, 1, 1)
    m = jnp.maximum(jnp.max(scores, axis=-1, keepdims=True), sink)
    p = jnp.exp(scores - m)
    denom = jnp.sum(p, axis=-1, keepdims=True) + jnp.exp(sink - m)
    probs = (p / denom).astype(v.dtype)

    out = jnp.einsum('bnhgqk,bnkhd->bnqhgd', probs, v_band)
    return out.reshape(B, S, N_Q_HEADS * HEAD_DIM) @ w_o


def multiscale_pool_mixer(h, pool_w, pool_scale):
    B, S, D = h.shape
    hf = h.astype(jnp.float32)
    cs = jnp.concatenate([jnp.zeros((B, 1, D), jnp.float32), jnp.cumsum(hf, axis=1)], axis=1)
    t1 = jnp.arange(1, S + 1, dtype=jnp.float32)[None, :, None]
    diffs = []
    for g, w in enumerate(POOL_WINDOWS):
        sl = slice(g * POOL_GROUP_DIM, (g + 1) * POOL_GROUP_DIM)
        csg = cs[:, :, sl]
        win_sum = jnp.concatenate([csg[:, 1:w], csg[:, w:] - csg[:, :S + 1 - w]], axis=1)
        count = jnp.minimum(t1, float(w))
        diffs.append(win_sum / count - hf[:, :, sl])
    d = jnp.stack(diffs, axis=2)
    y = jnp.einsum('bsgc,gce->bsge', d.astype(h.dtype), pool_w).reshape(B, S, D)
    return y * pool_scale


def conv_glu_ffn(h, w_up, conv_w, conv_b, w_down):
    u = h @ w_up
    up = jnp.pad(u, ((0, 0), (CONV_WIDTH - 1, 0), (0, 0)))
    S = h.shape[1]
    u = conv_b + conv_w[0] * up[:, 0:S] + conv_w[1] * up[:, 1:S + 1] + conv_w[2] * up[:, 2:S + 2]
    gate, val = u[..., :D_FF], u[..., D_FF:]
    return (jax.nn.silu(gate) * val) @ w_down


def setup_inputs(seed: int = 0) -> dict:
    key = jax.random.key(seed)
    ks = jax.random.split(key, 20)
    D = D_MODEL
    nrm = jax.random.normal
    f32 = jnp.float32
    return {
        "x": nrm(ks[0], (BATCH, SEQ, D), f32),
        "c": nrm(ks[1], (BATCH, D), f32),
        "mod_w": nrm(ks[2], (DEPTH, D, 6 * D), f32) * (0.5 * D ** -0.5),
        "mod_b": nrm(ks[3], (DEPTH, 6 * D), f32) * 0.02,
        "mix_norm_gain": 1.0 + 0.05 * nrm(ks[4], (DEPTH, D), f32),
        "ffn_norm_gain": 1.0 + 0.05 * nrm(ks[5], (DEPTH, D), f32),
        "w_qkv": nrm(ks[6], (N_ATTN_LAYERS, D, QKV_DIM), f32) * D ** -0.5,
        "q_gain": 1.0 + 0.05 * nrm(ks[7], (N_ATTN_LAYERS, HEAD_DIM), f32),
        "k_gain": 1.0 + 0.05 * nrm(ks[8], (N_ATTN_LAYERS, HEAD_DIM), f32),
        "sinks": nrm(ks[9], (N_ATTN_LAYERS, N_Q_HEADS), f32),
        "w_o": nrm(ks[10], (N_ATTN_LAYERS, N_Q_HEADS * HEAD_DIM, D), f32) * (N_Q_HEADS * HEAD_DIM) ** -0.5,
        "pool_w": nrm(ks[11], (N_POOL_LAYERS, N_POOL_GROUPS, POOL_GROUP_DIM, POOL_GROUP_DIM), f32) * POOL_GROUP_DIM ** -0.5,
        "pool_scale": 1.0 + 0.1 * nrm(ks[12], (N_POOL_LAYERS, D), f32),
        "w_up": nrm(ks[13], (DEPTH, D, 2 * D_FF), f32) * D ** -0.5,
        "conv_w": nrm(ks[14], (DEPTH, CONV_WIDTH, 2 * D_FF), f32) * CONV_WIDTH ** -0.5,
        "conv_b": nrm(ks[15], (DEPTH, 2 * D_FF), f32) * 0.02,
        "w_down": nrm(ks[16], (DEPTH, D_FF, D), f32) * D_FF ** -0.5,
    }


def reference(x, c, mod_w, mod_b, mix_norm_gain, ffn_norm_gain, w_qkv, q_gain, k_gain,
              sinks, w_o, pool_w, pool_scale, w_up, conv_w, conv_b, w_down):
    c_act = jax.nn.silu(c)
    for i in range(DEPTH):
        mod = c_act @ mod_w[i] + mod_b[i]
        sh_m, sc_m, g_m, sh_f, sc_f, g_f = [m[:, None, :] for m in jnp.split(mod, 6, axis=-1)]

        h = rmsnorm(x, mix_norm_gain[i]) * (1.0 + sc_m) + sh_m
        if i % N_MIXERS == 0:
            a = i // N_MIXERS
            y = swa_sink_attention(h, w_qkv[a], q_gain[a], k_gain[a], sinks[a], w_o[a])
        else:
            p = i // N_MIXERS
            y = multiscale_pool_mixer(h, pool_w[p], pool_scale[p])
        x = x + g_m * y

        h = rmsnorm(x, ffn_norm_gain[i]) * (1.0 + sc_f) + sh_f
        x = x + g_f * conv_glu_ffn(h, w_up[i], conv_w[i], conv_b[i], w_down[i])
    return x
```

```python
import contextlib
import math
import numpy as np
import concourse.bass as bass
import concourse.mybir as mybir
from concourse.bass_utils import run_bass_kernel_spmd

F32 = mybir.dt.float32
BF16 = mybir.dt.bfloat16
AF = mybir.ActivationFunctionType
ALU = mybir.AluOpType
AX = mybir.AxisListType

D = 1024
NCH = 8
SEQ = 8192
TOK = 2048
HALO = 256
XT = TOK + HALO
NBLK = XT // 128
FS = 224
FW = 416
NWIN = 5
DFF = 2816
NPAIR = 22
QUARTERS = [(0, 6), (6, 6), (12, 5), (17, 5)]
MIX_TILES = [(0, 1, False), (1, 3, True), (4, 3, True), (7, 3, True), (10, 3, True), (13, 3, True), (16, 2, True)]
MW = 384
NPIECE = 24
EPS = 1e-6

PK_C = 0
PK_MODB = 8
PK_GAIN = 104
PK_QKG = 136
PK_SINK = 138
PK_PSC = 154
PK_CONV = 162
PK_VMASK = 514
PK_PINV = 770
PK_N = 898

SAME_ENG_SYNC = True
NSLOT = 6


class Buf:
    __slots__ = ("name", "w", "rs")

    def __init__(self, name=""):
        self.name = name
        self.w = None
        self.rs = {}


class Sched:
    ENGS = ("pe", "act", "dve", "pool", "sp")

    def __init__(self, nc):
        self.nc = nc
        self.streams = {e: [] for e in self.ENGS}
        self.cnt = {e: 0 for e in self.ENGS}
        self.waited = {e: {} for e in self.ENGS}
        self.dma_cnt = {"sp": 0, "pool": 0}
        self.semval = {}
        for q in ("sp", "pool"):
            for s in range(NSLOT):
                self.semval["d%s%d" % (q, s)] = 0

    def _deps(self, eng, reads, writes):
        need = {}

        def add(ev, war):
            if ev is None:
                return
            k, v = ev
            if k == eng:
                if eng == "pe" or not SAME_ENG_SYNC:
                    return
            if k in self.cnt and v > self.cnt[k]:
                raise RuntimeError("forward dependency on pending %s event (deadlock): %s" % (k, eng))
            if v > need.get(k, 0):
                need[k] = v

        for b in reads:
            add(b.w, False)
        for b in writes:
            add(b.w, False)
            for k, v in b.rs.items():
                add((k, v), True)
        return need

    def _commit_waits(self, eng, need):
        wt = self.waited[eng]
        waits = []
        for k, v in need.items():
            if v > wt.get(k, 0):
                wt[k] = v
                waits.append((k, v))
        return waits

    def _mark(self, ev, reads, writes):
        k, v = ev
        for b in reads:
            if v > b.rs.get(k, 0):
                b.rs[k] = v
        for b in writes:
            b.w = ev
            b.rs = {}

    def op(self, eng, fn, reads=(), writes=(), inc=True):
        need = self._deps(eng, reads, writes)
        waits = self._commit_waits(eng, need)
        if inc:
            self.cnt[eng] += 1
            ev = (eng, self.cnt[eng])
        else:
            ev = (eng, self.cnt[eng] + 1)
        self._mark(ev, reads, writes)
        self.streams[eng].append((waits, fn, "inc" if inc else None))

    def dma(self, q, fn, reads=(), writes=()):
        i = self.dma_cnt[q]
        self.dma_cnt[q] += 1
        key = "d%s%d" % (q, i % NSLOT)
        need = self._deps(q, reads, writes)
        pv = self.semval[key]
        if pv > 0:
            need[key] = max(need.get(key, 0), pv)
        waits = self._commit_waits(q, need)
        self.semval[key] = pv + 16
        ev = (key, pv + 16)
        self._mark(ev, reads, writes)
        self.streams[q].append((waits, fn, key))

    def barrier(self):
        need = {e: self.cnt[e] for e in self.ENGS if self.cnt[e] > 0}
        for k, v in self.semval.items():
            if v > 0:
                need[k] = v
        for e in self.ENGS:
            n2 = {k: v for k, v in need.items() if k != e}
            waits = self._commit_waits(e, n2)
            if waits:
                self.streams[e].append((waits, None, None))

    def finish(self):
        need = {k: v for k, v in self.semval.items() if v > 0}
        waits = self._commit_waits("sp", need)
        if waits:
            self.streams["sp"].append((waits, None, None))

    def emit(self):
        nc = self.nc
        names = list(self.ENGS) + sorted(self.semval.keys())
        with contextlib.ExitStack() as st:
            sems = {n: st.enter_context(nc.semaphore("s_" + n)) for n in names}
            block = st.enter_context(nc.Block())

            def run(e, eng):
                for waits, fn, mode in self.streams[e]:
                    for k, v in waits:
                        eng.wait_ge(sems[k], v)
                    if fn is None:
                        continue
                    ins = fn(eng)
                    if mode == "inc":
                        ins.then_inc(sems[e], 1)
                    elif mode is not None:
                        ins.then_inc(sems[mode], 16)

            @block.tensor
            def _(eng):
                run("pe", eng)

            @block.scalar
            def _(eng):
                run("act", eng)

            @block.vector
            def _(eng):
                run("dve", eng)

            @block.gpsimd
            def _(eng):
                run("pool", eng)

            @block.sync
            def _(eng):
                run("sp", eng)


def build_program(stop_after=None):
    nc = bass.Bass("TRN2", target_bir_lowering=False)

    def dram(name, shape, kind="ExternalInput", dt=F32):
        return nc.dram_tensor(name, list(shape), dt, kind=kind).ap()

    xT_d = dram("xT", [D, XT])
    pk_d = dram("pk", [128, PK_N])
    qkrow_d = dram("qkrow", [1, 128])
    cmat_d = dram("cmat", [128, 4, 128])
    masks_d = dram("masks", [128, 3, 512])
    cs_d = dram("cs", [128, 2, XT])
    modw_d = dram("modw", [2, NPIECE, 128, 8, 256])
    wqkv_d = dram("wqkv", [128, 8, 1536])
    wo_d = dram("wo", [128, 8, 1024])
    poolw_d = dram("poolw", [128, 4, 2, 256])
    wup_d = dram("wup", [2, NPAIR, 128, 8, 2, 128])
    wdown_d = dram("wdown", [2, 128, NPAIR, 1024])
    yT_d = dram("yT", [D, TOK], kind="ExternalOutput")
    yT_v = yT_d.rearrange("(c p) t -> p c t", p=128)
    xT_v = xT_d.rearrange("(c p) t -> p c t", p=128)

    S = Sched(nc)
    st = contextlib.ExitStack()
    with st:
        def sb(name, shape, dt):
            return st.enter_context(nc.sbuf_tensor(name, list(shape), dt))

        X = sb("X", [128, NCH, XT], F32)
        PK = sb("PK", [128, PK_N], F32)
        CM = sb("CM", [128, 4, 128], BF16)
        MK = sb("MK", [128, 3, 512], BF16)
        cact = sb("cact", [128, 8], BF16)
        modv = sb("modv", [128, 2, 48], F32)
        Amod = sb("Amod", [128, 2, 2, 8], F32)
        Gm1 = sb("Gm1", [128, 8], F32)
        negc = sb("negc", [128, 1], F32)
        esink = sb("esink", [128, 16], F32)
        epsT = sb("epsT", [128, 1], F32)
        qkrow = sb("qkrow_s", [1, 128], F32)
        onesrow = sb("onesrow", [1, 128], BF16)
        c1 = sb("c1", [1, 2], F32)
        c1b = sb("c1b", [1, 2], BF16)
        den = [sb("den%d" % i, [128, 4], F32) for i in range(2)]
        pfix = [sb("pfix%d" % i, [128, 16], F32) for i in range(2)]
        mwbuf = [sb("mwbuf%d" % i, [128, 8, 256], BF16) for i in range(2)]
        AW = 28400
        ARENA = sb("ARENA", [128, AW], F32)
        PS = [st.enter_context(nc.psum_tensor("PS%d" % i, [128, 512], F32)) for i in range(8)]
        PSB = [Buf("PS%d" % i) for i in range(8)]

        XB = [[Buf("X%d_%d" % (c, b)) for b in range(NBLK)] for c in range(NCH)]

        def xb(c, t0, t1):
            return XB[c][t0 // 128:(t1 - 1) // 128 + 1]

        B_PK = Buf("PK"); B_CM = Buf("CM"); B_MK = Buf("MK"); B_cact = Buf("cact")
        B_modv = [[Buf("modv%d_%d" % (l, i)) for i in range(6)] for l in range(2)]
        B_Amod = [[Buf(), Buf()], [Buf(), Buf()]]
        B_Gm1 = Buf(); B_negc = Buf(); B_esink = Buf(); B_eps = Buf()
        B_misc = Buf("misc")
        B_den = [Buf(), Buf()]; B_pfix = [Buf(), Buf()]
        B_mw = [Buf(), Buf()]

        ident = CM[:, 0, :]
        onesm = CM[:, 1, :]
        bdm = CM[:, 2, :]
        permm = CM[:, 3, :]

        class Carver:
            def __init__(self):
                self.off = 0

            def take(self, shape, dt):
                n = 1
                for s in shape[1:]:
                    n *= s
                words = (n + 1) // 2 if dt == BF16 else n
                a = ARENA[:, self.off:self.off + words]
                self.off += words
                assert self.off <= AW, "arena overflow %d" % self.off
                if dt == BF16:
                    a = a.bitcast(BF16)[:, 0:n]
                if len(shape) == 3:
                    a = a.rearrange("p (a b) -> p a b", a=shape[1])
                elif len(shape) == 4:
                    a = a.rearrange("p (a b c) -> p a b c", a=shape[1], b=shape[2])
                return a

        def ACT(out, in_, func, reads, writes, bias=None, scale=None):
            kw = {}
            if bias is not None:
                kw["bias"] = bias
            if scale is not None:
                kw["scale"] = scale
            S.op("act", lambda e: e.activation(out=out, in_=in_, func=func, **kw), reads, writes)

        def STT(out, in0, scalar, in1, op0, op1, reads, writes, eng="dve"):
            S.op(eng, lambda e: e.scalar_tensor_tensor(out=out, in0=in0, scalar=scalar, in1=in1, op0=op0, op1=op1),
                 reads, writes)

        def TT(out, in0, in1, op, reads, writes, eng="dve"):
            S.op(eng, lambda e: e.tensor_tensor(out=out, in0=in0, in1=in1, op=op), reads, writes)

        def TS(out, in0, s1, s2, op0, op1, reads, writes, eng="dve"):
            if op1 is None:
                S.op(eng, lambda e: e.tensor_scalar(out=out, in0=in0, scalar1=s1, scalar2=None, op0=op0), reads, writes)
            else:
                S.op(eng, lambda e: e.tensor_scalar(out=out, in0=in0, scalar1=s1, scalar2=s2, op0=op0, op1=op1),
                     reads, writes)

        def MM(out, lhsT, rhs, start, stop, reads, writes, inc):
            S.op("pe", lambda e: e.matmul(out, lhsT=lhsT, rhs=rhs, start=start, stop=stop), reads, writes, inc=inc)

        def modcol(l, which, c):
            return modv[:, l, which * 8 + c:which * 8 + c + 1]

        S.dma("sp", lambda e: e.dma_start(out=PK[:], in_=pk_d), writes=[B_PK])
        S.dma("sp", lambda e: e.dma_start(out=qkrow[:], in_=qkrow_d), writes=[B_misc])
        S.dma("pool", lambda e: e.dma_start(out=CM[:], in_=cmat_d), writes=[B_CM])
        S.dma("pool", lambda e: e.dma_start(out=MK[:], in_=masks_d), writes=[B_MK])
        xtiles = [(0, 512), (512, 1024), (1024, 1536), (1536, 2048), (2048, 2304)]
        for (a, b) in xtiles[:1]:
            S.dma("sp", lambda e, a=a, b=b: e.dma_start(out=X[:, :, a:b], in_=xT_v[:, :, a:b]),
                  writes=[bb for c in range(NCH) for bb in xb(c, a, b)])

        S.op("dve", lambda e: e.memset(epsT[:], EPS), writes=[B_eps])
        S.op("dve", lambda e: e.memset(onesrow[:], 1.0), writes=[B_misc])
        ACT(cact[:], PK[:, PK_C:PK_C + 8], AF.Silu, [B_PK], [B_cact])

        def mod_job(l, PMOD):
            def issue(piece):
                S.dma("pool", lambda e, piece=piece: e.dma_start(out=mwbuf[piece % 2][:], in_=modw_d[l, piece]),
                      writes=[B_mw[piece % 2]])
            issue(0)
            for piece in range(NPIECE):
                if piece + 1 < NPIECE:
                    issue(piece + 1)
                mb = mwbuf[piece % 2]
                for m in range(2):
                    for k in range(8):
                        MM(PS[PMOD][:, m:m + 1], mb[:, k, m * 128:(m + 1) * 128], cact[:, k:k + 1],
                           k == 0, k == 7, [B_mw[piece % 2], B_cact], [PSB[PMOD]], inc=(m == 1 and k == 7))
                which = piece // 4
                TT(modv[:, l, 2 * piece:2 * piece + 2], PS[PMOD][:, 0:2],
                   PK[:, PK_MODB + l * 48 + 2 * piece:PK_MODB + l * 48 + 2 * piece + 2], ALU.add,
                   [PSB[PMOD], B_PK], [B_modv[l][which]])
                if piece in (7, 19):
                    mf = 0 if piece == 7 else 1
                    wsc = 1 if mf == 0 else 4
                    STT(Amod[:, l, mf, :], modv[:, l, wsc * 8:wsc * 8 + 8], 1.0,
                        PK[:, PK_GAIN + (l * 2 + mf) * 8:PK_GAIN + (l * 2 + mf) * 8 + 8], ALU.add, ALU.mult,
                        [B_modv[l][wsc], B_PK], [B_Amod[l][mf]])
                if piece == 11 and l == 1:
                    TT(Gm1[:], modv[:, 1, 16:24], PK[:, PK_PSC:PK_PSC + 8], ALU.mult, [B_modv[1][2], B_PK], [B_Gm1])
                yield piece

        def run_job(job, n):
            for _ in range(n):
                try:
                    next(job)
                except StopIteration:
                    return

        job0 = mod_job(0, 4)
        job1 = mod_job(1, 7)

        TT(qkrow[:], qkrow[:], qkrow[:], ALU.mult, [B_misc], [B_misc])
        S.op("dve", lambda e: e.tensor_reduce(out=c1[:, 0:1], in_=qkrow[:], axis=AX.X, op=ALU.max), [B_misc], [B_misc])
        TS(c1b[:, 0:1], c1[:, 0:1], -8.0, None, ALU.mult, None, [B_misc], [B_misc])
        MM(PS[2][:, 0:1], onesrow[:, :], c1b[:, 0:1], True, True, [B_misc], [PSB[2]], inc=True)
        S.op("dve", lambda e: e.tensor_copy(out=negc[:], in_=PS[2][:, 0:1]), [PSB[2]], [B_negc])
        ACT(esink[:], PK[:, PK_SINK:PK_SINK + 16], AF.Exp, [B_PK, B_negc], [B_esink], bias=negc[:, 0:1], scale=1.0)

        run_job(job0, 8)

        def norm_tile(l, mf, t0, t1, xsq, B_xsq, rstd, B_rstd, xn, B_xn, PST, dst_fn, dst_bufs_fn, post=None,
                      sq_eng="pool"):
            W = t1 - t0
            for c in range(NCH):
                if sq_eng == "act":
                    ACT(xsq[c % 2][:, 0:W], X[:, c, t0:t1], AF.Square, xb(c, t0, t1), [B_xsq[c % 2]])
                else:
                    TT(xsq[c % 2][:, 0:W], X[:, c, t0:t1], X[:, c, t0:t1], ALU.mult, xb(c, t0, t1), [B_xsq[c % 2]],
                       eng=sq_eng)
                MM(PS[PST][:, 0:W], onesm, xsq[c % 2][:, 0:W], c == 0, c == NCH - 1, [B_xsq[c % 2], B_CM],
                   [PSB[PST]], inc=True)
            ACT(rstd[:, 0:W], PS[PST][:, 0:W], AF.Ln, [PSB[PST], B_eps], [B_rstd], bias=epsT[:, 0:1], scale=1.0)
            ACT(rstd[:, 0:W], rstd[:, 0:W], AF.Exp, [B_rstd], [B_rstd], scale=-0.5)
            wsh = 0 if mf == 0 else 3
            for c in range(NCH):
                STT(xn[c % 2][:, 0:W], X[:, c, t0:t1], Amod[:, l, mf, c:c + 1], rstd[:, 0:W], ALU.mult, ALU.mult,
                    xb(c, t0, t1) + [B_Amod[l][mf], B_rstd], [B_xn[c % 2]])
                ACT(dst_fn(c), xn[c % 2][:, 0:W], AF.Identity, [B_xn[c % 2], B_modv[l][wsh]], dst_bufs_fn(c),
                    bias=modcol(l, wsh, c), scale=1.0)
                if post is not None:
                    post(c)

        cv = Carver()
        wqkv = cv.take([128, 8, 1536], BF16); B_wqkv = Buf()
        wo = cv.take([128, 8, 1024], BF16); B_wo = Buf()
        h0b = [cv.take([128, 8, MW], BF16) for _ in range(2)]; B_h0b = [[Buf() for _ in range(8)] for _ in range(2)]
        qt = [cv.take([128, 8, MW], BF16) for _ in range(2)]; B_qt = [[Buf() for _ in range(8)] for _ in range(2)]
        kT = [cv.take([128, 2, 128 + MW], BF16) for _ in range(2)]; B_kT = [[Buf(), Buf()], [Buf(), Buf()]]
        Vt = [cv.take([128, 4, 4, 65], BF16) for _ in range(2)]; B_V = [[Buf() for _ in range(4)] for _ in range(2)]
        xsq = [cv.take([128, MW], BF16) for _ in range(2)]; B_xsq = [Buf(), Buf()]
        rstd = cv.take([128, MW], F32); B_rstd = Buf()
        xn = [cv.take([128, MW], F32) for _ in range(2)]; B_xn = [Buf(), Buf()]
        sq = [cv.take([128, MW], BF16) for _ in range(2)]; B_sq = [Buf(), Buf()]
        lnr = [cv.take([128, MW], F32) for _ in range(2)]; B_lnr = [Buf(), Buf()]
        qn = [cv.take([128, MW], BF16) for _ in range(2)]; B_qn = [Buf(), Buf()]
        t1b = [cv.take([128, MW], F32) for _ in range(2)]; B_t1 = [Buf(), Buf()]
        t2b = [cv.take([128, MW], F32) for _ in range(2)]; B_t2 = [Buf(), Buf()]
        csb = [cv.take([128, 2, MW], F32) for _ in range(2)]; B_cs = [Buf(), Buf()]
        PT = [cv.take([128, 2, 512], BF16) for _ in range(2)]; B_PT = [[Buf(), Buf()], [Buf(), Buf()]]
        otok = [cv.take([128, 1024], BF16) for _ in range(2)]; B_otok = [[Buf() for _ in range(4)] for _ in range(2)]
        oT = cv.take([128, 8, MW], BF16); B_oT = [Buf() for _ in range(3)]

        S.dma("pool", lambda e: e.dma_start(out=wqkv, in_=wqkv_d), writes=[B_wqkv])
        for (a, b) in xtiles[1:]:
            S.dma("sp", lambda e, a=a, b=b: e.dma_start(out=X[:, :, a:b], in_=xT_v[:, :, a:b]),
                  reads=[B_Amod[0][0]], writes=[bb for c in range(NCH) for bb in xb(c, a, b)])
        S.dma("pool", lambda e: e.dma_start(out=wo, in_=wo_d), writes=[B_wo])
        for par in range(2):
            S.op("dve", lambda e, par=par: e.memset(Vt[par], 1.0), writes=B_V[par])

        PTR = PS[7][:].bitcast(BF16).rearrange("p (c t) -> p c t", c=8)
        ABANK = [0, 1]
        PNST = 2
        PSTAT = 3
        PPERM = 4
        RING = [5, 6]
        POUT = 7
        SQ_ENG = "act"
        NSQ_ENG = "act"
        T1_ENG = "pool"
        ADD_ENG = "dve"

        def norm_stream(ti):
            b0, nb, has_q = MIX_TILES[ti]
            hp_ = ti % 2
            t0 = b0 * 128
            W = nb * 128
            t1 = t0 + W
            h0, B_h0 = h0b[hp_], B_h0b[hp_]
            for c in range(NCH):
                if NSQ_ENG == "act":
                    ACT(xsq[c % 2][:, 0:W], X[:, c, t0:t1], AF.Square, xb(c, t0, t1), [B_xsq[c % 2]])
                else:
                    TT(xsq[c % 2][:, 0:W], X[:, c, t0:t1], X[:, c, t0:t1], ALU.mult, xb(c, t0, t1), [B_xsq[c % 2]], eng=NSQ_ENG)
                MM(PS[PNST][:, 0:W], onesm, xsq[c % 2][:, 0:W], c == 0, c == NCH - 1, [B_xsq[c % 2], B_CM],
                   [PSB[PNST]], inc=True)
                if c % 2 == 1:
                    yield
            ACT(rstd[:, 0:W], PS[PNST][:, 0:W], AF.Ln, [PSB[PNST], B_eps], [B_rstd], bias=epsT[:, 0:1], scale=1.0)
            ACT(rstd[:, 0:W], rstd[:, 0:W], AF.Exp, [B_rstd], [B_rstd], scale=-0.5)
            yield
            for c in range(NCH):
                STT(xn[c % 2][:, 0:W], X[:, c, t0:t1], Amod[:, 0, 0, c:c + 1], rstd[:, 0:W], ALU.mult, ALU.mult,
                    xb(c, t0, t1) + [B_Amod[0][0], B_rstd], [B_xn[c % 2]])
                ACT(h0[:, c, 0:W], xn[c % 2][:, 0:W], AF.Identity, [B_xn[c % 2], B_modv[0][0]], [B_h0[c]],
                    bias=modcol(0, 0, c), scale=1.0)
                if c % 2 == 1:
                    yield

        def qkp_stream(ti):
            b0, nb, has_q = MIX_TILES[ti]
            par = ti % 2
            t0 = b0 * 128
            W = nb * 128
            t1 = t0 + W
            h0, B_h0 = h0b[par], B_h0b[par]
            if ti > 0:
                pnb = MIX_TILES[ti - 1][1]
                for kc in range(2):
                    S.op("dve", lambda e, kc=kc: e.tensor_copy(
                        out=kT[par][:, kc, 0:128], in_=kT[1 - par][:, kc, pnb * 128:pnb * 128 + 128]),
                        [B_kT[1 - par][kc]], [B_kT[par][kc]])
                S.op("dve", lambda e: e.tensor_copy(out=Vt[par][:, 0, :, 0:64], in_=Vt[1 - par][:, pnb, :, 0:64]),
                     [B_V[1 - par][pnb]], [B_V[par][0]])
            S.dma("sp", lambda e: e.dma_start(out=csb[par][:, :, 0:W], in_=cs_d[:, :, t0:t1]), writes=[B_cs[par]])
            for bi in range(nb):
                pq = ABANK[bi % 2]
                for k in range(8):
                    MM(PS[pq][:, 0:256], h0[:, k, bi * 128:(bi + 1) * 128], wqkv[:, k, 1280:1536], k == 0, k == 7,
                       [B_wqkv, B_h0[k]], [PSB[pq]], inc=(k == 7))
                S.op("dve", lambda e, pq=pq, bi=bi: e.tensor_copy(
                    out=Vt[par][:, 1 + bi, :, 0:64], in_=PS[pq][:, 0:256].rearrange("p (h d) -> p h d", h=4)),
                    [PSB[pq]], [B_V[par][1 + bi]])
                yield
            ocs = ([("q", i) for i in range(8)] if has_q else []) + [("k", 0), ("k", 1)]
            n = len(ocs)

            def A_(i):
                kind, oc = ocs[i]
                col = oc * 128 if kind == "q" else 1024 + oc * 128
                pq = ABANK[i % 2]
                for k in range(8):
                    MM(PS[pq][:, 0:W], wqkv[:, k, col:col + 128], h0[:, k, 0:W], k == 0, k == 7,
                       [B_wqkv, B_h0[k]], [PSB[pq]], inc=(k == 7))

            def SQ_(i):
                pq = ABANK[i % 2]
                if SQ_ENG == "act":
                    ACT(sq[i % 2][:, 0:W], PS[pq][:, 0:W], AF.Square, [PSB[pq]], [B_sq[i % 2]])
                else:
                    TT(sq[i % 2][:, 0:W], PS[pq][:, 0:W], PS[pq][:, 0:W], ALU.mult, [PSB[pq]], [B_sq[i % 2]], eng=SQ_ENG)

            def ST_(i):
                MM(PS[PSTAT][:, 0:W], bdm, sq[i % 2][:, 0:W], True, True, [B_CM, B_sq[i % 2]], [PSB[PSTAT]], inc=True)

            def LN_(i):
                ACT(lnr[i % 2][:, 0:W], PS[PSTAT][:, 0:W], AF.Ln, [PSB[PSTAT], B_eps], [B_lnr[i % 2]], bias=epsT[:, 0:1], scale=1.0)
                ACT(lnr[i % 2][:, 0:W], lnr[i % 2][:, 0:W], AF.Exp, [B_lnr[i % 2]], [B_lnr[i % 2]], scale=-0.5)

            def QN_(i):
                kind, oc = ocs[i]
                pq = ABANK[i % 2]
                gcol = PK_QKG if kind == "q" else PK_QKG + 1
                STT(qn[i % 2][:, 0:W], PS[pq][:, 0:W], PK[:, gcol:gcol + 1], lnr[i % 2][:, 0:W], ALU.mult, ALU.mult,
                    [PSB[pq], B_PK, B_lnr[i % 2]], [B_qn[i % 2]])

            def PM_(i):
                MM(PS[PPERM][:, 0:W], permm, qn[i % 2][:, 0:W], True, True, [B_CM, B_qn[i % 2]], [PSB[PPERM]], inc=True)

            def T1_(i):
                TT(t1b[i % 2][:, 0:W], qn[i % 2][:, 0:W], csb[par][:, 0, 0:W], ALU.mult, [B_qn[i % 2], B_cs[par]],
                   [B_t1[i % 2]], eng=T1_ENG)

            def T2_(i):
                TT(t2b[i % 2][:, 0:W], PS[PPERM][:, 0:W], csb[par][:, 1, 0:W], ALU.mult, [PSB[PPERM], B_cs[par]], [B_t2[i % 2]])

            def AD_(i):
                kind, oc = ocs[i]
                if kind == "q":
                    dst, dbuf = qt[par][:, oc, 0:W], B_qt[par][oc]
                else:
                    dst, dbuf = kT[par][:, oc, 128:128 + W], B_kT[par][oc]
                TT(dst, t1b[i % 2][:, 0:W], t2b[i % 2][:, 0:W], ALU.add, [B_t1[i % 2], B_t2[i % 2]], [dbuf], eng=ADD_ENG)

            for s_ in range(n + 4):
                if 0 <= s_ - 1 < n: SQ_(s_ - 1)
                if 0 <= s_ - 4 < n: T2_(s_ - 4)
                if 0 <= s_ - 4 < n: AD_(s_ - 4)
                if 0 <= s_ - 2 < n: LN_(s_ - 2)
                if 0 <= s_ - 2 < n: QN_(s_ - 2)
                yield
                if 0 <= s_ < n: A_(s_)
                if 0 <= s_ - 1 < n: ST_(s_ - 1)
                if 0 <= s_ - 3 < n: PM_(s_ - 3)
                if 0 <= s_ - 3 < n: T1_(s_ - 3)
                yield

        PTH = [PT[0][:, 0, :], PT[0][:, 1, :], PT[1][:, 0, :], PT[1][:, 1, :]]
        B_PTH = [Buf() for _ in range(4)]
        B_POUT = [PSB[7]] * 3
        B_otk = [[Buf() for _ in range(8)] for _ in range(2)]

        def attn_stream(ti):
            b0, nb, has_q = MIX_TILES[ti]
            par = ti % 2
            t0 = b0 * 128
            W = nb * 128
            t1 = t0 + W
            units = [(bi, h, hf) for bi in range(nb) for h in range(4) for hf in range(2)]
            NV = len(units)

            def SC_(v):
                bi, h, hf = units[v]
                gb = b0 + bi
                mprev = MK[:, 2, 0:256] if gb == 2 else MK[:, 1, 0:256]
                mcur = MK[:, 0, 0:256]
                hp = (h % 2) * 64
                kc = h // 2
                bank = RING[v % 2]
                rqh = qt[par][hp:hp + 64, 4 * kc + 2 * hf:4 * kc + 2 * hf + 2, bi * 128:(bi + 1) * 128]
                qbufs = [B_qt[par][4 * kc + 2 * hf + g] for g in range(2)]
                MM(PS[bank][:, 0:256], kT[par][hp:hp + 64, kc, bi * 128:bi * 128 + 128], rqh, True, False,
                   [B_kT[par][kc]] + qbufs, [PSB[bank]], inc=False)
                MM(PS[bank][:, 0:256], ident, mprev, False, True, [B_CM, B_MK], [PSB[bank]], inc=False)
                MM(PS[bank][:, 256:512], kT[par][hp:hp + 64, kc, 128 + bi * 128:256 + bi * 128], rqh, True, False,
                   [B_kT[par][kc]] + qbufs, [PSB[bank]], inc=False)
                MM(PS[bank][:, 256:512], ident, mcur, False, True, [B_CM, B_MK], [PSB[bank]], inc=True)

            def EX_(v):
                bank = RING[v % 2]
                ACT(PTH[v % 4], PS[bank][:, :], AF.Exp, [PSB[bank], B_negc], [B_PTH[v % 4]], bias=negc[:, 0:1], scale=0.125)

            def PV_(v):
                bi, h, hf = units[v]
                r = v % 3
                pt = PTH[v % 4]
                for g in range(2):
                    o0 = r * 130 + g * 65
                    MM(PS[POUT][:, o0:o0 + 65], pt[:, g * 128:(g + 1) * 128], Vt[par][:, bi, h, :],
                       True, False, [B_PTH[v % 4], B_V[par][bi]], [B_POUT[r]], inc=False)
                    MM(PS[POUT][:, o0:o0 + 65], pt[:, 256 + g * 128:256 + (g + 1) * 128], Vt[par][:, bi + 1, h, :],
                       False, True, [B_PTH[v % 4], B_V[par][bi + 1]], [B_POUT[r]], inc=(g == 1))

            def NM_(v):
                bi, h, hf = units[v]
                r = v % 3
                oi = bi % 2
                ov = PS[POUT][:, r * 130:(r + 1) * 130].rearrange("p (g d) -> p g d", g=2)
                di = v % 2
                qh0 = 4 * h + 2 * hf
                TT(den[di][:, 0:2], ov[:, :, 64], esink[:, qh0:qh0 + 2], ALU.add, [B_POUT[r], B_esink], [B_den[di]])
                S.op("dve", lambda e: e.reciprocal(out=den[di][:, 0:2], in_=den[di][:, 0:2]), [B_den[di]], [B_den[di]])
                TT(otok[oi][:, 64 * qh0:64 * qh0 + 128].rearrange("p (g d) -> p g d", g=2), ov[:, :, 0:64],
                   den[di][:, 0:2].unsqueeze(2).to_broadcast([128, 2, 64]), ALU.mult, [B_POUT[r], B_den[di]],
                   [B_otk[oi][2 * h + hf]])
                if h == 3 and hf == 1:
                    for c in range(8):
                        S.op("pe", lambda e, c=c: e.transpose(out=PTR[:, c, :], in_=otok[oi][:, c * 128:(c + 1) * 128],
                                                              identity=ident),
                             [B_otk[oi][c], B_CM], B_POUT, inc=(c == 7))
                    ACT(oT[:, :, bi * 128:(bi + 1) * 128], PTR, AF.Copy, B_POUT, [B_oT[bi]])

            SC_(0)
            for v in range(NV):
                if v + 1 < NV:
                    SC_(v + 1)
                EX_(v)
                PV_(v)
                NM_(v)
                yield
            for e_ in range(8):
                pq = RING[e_ % 2]
                for f in range(8):
                    MM(PS[pq][:, 0:W], wo[:, f, e_ * 128:(e_ + 1) * 128], oT[:, f, 0:W], f == 0, f == 7,
                       [B_wo] + B_oT[0:nb], [PSB[pq]], inc=(f == 7))
                STT(X[:, e_, t0:t1], PS[pq][:, 0:W], modcol(0, 2, e_), X[:, e_, t0:t1], ALU.mult, ALU.add,
                    [PSB[pq], B_modv[0][2]] + xb(e_, t0, t1), xb(e_, t0, t1))
                yield

        def drain(g):
            for _ in g:
                pass

        def interleave(gens):
            gens = [g for g in gens if g is not None]
            while gens:
                for g in list(gens):
                    try:
                        next(g)
                    except StopIteration:
                        gens.remove(g)

        def job_slices(job, n):
            for _ in range(n):
                run_job(job, 1)
                yield
                yield
                yield

        NT_ = len(MIX_TILES)
        drain(norm_stream(0))
        drain(qkp_stream(0))
        run_job(job0, 4)
        drain(norm_stream(1))
        drain(qkp_stream(1))
        drain(norm_stream(2))
        for ti in range(1, NT_):
            interleave([qkp_stream(ti + 1) if ti + 1 < NT_ else None, attn_stream(ti),
                        norm_stream(ti + 2) if ti + 2 < NT_ else None, job_slices(job0, 2)])
        run_job(job0, NPIECE)
        S.barrier()

        def ffn(l, job, final):
            cv = Carver()
            hF = cv.take([128, 8, NWIN * FW + 2], BF16)
            B_hF = [[Buf() for _ in range(NWIN)] for _ in range(8)]
            abuf = cv.take([128, 6, NWIN * FW], BF16)
            B_a = [[Buf() for _ in range(NWIN)] for _ in range(6)]
            wub = [cv.take([128, 8, 2, 128], BF16) for _ in range(3)]; B_wub = [Buf() for _ in range(3)]
            wdb = cv.take([128, 6, 1024], BF16); B_wdb = Buf()
            tg = [cv.take([128, FW], F32) for _ in range(2)]; B_tg = [Buf(), Buf()]
            tv = [cv.take([128, FW], F32) for _ in range(2)]; B_tv = [Buf(), Buf()]
            sg = [cv.take([128, FW], F32) for _ in range(2)]; B_sg = [Buf(), Buf()]
            xsq = [cv.take([128, FW + 2], BF16) for _ in range(2)]; B_xsq = [Buf(), Buf()]
            rstd = cv.take([128, FW + 2], F32); B_rstd = Buf()
            xn = [cv.take([128, FW + 2], F32) for _ in range(2)]; B_xn = [Buf(), Buf()]

            def issue_wup(j, cnt):
                slot = cnt % 3
                S.dma("pool", lambda e, j=j, slot=slot: e.dma_start(out=wub[slot], in_=wup_d[l, j]), writes=[B_wub[slot]])

            issue_wup(0, 0)
            issue_wup(1, 1)
            def norm_win(w):
                t0 = FS + FW * w - (2 if w == 0 else 0)
                t1 = FS + FW * (w + 1)
                h0c = t0 - (FS - 2)
                norm_tile(l, 1, t0, t1, xsq, B_xsq, rstd, B_rstd, xn, B_xn, 6,
                          lambda c, h0c=h0c, t0=t0, t1=t1: hF[:, c, h0c:h0c + (t1 - t0)], lambda c, w=w: [B_hF[c][w]])
                if w == 0:
                    for c in range(8):
                        TT(hF[:, c, 0:34], hF[:, c, 0:34], PK[:, PK_VMASK + 222:PK_VMASK + 256], ALU.mult,
                           [B_hF[c][0], B_PK], [B_hF[c][0]])
            norm_win(0)
            cnt = 0
            ev = 0
            for qi, (j0, nq) in enumerate(QUARTERS):
                S.dma("pool", lambda e, j0=j0, nq=nq: e.dma_start(out=wdb[:, 0:nq, :], in_=wdown_d[l, :, j0:j0 + nq, :]),
                      writes=[B_wdb])
                for jl in range(nq):
                    j = j0 + jl
                    if j + 2 < NPAIR:
                        issue_wup(j + 2, cnt + 2)
                    slot = cnt % 3
                    cnt += 1
                    for w in range(NWIN):
                        if j == 0 and w + 1 < NWIN:
                            norm_win(w + 1)
                        pg = ev % 2
                        pv = 2 + ev % 2
                        ev += 1
                        hreads = lambda k: [B_hF[k][w]] + ([B_hF[k][w - 1]] if w > 0 else [])
                        for k in range(8):
                            MM(PS[pg][:, 0:FW + 2], wub[slot][:, k, 0, :], hF[:, k, FW * w:FW * w + FW + 2], k == 0, k == 7,
                               [B_wub[slot]] + hreads(k), [PSB[pg]], inc=(k == 7))
                        for k in range(8):
                            MM(PS[pv][:, 0:FW + 2], wub[slot][:, k, 1, :], hF[:, k, FW * w:FW * w + FW + 2], k == 0, k == 7,
                               [B_wub[slot]] + hreads(k), [PSB[pv]], inc=(k == 7))
                        ti = ev % 2

                        def cp(cc, i):
                            o = PK_CONV + (l * 44 + cc) * 4 + i
                            return PK[:, o:o + 1]
                        for (pp, tb, Bt, cc) in ((pg, tg[ti], B_tg[ti], j), (pv, tv[ti], B_tv[ti], 22 + j)):
                            ACT(tb[:, :], PS[pp][:, 2:FW + 2], AF.Identity, [PSB[pp], B_PK], [Bt], bias=cp(cc, 3), scale=cp(cc, 2))
                            STT(tb[:, :], PS[pp][:, 1:FW + 1], cp(cc, 1), tb[:, :], ALU.mult, ALU.add, [PSB[pp], B_PK, Bt], [Bt])
                            STT(tb[:, :], PS[pp][:, 0:FW], cp(cc, 0), tb[:, :], ALU.mult, ALU.add, [PSB[pp], B_PK, Bt], [Bt])
                        ACT(sg[ti][:, :], tg[ti][:, :], AF.Silu, [B_tg[ti]], [B_sg[ti]])
                        TT(abuf[:, jl, FW * w:FW * (w + 1)], sg[ti][:, :], tv[ti][:, :], ALU.mult, [B_sg[ti], B_tv[ti]],
                           [B_a[jl][w]])
                    if job is not None:
                        run_job(job, 2 if j < 2 else 1)
                dcnt = 0
                for w in range(NWIN):
                    a0 = FS + FW * w
                    a1 = a0 + FW
                    for e_ in range(8):
                        pd = 4 + dcnt % 2
                        dcnt += 1
                        for jl in range(nq):
                            MM(PS[pd][:, 0:FW], wdb[:, jl, e_ * 128:(e_ + 1) * 128], abuf[:, jl, FW * w:FW * (w + 1)],
                               jl == 0, jl == nq - 1, [B_wdb, B_a[jl][w]], [PSB[pd]], inc=(jl == nq - 1))
                        STT(X[:, e_, a0:a1], PS[pd][:, 0:FW], modcol(l, 5, e_), X[:, e_, a0:a1], ALU.mult, ALU.add,
                            [PSB[pd], B_modv[l][5]] + xb(e_, a0, a1), xb(e_, a0, a1))
                    if final and qi == len(QUARTERS) - 1:
                        r0 = max(a0, HALO)
                        S.dma("sp", lambda e, r0=r0, a1=a1: e.dma_start(out=yT_v[:, :, r0 - HALO:a1 - HALO], in_=X[:, :, r0:a1]),
                              reads=[bb for c in range(NCH) for bb in xb(c, r0, a1)])
            if job is not None:
                run_job(job, NPIECE)

        def dump_and_finish():
            for (a, b) in ((256, 768), (768, 1280), (1280, 1792), (1792, 2304)):
                S.dma("sp", lambda e, a=a, b=b: e.dma_start(out=yT_v[:, :, a - HALO:b - HALO], in_=X[:, :, a:b]),
                      reads=[bb for c in range(NCH) for bb in xb(c, a, b)])

        done = False
        if stop_after == "mix0":
            dump_and_finish(); done = True
        if not done:
            ffn(0, job1, final=False)
            S.barrier()
            if stop_after == "ffn0":
                dump_and_finish(); done = True

        if not done:
            cv = Carver()
            PWd = FW + 16
            NB4 = 4
            pw = cv.take([128, 4, 2, 256], BF16); B_pw = Buf()
            xsq = [cv.take([128, PWd], BF16) for _ in range(2)]; B_xsq = [Buf(), Buf()]
            rstd2 = [cv.take([128, PWd], F32) for _ in range(2)]; B_rstd2 = [Buf(), Buf()]
            xn = [cv.take([128, PWd], F32) for _ in range(NB4)]; B_xn = [Buf() for _ in range(NB4)]
            h1 = [cv.take([128, PWd], F32) for _ in range(NB4)]; B_h1 = [Buf() for _ in range(NB4)]
            sA = [cv.take([128, PWd], F32) for _ in range(NB4)]; B_sA = [Buf() for _ in range(NB4)]
            sB = [cv.take([128, PWd], F32) for _ in range(NB4)]; B_sB = [Buf() for _ in range(NB4)]
            dT = [cv.take([128, 8, FW], BF16) for _ in range(2)]; B_dT = [[Buf() for _ in range(8)] for _ in range(2)]
            S.dma("pool", lambda e: e.dma_start(out=pw, in_=poolw_d), writes=[B_pw])
            CH_ORDER = [0, 4, 1, 5, 2, 6, 3, 7]
            wins = list(range(NWIN - 1, -1, -1))

            def pstats(wi):
                w = wins[wi]
                r0 = FS + FW * w - 16
                t1 = FS + FW * (w + 1)
                pst = 6 + wi % 2
                for c in range(NCH):
                    ACT(xsq[c % 2][:, 0:PWd], X[:, c, r0:t1], AF.Square, xb(c, r0, t1), [B_xsq[c % 2]])
                    MM(PS[pst][:, 0:PWd], onesm, xsq[c % 2][:, 0:PWd], c == 0, c == NCH - 1, [B_xsq[c % 2], B_CM],
                       [PSB[pst]], inc=True)
                rs, Brs = rstd2[wi % 2], B_rstd2[wi % 2]
                ACT(rs[:, :], PS[pst][:, 0:PWd], AF.Ln, [PSB[pst], B_eps], [Brs], bias=epsT[:, 0:1], scale=1.0)
                ACT(rs[:, :], rs[:, :], AF.Exp, [Brs], [Brs], scale=-0.5)

            pcnt = 0
            pstats(0)
            for wi, w in enumerate(wins):
                t0 = FS + FW * w
                t1 = t0 + FW
                r0 = t0 - 16
                dpar = wi % 2
                rs, Brs = rstd2[wi % 2], B_rstd2[wi % 2]
                if wi + 1 < len(wins):
                    pstats(wi + 1)
                fin = {}

                def st0(ci):
                    c = CH_ORDER[ci]
                    bi4 = ci % NB4
                    hb, Bh = h1[bi4], B_h1[bi4]
                    STT(xn[bi4][:, :], X[:, c, r0:t1], Amod[:, 1, 0, c:c + 1], rs[:, :], ALU.mult, ALU.mult,
                        xb(c, r0, t1) + [B_Amod[1][0], Brs], [B_xn[bi4]])
                    ACT(hb[:, :], xn[bi4][:, :], AF.Identity, [B_xn[bi4], B_modv[1][0]], [Bh], bias=modcol(1, 0, c), scale=1.0)
                    if w == 0:
                        TT(hb[:, 0:48], hb[:, 0:48], PK[:, PK_VMASK + 208:PK_VMASK + 256], ALU.mult, [Bh, B_PK], [Bh])

                def st1(ci):
                    c = CH_ORDER[ci]
                    bi4 = ci % NB4
                    g = c // 2
                    aeng = "pool" if c >= 4 else "dve"
                    src, Bsrc = h1[bi4], B_h1[bi4]
                    bufs = [(sA[bi4], B_sA[bi4]), (sB[bi4], B_sB[bi4])]
                    for step in range(g + 1):
                        sh = 1 << step
                        lo = (1 << (step + 1)) - 1
                        dstb, Bd = bufs[step % 2]
                        TT(dstb[:, lo:PWd], src[:, lo:PWd], src[:, lo - sh:PWd - sh], ALU.add, [Bsrc], [Bd], eng=aeng)
                        src, Bsrc = dstb, Bd
                    fin[ci] = (src, Bsrc)

                def st2(ci):
                    c = CH_ORDER[ci]
                    bi4 = ci % NB4
                    g = c // 2
                    hb, Bh = h1[bi4], B_h1[bi4]
                    src, Bsrc = fin[ci]
                    wd = float(1 << (g + 1))
                    STT(dT[dpar][:, c, :], src[:, 16:PWd], 1.0 / wd, hb[:, 16:PWd], ALU.mult, ALU.subtract,
                        [Bsrc, Bh], [B_dT[dpar][c]])
                    if w == 0:
                        pf, Bp = pfix[c % 2], B_pfix[c % 2]
                        TT(pf[:, :], src[:, 48:64], PK[:, PK_PINV + c * 16:PK_PINV + (c + 1) * 16], ALU.mult, [Bsrc, B_PK], [Bp])
                        TT(dT[dpar][:, c, 32:48], pf[:, :], hb[:, 48:64], ALU.subtract, [Bp, Bh, B_dT[dpar][c]], [B_dT[dpar][c]])

                for it in range(NCH + 3):
                    if 0 <= it - 3 < NCH:
                        st2(it - 3)
                    if 0 <= it - 1 < NCH:
                        st1(it - 1)
                    if it < NCH:
                        st0(it)
                for g in range(4):
                    for eo in range(2):
                        pq = pcnt % 2
                        pcnt += 1
                        ch = 2 * g + eo
                        for ci in range(2):
                            MM(PS[pq][:, 0:FW], pw[:, g, ci, eo * 128:(eo + 1) * 128], dT[dpar][:, 2 * g + ci, :], ci == 0, ci == 1,
                               [B_pw, B_dT[dpar][2 * g + ci]], [PSB[pq]], inc=(ci == 1))
                        STT(X[:, ch, t0:t1], PS[pq][:, 0:FW], Gm1[:, ch:ch + 1], X[:, ch, t0:t1], ALU.mult, ALU.add,
                            [PSB[pq], B_Gm1] + xb(ch, t0, t1), xb(ch, t0, t1))
            S.barrier()
            if stop_after == "mix1":
                dump_and_finish(); done = True
        if not done:
            ffn(1, None, final=True)

        S.finish()
        S.emit()
    return nc


def _const_tables():
    ident = np.eye(128, dtype=np.float32)
    onesm = np.full((128, 128), 1.0 / 1024.0, np.float32)
    bd = np.zeros((128, 128), np.float32)
    bd[0:64, 0:64] = 1.0 / 64.0
    bd[64:128, 64:128] = 1.0 / 64.0
    perm = np.zeros((128, 128), np.float32)
    for m in range(128):
        d = m % 64
        if d < 8:
            perm[m + 8, m] = 1.0
        elif d < 16:
            perm[m - 8, m] = 1.0
    cmat = np.stack([ident, onesm, bd, perm], axis=1)
    j = np.arange(128)[:, None]
    q = np.arange(128)[None, :]
    NEG = -1.0e4
    cur = np.where(j <= q, 0.0, NEG).astype(np.float32)
    prev = np.where(j > q, 0.0, NEG).astype(np.float32)
    cur4 = np.tile(cur, (1, 4))
    prev4 = np.tile(prev, (1, 4))
    allneg = np.full((128, 512), NEG, np.float32)
    return np.ascontiguousarray(cmat), cur4, prev4, allneg


def _rope_table(pos0):
    half = 8
    inv_freq = (500000.0 ** (-np.arange(0, half, dtype=np.float32) * 2.0 / 16.0)).astype(np.float32)
    pos = (pos0 + np.arange(XT)).astype(np.float32)
    ang = pos[None, :] * inv_freq[:, None]
    cos = np.cos(ang).astype(np.float32)
    sin = np.sin(ang).astype(np.float32)
    cs = np.zeros((128, 2, XT), np.float32)
    cs[:, 0, :] = 1.0
    for p in range(128):
        d = p % 64
        if d < 8:
            cs[p, 0] = cos[d]
            cs[p, 1] = -sin[d]
        elif d < 16:
            cs[p, 0] = cos[d - 8]
            cs[p, 1] = sin[d - 8]
    return cs


_PROGRAM_CACHE = {}


def kernel(x, c, mod_w, mod_b, mix_norm_gain, ffn_norm_gain, w_qkv, q_gain, k_gain, sinks, w_o, pool_w,
           pool_scale, w_up, conv_w, conv_b, w_down, _stop_after=None):
    f32 = np.float32
    x = np.asarray(x, f32); c = np.asarray(c, f32)
    mod_w = np.asarray(mod_w, f32); mod_b = np.asarray(mod_b, f32)
    w_qkv = np.asarray(w_qkv, f32); w_o = np.asarray(w_o, f32)
    w_up = np.asarray(w_up, f32); w_down = np.asarray(w_down, f32)
    conv_w = np.asarray(conv_w, f32); conv_b = np.asarray(conv_b, f32)
    pool_w = np.asarray(pool_w, f32)

    modw_r = np.ascontiguousarray(mod_w.reshape(2, 8, 128, NPIECE, 256).transpose(0, 3, 2, 1, 4))
    wq = w_qkv[0]
    qcols = []
    for ci in range(8):
        hi, g = ci // 4, ci % 4
        lo_head = 8 * hi + g
        hi_head = 8 * hi + 4 + g
        qcols += list(range(lo_head * 64, lo_head * 64 + 64)) + list(range(hi_head * 64, hi_head * 64 + 64))
    wq_perm = np.concatenate([wq[:, qcols], wq[:, 1024:]], axis=1)
    wqkv_r = np.ascontiguousarray(wq_perm.reshape(8, 128, 1536).transpose(1, 0, 2))
    wo_r = np.ascontiguousarray(w_o[0].reshape(8, 128, 1024).transpose(1, 0, 2))
    poolw_r = np.ascontiguousarray(pool_w[0].reshape(4, 2, 128, 256).transpose(2, 0, 1, 3))
    wup_r = np.ascontiguousarray(w_up.reshape(2, 8, 128, 2, NPAIR, 128).transpose(0, 4, 2, 1, 3, 5))
    wdown_r = np.ascontiguousarray(w_down.reshape(2, NPAIR, 128, 1024).transpose(0, 2, 1, 3))
    cmat, cur4, prev4, allneg = _const_tables()
    qkrow = np.concatenate([np.asarray(q_gain, f32)[0], np.asarray(k_gain, f32)[0]])[None, :].astype(f32)

    convp = np.zeros((128, 2, 44, 4), f32)
    for l in range(2):
        convp[:, l, :, 0:3] = conv_w[l].reshape(3, 44, 128).transpose(2, 1, 0)
        convp[:, l, :, 3] = conv_b[l].reshape(44, 128).T
    gains = np.stack([np.asarray(mix_norm_gain, f32), np.asarray(ffn_norm_gain, f32)], axis=1)
    gains_r = gains.reshape(2, 2, 8, 128).transpose(3, 0, 1, 2).reshape(128, 32)
    modb_r = mod_b.reshape(2, 48, 128).transpose(2, 0, 1).reshape(128, 96)
    pidx = np.arange(128) % 64
    qkg = np.stack([np.asarray(q_gain, f32)[0][pidx], np.asarray(k_gain, f32)[0][pidx]], axis=1)
    sinks_bc = np.tile(np.asarray(sinks, f32)[0][None, :], (128, 1))
    psc = np.asarray(pool_scale, f32)[0].reshape(8, 128).T

    in_maps = []
    for core in range(8):
        b, q = core // 4, core % 4
        T0 = q * TOK
        xs = np.zeros((XT, D), f32)
        lo = T0 - HALO
        if lo >= 0:
            xs[:] = x[b, lo:T0 + TOK]
        else:
            xs[HALO:] = x[b, 0:TOK]
        pos = lo + np.arange(XT)
        pk = np.zeros((128, PK_N), f32)
        pk[:, PK_C:PK_C + 8] = c[b].reshape(8, 128).T
        pk[:, PK_MODB:PK_MODB + 96] = modb_r
        pk[:, PK_GAIN:PK_GAIN + 32] = gains_r
        pk[:, PK_QKG:PK_QKG + 2] = qkg
        pk[:, PK_SINK:PK_SINK + 16] = sinks_bc
        pk[:, PK_PSC:PK_PSC + 8] = psc
        pk[:, PK_CONV:PK_CONV + 352] = convp.reshape(128, 352)
        pk[:, PK_VMASK:PK_VMASK + 256] = (pos[0:256] >= 0).astype(f32)[None, :]
        pinv = np.zeros((8, 16), f32)
        for ch in range(8):
            wd = float(2 ** (ch // 2 + 1))
            pp = pos[256:272].astype(f32)
            pinv[ch] = 1.0 / np.minimum(pp + 1.0, wd)
        pk[:, PK_PINV:PK_PINV + 128] = pinv.reshape(1, 128)
        masks = np.stack([cur4, prev4, allneg if q == 0 else prev4], axis=1)
        in_maps.append({
            "xT": np.ascontiguousarray(xs.T),
            "pk": pk,
            "qkrow": qkrow,
            "cmat": cmat,
            "masks": np.ascontiguousarray(masks),
            "cs": _rope_table(lo),
            "modw": modw_r,
            "wqkv": wqkv_r,
            "wo": wo_r,
            "poolw": poolw_r,
            "wup": wup_r,
            "wdown": wdown_r,
        })

    key = _stop_after
    if key not in _PROGRAM_CACHE:
        _PROGRAM_CACHE[key] = build_program(_stop_after)
    nc = _PROGRAM_CACHE[key]
    res = run_bass_kernel_spmd(nc, in_maps, core_ids=list(range(8)))
    out = np.zeros((2, SEQ, D), f32)
    for core in range(8):
        b, q = core // 4, core % 4
        out[b, q * TOK:(q + 1) * TOK, :] = res.results[core]["yT"].T
    return out
```

```python
import contextlib
import math
import numpy as np
import concourse.bass as bass
import concourse.mybir as mybir
from concourse.bass_utils import run_bass_kernel_spmd

F32 = mybir.dt.float32
BF16 = mybir.dt.bfloat16
AF = mybir.ActivationFunctionType
ALU = mybir.AluOpType
AX = mybir.AxisListType

D = 1024
NCH = 8
SEQ = 8192
TOK = 2048
HALO = 256
XT = TOK + HALO
NBLK = XT // 128
FS = 224
FW = 416
NWIN = 5
DFF = 2816
NPAIR = 22
QUARTERS = [(0, 6), (6, 6), (12, 5), (17, 5)]
MIX_TILES = [(0, 1, False), (1, 3, True), (4, 3, True), (7, 3, True), (10, 3, True), (13, 3, True), (16, 2, True)]
MW = 384
NPIECE = 24
EPS = 1e-6

PK_C = 0
PK_MODB = 8
PK_GAIN = 104
PK_QKG = 136
PK_SINK = 138
PK_PSC = 154
PK_CONV = 162
PK_VMASK = 514
PK_PINV = 770
PK_N = 898

SAME_ENG_SYNC = True
NSLOT = 6


class Buf:
    __slots__ = ("name", "w", "rs")

    def __init__(self, name=""):
        self.name = name
        self.w = None
        self.rs = {}


class Sched:
    ENGS = ("pe", "act", "dve", "pool", "sp")

    def __init__(self, nc):
        self.nc = nc
        self.streams = {e: [] for e in self.ENGS}
        self.cnt = {e: 0 for e in self.ENGS}
        self.waited = {e: {} for e in self.ENGS}
        self.dma_cnt = {"sp": 0, "pool": 0}
        self.semval = {}
        for q in ("sp", "pool"):
            for s in range(NSLOT):
                self.semval["d%s%d" % (q, s)] = 0

    def _deps(self, eng, reads, writes):
        need = {}

        def add(ev, war):
            if ev is None:
                return
            k, v = ev
            if k == eng:
                if eng == "pe" or not SAME_ENG_SYNC:
                    return
            if k in self.cnt and v > self.cnt[k]:
                raise RuntimeError("forward dependency on pending %s event (deadlock): %s" % (k, eng))
            if v > need.get(k, 0):
                need[k] = v

        for b in reads:
            add(b.w, False)
        for b in writes:
            add(b.w, False)
            for k, v in b.rs.items():
                add((k, v), True)
        return need

    def _commit_waits(self, eng, need):
        wt = self.waited[eng]
        waits = []
        for k, v in need.items():
            if v > wt.get(k, 0):
                wt[k] = v
                waits.append((k, v))
        return waits

    def _mark(self, ev, reads, writes):
        k, v = ev
        for b in reads:
            if v > b.rs.get(k, 0):
                b.rs[k] = v
        for b in writes:
            b.w = ev
            b.rs = {}

    def op(self, eng, fn, reads=(), writes=(), inc=True):
        need = self._deps(eng, reads, writes)
        waits = self._commit_waits(eng, need)
        if inc:
            self.cnt[eng] += 1
            ev = (eng, self.cnt[eng])
        else:
            ev = (eng, self.cnt[eng] + 1)
        self._mark(ev, reads, writes)
        self.streams[eng].append((waits, fn, "inc" if inc else None))

    def dma(self, q, fn, reads=(), writes=()):
        i = self.dma_cnt[q]
        self.dma_cnt[q] += 1
        key = "d%s%d" % (q, i % NSLOT)
        need = self._deps(q, reads, writes)
        pv = self.semval[key]
        if pv > 0:
            need[key] = max(need.get(key, 0), pv)
        waits = self._commit_waits(q, need)
        self.semval[key] = pv + 16
        ev = (key, pv + 16)
        self._mark(ev, reads, writes)
        self.streams[q].append((waits, fn, key))

    def barrier(self):
        need = {e: self.cnt[e] for e in self.ENGS if self.cnt[e] > 0}
        for k, v in self.semval.items():
            if v > 0:
                need[k] = v
        for e in self.ENGS:
            n2 = {k: v for k, v in need.items() if k != e}
            waits = self._commit_waits(e, n2)
            if waits:
                self.streams[e].append((waits, None, None))

    def finish(self):
        need = {k: v for k, v in self.semval.items() if v > 0}
        waits = self._commit_waits("sp", need)
        if waits:
            self.streams["sp"].append((waits, None, None))

    def emit(self):
        nc = self.nc
        names = list(self.ENGS) + sorted(self.semval.keys())
        with contextlib.ExitStack() as st:
            sems = {n: st.enter_context(nc.semaphore("s_" + n)) for n in names}
            block = st.enter_context(nc.Block())

            def run(e, eng):
                for waits, fn, mode in self.streams[e]:
                    for k, v in waits:
                        eng.wait_ge(sems[k], v)
                    if fn is None:
                        continue
                    ins = fn(eng)
                    if mode == "inc":
                        ins.then_inc(sems[e], 1)
                    elif mode is not None:
                        ins.then_inc(sems[mode], 16)

            @block.tensor
            def _(eng):
                run("pe", eng)

            @block.scalar
            def _(eng):
                run("act", eng)

            @block.vector
            def _(eng):
                run("dve", eng)

            @block.gpsimd
            def _(eng):
                run("pool", eng)

            @block.sync
            def _(eng):
                run("sp", eng)


def build_program(stop_after=None):
    nc = bass.Bass("TRN2", target_bir_lowering=False)

    def dram(name, shape, kind="ExternalInput", dt=F32):
        return nc.dram_tensor(name, list(shape), dt, kind=kind).ap()

    xT_d = dram("xT", [D, XT])
    pk_d = dram("pk", [128, PK_N])
    qkrow_d = dram("qkrow", [1, 128])
    cmat_d = dram("cmat", [128, 4, 128])
    masks_d = dram("masks", [128, 3, 512])
    cs_d = dram("cs", [128, 2, XT])
    modw_d = dram("modw", [2, NPIECE, 128, 8, 256])
    wqkv_d = dram("wqkv", [128, 8, 1536])
    wo_d = dram("wo", [128, 8, 1024])
    poolw_d = dram("poolw", [128, 4, 2, 256])
    wup_d = dram("wup", [2, NPAIR, 128, 8, 2, 128])
    wdown_d = dram("wdown", [2, 128, NPAIR, 1024])
    yT_d = dram("yT", [D, TOK], kind="ExternalOutput")
    yT_v = yT_d.rearrange("(c p) t -> p c t", p=128)
    xT_v = xT_d.rearrange("(c p) t -> p c t", p=128)

    S = Sched(nc)
    st = contextlib.ExitStack()
    with st:
        def sb(name, shape, dt):
            return st.enter_context(nc.sbuf_tensor(name, list(shape), dt))

        X = sb("X", [128, NCH, XT], F32)
        PK = sb("PK", [128, PK_N], F32)
        CM = sb("CM", [128, 4, 128], BF16)
        MK = sb("MK", [128, 3, 512], BF16)
        cact = sb("cact", [128, 8], BF16)
        modv = sb("modv", [128, 2, 48], F32)
        Amod = sb("Amod", [128, 2, 2, 8], F32)
        Gm1 = sb("Gm1", [128, 8], F32)
        negc = sb("negc", [128, 1], F32)
        esink = sb("esink", [128, 16], F32)
        epsT = sb("epsT", [128, 1], F32)
        qkrow = sb("qkrow_s", [1, 128], F32)
        onesrow = sb("onesrow", [1, 128], BF16)
        c1 = sb("c1", [1, 2], F32)
        c1b = sb("c1b", [1, 2], BF16)
        den = [sb("den%d" % i, [128, 4], F32) for i in range(2)]
        pfix = [sb("pfix%d" % i, [128, 16], F32) for i in range(2)]
        mwbuf = [sb("mwbuf%d" % i, [128, 8, 256], BF16) for i in range(2)]
        AW = 26800
        ARENA = sb("ARENA", [128, AW], F32)
        PS = [st.enter_context(nc.psum_tensor("PS%d" % i, [128, 512], F32)) for i in range(8)]
        PSB = [Buf("PS%d" % i) for i in range(8)]

        XB = [[Buf("X%d_%d" % (c, b)) for b in range(NBLK)] for c in range(NCH)]

        def xb(c, t0, t1):
            return XB[c][t0 // 128:(t1 - 1) // 128 + 1]

        B_PK = Buf("PK"); B_CM = Buf("CM"); B_MK = Buf("MK"); B_cact = Buf("cact")
        B_modv = [[Buf("modv%d_%d" % (l, i)) for i in range(6)] for l in range(2)]
        B_Amod = [[Buf(), Buf()], [Buf(), Buf()]]
        B_Gm1 = Buf(); B_negc = Buf(); B_esink = Buf(); B_eps = Buf()
        B_misc = Buf("misc")
        B_den = [Buf(), Buf()]; B_pfix = [Buf(), Buf()]
        B_mw = [Buf(), Buf()]

        ident = CM[:, 0, :]
        onesm = CM[:, 1, :]
        bdm = CM[:, 2, :]
        permm = CM[:, 3, :]

        class Carver:
            def __init__(self):
                self.off = 0

            def take(self, shape, dt):
                n = 1
                for s in shape[1:]:
                    n *= s
                words = (n + 1) // 2 if dt == BF16 else n
                a = ARENA[:, self.off:self.off + words]
                self.off += words
                assert self.off <= AW, "arena overflow %d" % self.off
                if dt == BF16:
                    a = a.bitcast(BF16)[:, 0:n]
                if len(shape) == 3:
                    a = a.rearrange("p (a b) -> p a b", a=shape[1])
                elif len(shape) == 4:
                    a = a.rearrange("p (a b c) -> p a b c", a=shape[1], b=shape[2])
                return a

        def ACT(out, in_, func, reads, writes, bias=None, scale=None):
            kw = {}
            if bias is not None:
                kw["bias"] = bias
            if scale is not None:
                kw["scale"] = scale
            S.op("act", lambda e: e.activation(out=out, in_=in_, func=func, **kw), reads, writes)

        def STT(out, in0, scalar, in1, op0, op1, reads, writes, eng="dve"):
            S.op(eng, lambda e: e.scalar_tensor_tensor(out=out, in0=in0, scalar=scalar, in1=in1, op0=op0, op1=op1),
                 reads, writes)

        def TT(out, in0, in1, op, reads, writes, eng="dve"):
            S.op(eng, lambda e: e.tensor_tensor(out=out, in0=in0, in1=in1, op=op), reads, writes)

        def TS(out, in0, s1, s2, op0, op1, reads, writes, eng="dve"):
            if op1 is None:
                S.op(eng, lambda e: e.tensor_scalar(out=out, in0=in0, scalar1=s1, scalar2=None, op0=op0), reads, writes)
            else:
                S.op(eng, lambda e: e.tensor_scalar(out=out, in0=in0, scalar1=s1, scalar2=s2, op0=op0, op1=op1),
                     reads, writes)

        def MM(out, lhsT, rhs, start, stop, reads, writes, inc):
            S.op("pe", lambda e: e.matmul(out, lhsT=lhsT, rhs=rhs, start=start, stop=stop), reads, writes, inc=inc)

        def modcol(l, which, c):
            return modv[:, l, which * 8 + c:which * 8 + c + 1]

        S.dma("sp", lambda e: e.dma_start(out=PK[:], in_=pk_d), writes=[B_PK])
        S.dma("sp", lambda e: e.dma_start(out=qkrow[:], in_=qkrow_d), writes=[B_misc])
        S.dma("pool", lambda e: e.dma_start(out=CM[:], in_=cmat_d), writes=[B_CM])
        S.dma("pool", lambda e: e.dma_start(out=MK[:], in_=masks_d), writes=[B_MK])
        xtiles = [(0, 512), (512, 1024), (1024, 1536), (1536, 2048), (2048, 2304)]
        for (a, b) in xtiles[:1]:
            S.dma("sp", lambda e, a=a, b=b: e.dma_start(out=X[:, :, a:b], in_=xT_v[:, :, a:b]),
                  writes=[bb for c in range(NCH) for bb in xb(c, a, b)])

        S.op("dve", lambda e: e.memset(epsT[:], EPS), writes=[B_eps])
        S.op("dve", lambda e: e.memset(onesrow[:], 1.0), writes=[B_misc])
        ACT(cact[:], PK[:, PK_C:PK_C + 8], AF.Silu, [B_PK], [B_cact])

        def mod_job(l, PMOD):
            def issue(piece):
                S.dma("pool", lambda e, piece=piece: e.dma_start(out=mwbuf[piece % 2][:], in_=modw_d[l, piece]),
                      writes=[B_mw[piece % 2]])
            issue(0)
            for piece in range(NPIECE):
                if piece + 1 < NPIECE:
                    issue(piece + 1)
                mb = mwbuf[piece % 2]
                for m in range(2):
                    for k in range(8):
                        MM(PS[PMOD][:, m:m + 1], mb[:, k, m * 128:(m + 1) * 128], cact[:, k:k + 1],
                           k == 0, k == 7, [B_mw[piece % 2], B_cact], [PSB[PMOD]], inc=(m == 1 and k == 7))
                which = piece // 4
                TT(modv[:, l, 2 * piece:2 * piece + 2], PS[PMOD][:, 0:2],
                   PK[:, PK_MODB + l * 48 + 2 * piece:PK_MODB + l * 48 + 2 * piece + 2], ALU.add,
                   [PSB[PMOD], B_PK], [B_modv[l][which]])
                if piece in (7, 19):
                    mf = 0 if piece == 7 else 1
                    wsc = 1 if mf == 0 else 4
                    STT(Amod[:, l, mf, :], modv[:, l, wsc * 8:wsc * 8 + 8], 1.0,
                        PK[:, PK_GAIN + (l * 2 + mf) * 8:PK_GAIN + (l * 2 + mf) * 8 + 8], ALU.add, ALU.mult,
                        [B_modv[l][wsc], B_PK], [B_Amod[l][mf]])
                if piece == 11 and l == 1:
                    TT(Gm1[:], modv[:, 1, 16:24], PK[:, PK_PSC:PK_PSC + 8], ALU.mult, [B_modv[1][2], B_PK], [B_Gm1])
                yield piece

        def run_job(job, n):
            for _ in range(n):
                try:
                    next(job)
                except StopIteration:
                    return

        job0 = mod_job(0, 4)
        job1 = mod_job(1, 7)

        TT(qkrow[:], qkrow[:], qkrow[:], ALU.mult, [B_misc], [B_misc])
        S.op("dve", lambda e: e.tensor_reduce(out=c1[:, 0:1], in_=qkrow[:], axis=AX.X, op=ALU.max), [B_misc], [B_misc])
        TS(c1b[:, 0:1], c1[:, 0:1], -8.0, None, ALU.mult, None, [B_misc], [B_misc])
        MM(PS[2][:, 0:1], onesrow[:, :], c1b[:, 0:1], True, True, [B_misc], [PSB[2]], inc=True)
        S.op("dve", lambda e: e.tensor_copy(out=negc[:], in_=PS[2][:, 0:1]), [PSB[2]], [B_negc])
        ACT(esink[:], PK[:, PK_SINK:PK_SINK + 16], AF.Exp, [B_PK, B_negc], [B_esink], bias=negc[:, 0:1], scale=1.0)

        run_job(job0, 8)

        def norm_tile(l, mf, t0, t1, xsq, B_xsq, rstd, B_rstd, xn, B_xn, PST, dst_fn, dst_bufs_fn, post=None,
                      sq_eng="pool"):
            W = t1 - t0
            for c in range(NCH):
                if sq_eng == "act":
                    ACT(xsq[c % 2][:, 0:W], X[:, c, t0:t1], AF.Square, xb(c, t0, t1), [B_xsq[c % 2]])
                else:
                    TT(xsq[c % 2][:, 0:W], X[:, c, t0:t1], X[:, c, t0:t1], ALU.mult, xb(c, t0, t1), [B_xsq[c % 2]],
                       eng=sq_eng)
                MM(PS[PST][:, 0:W], onesm, xsq[c % 2][:, 0:W], c == 0, c == NCH - 1, [B_xsq[c % 2], B_CM],
                   [PSB[PST]], inc=True)
            ACT(rstd[:, 0:W], PS[PST][:, 0:W], AF.Ln, [PSB[PST], B_eps], [B_rstd], bias=epsT[:, 0:1], scale=1.0)
            ACT(rstd[:, 0:W], rstd[:, 0:W], AF.Exp, [B_rstd], [B_rstd], scale=-0.5)
            wsh = 0 if mf == 0 else 3
            for c in range(NCH):
                STT(xn[c % 2][:, 0:W], X[:, c, t0:t1], Amod[:, l, mf, c:c + 1], rstd[:, 0:W], ALU.mult, ALU.mult,
                    xb(c, t0, t1) + [B_Amod[l][mf], B_rstd], [B_xn[c % 2]])
                ACT(dst_fn(c), xn[c % 2][:, 0:W], AF.Identity, [B_xn[c % 2], B_modv[l][wsh]], dst_bufs_fn(c),
                    bias=modcol(l, wsh, c), scale=1.0)
                if post is not None:
                    post(c)

        cv = Carver()
        wqkv = cv.take([128, 8, 1536], BF16); B_wqkv = Buf()
        wo = cv.take([128, 8, 1024], BF16); B_wo = Buf()
        h0 = cv.take([128, 8, MW], BF16); B_h0 = [Buf() for _ in range(8)]
        qt = [cv.take([128, 8, MW], BF16) for _ in range(2)]; B_qt = [[Buf() for _ in range(8)] for _ in range(2)]
        kT = [cv.take([128, 2, 128 + MW], BF16) for _ in range(2)]; B_kT = [[Buf(), Buf()], [Buf(), Buf()]]
        Vt = [cv.take([128, 4, 4, 65], BF16) for _ in range(2)]; B_V = [[Buf() for _ in range(4)] for _ in range(2)]
        xsq = [cv.take([128, MW], BF16) for _ in range(2)]; B_xsq = [Buf(), Buf()]
        rstd = cv.take([128, MW], F32); B_rstd = Buf()
        xn = [cv.take([128, MW], F32) for _ in range(2)]; B_xn = [Buf(), Buf()]
        sq = [cv.take([128, MW], BF16) for _ in range(2)]; B_sq = [Buf(), Buf()]
        lnr = [cv.take([128, MW], F32) for _ in range(2)]; B_lnr = [Buf(), Buf()]
        qn = [cv.take([128, MW], BF16) for _ in range(2)]; B_qn = [Buf(), Buf()]
        t1b = [cv.take([128, MW], F32) for _ in range(2)]; B_t1 = [Buf(), Buf()]
        t2b = [cv.take([128, MW], F32) for _ in range(2)]; B_t2 = [Buf(), Buf()]
        csb = [cv.take([128, 2, MW], F32) for _ in range(2)]; B_cs = [Buf(), Buf()]
        PT = [cv.take([128, 2, 512], BF16) for _ in range(2)]; B_PT = [[Buf(), Buf()], [Buf(), Buf()]]
        otok = [cv.take([128, 1024], BF16) for _ in range(2)]; B_otok = [[Buf() for _ in range(4)] for _ in range(2)]
        oT = cv.take([128, 8, MW], BF16); B_oT = [Buf() for _ in range(3)]

        S.dma("pool", lambda e: e.dma_start(out=wqkv, in_=wqkv_d), writes=[B_wqkv])
        for (a, b) in xtiles[1:]:
            S.dma("sp", lambda e, a=a, b=b: e.dma_start(out=X[:, :, a:b], in_=xT_v[:, :, a:b]),
                  reads=[B_Amod[0][0]], writes=[bb for c in range(NCH) for bb in xb(c, a, b)])
        S.dma("pool", lambda e: e.dma_start(out=wo, in_=wo_d), writes=[B_wo])
        for par in range(2):
            S.op("dve", lambda e, par=par: e.memset(Vt[par], 1.0), writes=B_V[par])

        PTR = PS[7][:].bitcast(BF16).rearrange("p (c t) -> p c t", c=8)
        ABANK = [0, 1, 2]
        PSTAT = 3
        PPERM = 4
        RING = [5, 6]
        POUT = 7
        SQ_ENG = "act"
        NSQ_ENG = "pool"
        T1_ENG = "pool"
        ADD_ENG = "dve"

        def qk_stream(ti):
            b0, nb, has_q = MIX_TILES[ti]
            par = ti % 2
            t0 = b0 * 128
            W = nb * 128
            t1 = t0 + W
            if ti > 0:
                pnb = MIX_TILES[ti - 1][1]
                for kc in range(2):
                    S.op("dve", lambda e, kc=kc: e.tensor_copy(
                        out=kT[par][:, kc, 0:128], in_=kT[1 - par][:, kc, pnb * 128:pnb * 128 + 128]),
                        [B_kT[1 - par][kc]], [B_kT[par][kc]])
                S.op("dve", lambda e: e.tensor_copy(out=Vt[par][:, 0, :, 0:64], in_=Vt[1 - par][:, pnb, :, 0:64]),
                     [B_V[1 - par][pnb]], [B_V[par][0]])
            S.dma("sp", lambda e: e.dma_start(out=csb[par][:, :, 0:W], in_=cs_d[:, :, t0:t1]), writes=[B_cs[par]])
            for c in range(NCH):
                if NSQ_ENG == "act":
                    ACT(xsq[c % 2][:, 0:W], X[:, c, t0:t1], AF.Square, xb(c, t0, t1), [B_xsq[c % 2]])
                else:
                    TT(xsq[c % 2][:, 0:W], X[:, c, t0:t1], X[:, c, t0:t1], ALU.mult, xb(c, t0, t1), [B_xsq[c % 2]], eng=NSQ_ENG)
                MM(PS[PSTAT][:, 0:W], onesm, xsq[c % 2][:, 0:W], c == 0, c == NCH - 1, [B_xsq[c % 2], B_CM],
                   [PSB[PSTAT]], inc=True)
                if c % 2 == 1:
                    yield
            ACT(rstd[:, 0:W], PS[PSTAT][:, 0:W], AF.Ln, [PSB[PSTAT], B_eps], [B_rstd], bias=epsT[:, 0:1], scale=1.0)
            ACT(rstd[:, 0:W], rstd[:, 0:W], AF.Exp, [B_rstd], [B_rstd], scale=-0.5)
            yield
            for c in range(NCH):
                TT(xn[c % 2][:, 0:W], X[:, c, t0:t1], rstd[:, 0:W], ALU.mult, xb(c, t0, t1) + [B_rstd], [B_xn[c % 2]],
                   eng="pool")
                TS(h0[:, c, 0:W], xn[c % 2][:, 0:W], Amod[:, 0, 0, c:c + 1], modcol(0, 0, c), ALU.mult, ALU.add,
                   [B_xn[c % 2], B_Amod[0][0], B_modv[0][0]], [B_h0[c]], eng="pool")
                if c % 2 == 1:
                    yield
            for bi in range(nb):
                pq = ABANK[bi % 3]
                for k in range(8):
                    MM(PS[pq][:, 0:256], h0[:, k, bi * 128:(bi + 1) * 128], wqkv[:, k, 1280:1536], k == 0, k == 7,
                       [B_wqkv, B_h0[k]], [PSB[pq]], inc=(k == 7))
                S.op("dve", lambda e, pq=pq, bi=bi: e.tensor_copy(
                    out=Vt[par][:, 1 + bi, :, 0:64], in_=PS[pq][:, 0:256].rearrange("p (h d) -> p h d", h=4)),
                    [PSB[pq]], [B_V[par][1 + bi]])
                yield
            ocs = ([("q", i) for i in range(8)] if has_q else []) + [("k", 0), ("k", 1)]
            n = len(ocs)

            def A_(i):
                kind, oc = ocs[i]
                col = oc * 128 if kind == "q" else 1024 + oc * 128
                pq = ABANK[i % 3]
                for k in range(8):
                    MM(PS[pq][:, 0:W], wqkv[:, k, col:col + 128], h0[:, k, 0:W], k == 0, k == 7,
                       [B_wqkv, B_h0[k]], [PSB[pq]], inc=(k == 7))

            def SQ_(i):
                pq = ABANK[i % 3]
                if SQ_ENG == "act":
                    ACT(sq[i % 2][:, 0:W], PS[pq][:, 0:W], AF.Square, [PSB[pq]], [B_sq[i % 2]])
                else:
                    TT(sq[i % 2][:, 0:W], PS[pq][:, 0:W], PS[pq][:, 0:W], ALU.mult, [PSB[pq]], [B_sq[i % 2]], eng=SQ_ENG)

            def ST_(i):
                MM(PS[PSTAT][:, 0:W], bdm, sq[i % 2][:, 0:W], True, True, [B_CM, B_sq[i % 2]], [PSB[PSTAT]], inc=True)

            def LN_(i):
                ACT(lnr[i % 2][:, 0:W], PS[PSTAT][:, 0:W], AF.Ln, [PSB[PSTAT], B_eps], [B_lnr[i % 2]], bias=epsT[:, 0:1], scale=1.0)
                ACT(lnr[i % 2][:, 0:W], lnr[i % 2][:, 0:W], AF.Exp, [B_lnr[i % 2]], [B_lnr[i % 2]], scale=-0.5)

            def QN_(i):
                kind, oc = ocs[i]
                pq = ABANK[i % 3]
                gcol = PK_QKG if kind == "q" else PK_QKG + 1
                STT(qn[i % 2][:, 0:W], PS[pq][:, 0:W], PK[:, gcol:gcol + 1], lnr[i % 2][:, 0:W], ALU.mult, ALU.mult,
                    [PSB[pq], B_PK, B_lnr[i % 2]], [B_qn[i % 2]])

            def PM_(i):
                MM(PS[PPERM][:, 0:W], permm, qn[i % 2][:, 0:W], True, True, [B_CM, B_qn[i % 2]], [PSB[PPERM]], inc=True)

            def T1_(i):
                TT(t1b[i % 2][:, 0:W], qn[i % 2][:, 0:W], csb[par][:, 0, 0:W], ALU.mult, [B_qn[i % 2], B_cs[par]],
                   [B_t1[i % 2]], eng=T1_ENG)

            def T2_(i):
                TT(t2b[i % 2][:, 0:W], PS[PPERM][:, 0:W], csb[par][:, 1, 0:W], ALU.mult, [PSB[PPERM], B_cs[par]], [B_t2[i % 2]])

            def AD_(i):
                kind, oc = ocs[i]
                if kind == "q":
                    dst, dbuf = qt[par][:, oc, 0:W], B_qt[par][oc]
                else:
                    dst, dbuf = kT[par][:, oc, 128:128 + W], B_kT[par][oc]
                TT(dst, t1b[i % 2][:, 0:W], t2b[i % 2][:, 0:W], ALU.add, [B_t1[i % 2], B_t2[i % 2]], [dbuf], eng=ADD_ENG)

            for s in range(n + 4):
                if 0 <= s - 1 < n: SQ_(s - 1)
                if 0 <= s - 4 < n: T2_(s - 4)
                if 0 <= s - 4 < n: AD_(s - 4)
                if 0 <= s - 2 < n: LN_(s - 2)
                if 0 <= s - 2 < n: QN_(s - 2)
                if 0 <= s < n: A_(s)
                if 0 <= s - 1 < n: ST_(s - 1)
                if 0 <= s - 3 < n: PM_(s - 3)
                if 0 <= s - 3 < n: T1_(s - 3)
                yield

        PTH = [PT[0][:, 0, :], PT[0][:, 1, :], PT[1][:, 0, :], PT[1][:, 1, :]]
        B_PTH = [Buf() for _ in range(4)]
        B_POUT = [PSB[7]] * 3
        B_otk = [[Buf() for _ in range(8)] for _ in range(2)]

        def attn_stream(ti):
            b0, nb, has_q = MIX_TILES[ti]
            par = ti % 2
            t0 = b0 * 128
            W = nb * 128
            t1 = t0 + W
            units = [(bi, h, hf) for bi in range(nb) for h in range(4) for hf in range(2)]
            NV = len(units)

            def SC_(v):
                bi, h, hf = units[v]
                gb = b0 + bi
                mprev = MK[:, 2, 0:256] if gb == 2 else MK[:, 1, 0:256]
                mcur = MK[:, 0, 0:256]
                hp = (h % 2) * 64
                kc = h // 2
                bank = RING[v % 2]
                rqh = qt[par][hp:hp + 64, 4 * kc + 2 * hf:4 * kc + 2 * hf + 2, bi * 128:(bi + 1) * 128]
                qbufs = [B_qt[par][4 * kc + 2 * hf + g] for g in range(2)]
                MM(PS[bank][:, 0:256], kT[par][hp:hp + 64, kc, bi * 128:bi * 128 + 128], rqh, True, False,
                   [B_kT[par][kc]] + qbufs, [PSB[bank]], inc=False)
                MM(PS[bank][:, 0:256], ident, mprev, False, True, [B_CM, B_MK], [PSB[bank]], inc=False)
                MM(PS[bank][:, 256:512], kT[par][hp:hp + 64, kc, 128 + bi * 128:256 + bi * 128], rqh, True, False,
                   [B_kT[par][kc]] + qbufs, [PSB[bank]], inc=False)
                MM(PS[bank][:, 256:512], ident, mcur, False, True, [B_CM, B_MK], [PSB[bank]], inc=True)

            def EX_(v):
                bank = RING[v % 2]
                ACT(PTH[v % 4], PS[bank][:, :], AF.Exp, [PSB[bank], B_negc], [B_PTH[v % 4]], bias=negc[:, 0:1], scale=0.125)

            def PV_(v):
                bi, h, hf = units[v]
                r = v % 3
                pt = PTH[v % 4]
                for g in range(2):
                    o0 = r * 130 + g * 65
                    MM(PS[POUT][:, o0:o0 + 65], pt[:, g * 128:(g + 1) * 128], Vt[par][:, bi, h, :],
                       True, False, [B_PTH[v % 4], B_V[par][bi]], [B_POUT[r]], inc=False)
                    MM(PS[POUT][:, o0:o0 + 65], pt[:, 256 + g * 128:256 + (g + 1) * 128], Vt[par][:, bi + 1, h, :],
                       False, True, [B_PTH[v % 4], B_V[par][bi + 1]], [B_POUT[r]], inc=(g == 1))

            def NM_(v):
                bi, h, hf = units[v]
                r = v % 3
                oi = bi % 2
                ov = PS[POUT][:, r * 130:(r + 1) * 130].rearrange("p (g d) -> p g d", g=2)
                di = v % 2
                qh0 = 4 * h + 2 * hf
                TT(den[di][:, 0:2], ov[:, :, 64], esink[:, qh0:qh0 + 2], ALU.add, [B_POUT[r], B_esink], [B_den[di]])
                S.op("dve", lambda e: e.reciprocal(out=den[di][:, 0:2], in_=den[di][:, 0:2]), [B_den[di]], [B_den[di]])
                TT(otok[oi][:, 64 * qh0:64 * qh0 + 128].rearrange("p (g d) -> p g d", g=2), ov[:, :, 0:64],
                   den[di][:, 0:2].unsqueeze(2).to_broadcast([128, 2, 64]), ALU.mult, [B_POUT[r], B_den[di]],
                   [B_otk[oi][2 * h + hf]])
                if h == 3 and hf == 1:
                    for c in range(8):
                        S.op("pe", lambda e, c=c: e.transpose(out=PTR[:, c, :], in_=otok[oi][:, c * 128:(c + 1) * 128],
                                                              identity=ident),
                             [B_otk[oi][c], B_CM], B_POUT, inc=(c == 7))
                    ACT(oT[:, :, bi * 128:(bi + 1) * 128], PTR, AF.Copy, B_POUT, [B_oT[bi]])

            SC_(0)
            for v in range(NV):
                if v + 1 < NV:
                    SC_(v + 1)
                EX_(v)
                PV_(v)
                NM_(v)
                if v % 2 == 1:
                    yield
            for e_ in range(8):
                pq = RING[e_ % 2]
                for f in range(8):
                    MM(PS[pq][:, 0:W], wo[:, f, e_ * 128:(e_ + 1) * 128], oT[:, f, 0:W], f == 0, f == 7,
                       [B_wo] + B_oT[0:nb], [PSB[pq]], inc=(f == 7))
                STT(X[:, e_, t0:t1], PS[pq][:, 0:W], modcol(0, 2, e_), X[:, e_, t0:t1], ALU.mult, ALU.add,
                    [PSB[pq], B_modv[0][2]] + xb(e_, t0, t1), xb(e_, t0, t1))
                yield

        def drain(g):
            for _ in g:
                pass

        def interleave(gens):
            gens = [g for g in gens if g is not None]
            while gens:
                for g in list(gens):
                    try:
                        next(g)
                    except StopIteration:
                        gens.remove(g)

        def job_slices(job, n):
            for _ in range(n):
                run_job(job, 1)
                yield
                yield
                yield

        drain(qk_stream(0))
        run_job(job0, 4)
        drain(qk_stream(1))
        NT_ = len(MIX_TILES)
        for ti in range(1, NT_):
            interleave([qk_stream(ti + 1) if ti + 1 < NT_ else None, attn_stream(ti), job_slices(job0, 2)])
        run_job(job0, NPIECE)
        S.barrier()

        def ffn(l, job, final):
            cv = Carver()
            hF = cv.take([128, 8, NWIN * FW + 2], BF16)
            B_hF = [[Buf() for _ in range(NWIN)] for _ in range(8)]
            abuf = cv.take([128, 6, NWIN * FW], BF16)
            B_a = [[Buf() for _ in range(NWIN)] for _ in range(6)]
            wub = [cv.take([128, 8, 2, 128], BF16) for _ in range(3)]; B_wub = [Buf() for _ in range(3)]
            wdb = cv.take([128, 6, 1024], BF16); B_wdb = Buf()
            tg = [cv.take([128, FW], F32) for _ in range(2)]; B_tg = [Buf(), Buf()]
            tv = [cv.take([128, FW], F32) for _ in range(2)]; B_tv = [Buf(), Buf()]
            sg = [cv.take([128, FW], F32) for _ in range(2)]; B_sg = [Buf(), Buf()]
            xsq = [cv.take([128, FW + 2], BF16) for _ in range(2)]; B_xsq = [Buf(), Buf()]
            rstd = cv.take([128, FW + 2], F32); B_rstd = Buf()
            xn = [cv.take([128, FW + 2], F32) for _ in range(2)]; B_xn = [Buf(), Buf()]

            def issue_wup(j, cnt):
                slot = cnt % 3
                S.dma("pool", lambda e, j=j, slot=slot: e.dma_start(out=wub[slot], in_=wup_d[l, j]), writes=[B_wub[slot]])

            issue_wup(0, 0)
            issue_wup(1, 1)
            def norm_win(w):
                t0 = FS + FW * w - (2 if w == 0 else 0)
                t1 = FS + FW * (w + 1)
                h0c = t0 - (FS - 2)
                norm_tile(l, 1, t0, t1, xsq, B_xsq, rstd, B_rstd, xn, B_xn, 6,
                          lambda c, h0c=h0c, t0=t0, t1=t1: hF[:, c, h0c:h0c + (t1 - t0)], lambda c, w=w: [B_hF[c][w]])
                if w == 0:
                    for c in range(8):
                        TT(hF[:, c, 0:34], hF[:, c, 0:34], PK[:, PK_VMASK + 222:PK_VMASK + 256], ALU.mult,
                           [B_hF[c][0], B_PK], [B_hF[c][0]])
            norm_win(0)
            cnt = 0
            ev = 0
            for qi, (j0, nq) in enumerate(QUARTERS):
                S.dma("pool", lambda e, j0=j0, nq=nq: e.dma_start(out=wdb[:, 0:nq, :], in_=wdown_d[l, :, j0:j0 + nq, :]),
                      writes=[B_wdb])
                for jl in range(nq):
                    j = j0 + jl
                    if j + 2 < NPAIR:
                        issue_wup(j + 2, cnt + 2)
                    slot = cnt % 3
                    cnt += 1
                    for w in range(NWIN):
                        if j == 0 and w + 1 < NWIN:
                            norm_win(w + 1)
                        pg = ev % 2
                        pv = 2 + ev % 2
                        ev += 1
                        hreads = lambda k: [B_hF[k][w]] + ([B_hF[k][w - 1]] if w > 0 else [])
                        for k in range(8):
                            MM(PS[pg][:, 0:FW + 2], wub[slot][:, k, 0, :], hF[:, k, FW * w:FW * w + FW + 2], k == 0, k == 7,
                               [B_wub[slot]] + hreads(k), [PSB[pg]], inc=(k == 7))
                        for k in range(8):
                            MM(PS[pv][:, 0:FW + 2], wub[slot][:, k, 1, :], hF[:, k, FW * w:FW * w + FW + 2], k == 0, k == 7,
                               [B_wub[slot]] + hreads(k), [PSB[pv]], inc=(k == 7))
                        ti = ev % 2

                        def cp(cc, i):
                            o = PK_CONV + (l * 44 + cc) * 4 + i
                            return PK[:, o:o + 1]
                        for (pp, tb, Bt, cc) in ((pg, tg[ti], B_tg[ti], j), (pv, tv[ti], B_tv[ti], 22 + j)):
                            ACT(tb[:, :], PS[pp][:, 2:FW + 2], AF.Identity, [PSB[pp], B_PK], [Bt], bias=cp(cc, 3), scale=cp(cc, 2))
                            STT(tb[:, :], PS[pp][:, 1:FW + 1], cp(cc, 1), tb[:, :], ALU.mult, ALU.add, [PSB[pp], B_PK, Bt], [Bt])
                            STT(tb[:, :], PS[pp][:, 0:FW], cp(cc, 0), tb[:, :], ALU.mult, ALU.add, [PSB[pp], B_PK, Bt], [Bt])
                        ACT(sg[ti][:, :], tg[ti][:, :], AF.Silu, [B_tg[ti]], [B_sg[ti]])
                        TT(abuf[:, jl, FW * w:FW * (w + 1)], sg[ti][:, :], tv[ti][:, :], ALU.mult, [B_sg[ti], B_tv[ti]],
                           [B_a[jl][w]])
                    if job is not None:
                        run_job(job, 2 if j < 2 else 1)
                dcnt = 0
                for w in range(NWIN):
                    a0 = FS + FW * w
                    a1 = a0 + FW
                    for e_ in range(8):
                        pd = 4 + dcnt % 2
                        dcnt += 1
                        for jl in range(nq):
                            MM(PS[pd][:, 0:FW], wdb[:, jl, e_ * 128:(e_ + 1) * 128], abuf[:, jl, FW * w:FW * (w + 1)],
                               jl == 0, jl == nq - 1, [B_wdb, B_a[jl][w]], [PSB[pd]], inc=(jl == nq - 1))
                        STT(X[:, e_, a0:a1], PS[pd][:, 0:FW], modcol(l, 5, e_), X[:, e_, a0:a1], ALU.mult, ALU.add,
                            [PSB[pd], B_modv[l][5]] + xb(e_, a0, a1), xb(e_, a0, a1))
                    if final and qi == len(QUARTERS) - 1:
                        r0 = max(a0, HALO)
                        S.dma("sp", lambda e, r0=r0, a1=a1: e.dma_start(out=yT_v[:, :, r0 - HALO:a1 - HALO], in_=X[:, :, r0:a1]),
                              reads=[bb for c in range(NCH) for bb in xb(c, r0, a1)])
            if job is not None:
                run_job(job, NPIECE)

        def dump_and_finish():
            for (a, b) in ((256, 768), (768, 1280), (1280, 1792), (1792, 2304)):
                S.dma("sp", lambda e, a=a, b=b: e.dma_start(out=yT_v[:, :, a - HALO:b - HALO], in_=X[:, :, a:b]),
                      reads=[bb for c in range(NCH) for bb in xb(c, a, b)])

        done = False
        if stop_after == "mix0":
            dump_and_finish(); done = True
        if not done:
            ffn(0, job1, final=False)
            S.barrier()
            if stop_after == "ffn0":
                dump_and_finish(); done = True

        if not done:
            cv = Carver()
            PWd = FW + 16
            NB4 = 4
            pw = cv.take([128, 4, 2, 256], BF16); B_pw = Buf()
            xsq = [cv.take([128, PWd], BF16) for _ in range(2)]; B_xsq = [Buf(), Buf()]
            rstd2 = [cv.take([128, PWd], F32) for _ in range(2)]; B_rstd2 = [Buf(), Buf()]
            xn = [cv.take([128, PWd], F32) for _ in range(NB4)]; B_xn = [Buf() for _ in range(NB4)]
            h1 = [cv.take([128, PWd], F32) for _ in range(NB4)]; B_h1 = [Buf() for _ in range(NB4)]
            sA = [cv.take([128, PWd], F32) for _ in range(NB4)]; B_sA = [Buf() for _ in range(NB4)]
            sB = [cv.take([128, PWd], F32) for _ in range(NB4)]; B_sB = [Buf() for _ in range(NB4)]
            dT = [cv.take([128, 8, FW], BF16) for _ in range(2)]; B_dT = [[Buf() for _ in range(8)] for _ in range(2)]
            S.dma("pool", lambda e: e.dma_start(out=pw, in_=poolw_d), writes=[B_pw])
            CH_ORDER = [0, 4, 1, 5, 2, 6, 3, 7]
            wins = list(range(NWIN - 1, -1, -1))

            def pstats(wi):
                w = wins[wi]
                r0 = FS + FW * w - 16
                t1 = FS + FW * (w + 1)
                pst = 6 + wi % 2
                for c in range(NCH):
                    ACT(xsq[c % 2][:, 0:PWd], X[:, c, r0:t1], AF.Square, xb(c, r0, t1), [B_xsq[c % 2]])
                    MM(PS[pst][:, 0:PWd], onesm, xsq[c % 2][:, 0:PWd], c == 0, c == NCH - 1, [B_xsq[c % 2], B_CM],
                       [PSB[pst]], inc=True)
                rs, Brs = rstd2[wi % 2], B_rstd2[wi % 2]
                ACT(rs[:, :], PS[pst][:, 0:PWd], AF.Ln, [PSB[pst], B_eps], [Brs], bias=epsT[:, 0:1], scale=1.0)
                ACT(rs[:, :], rs[:, :], AF.Exp, [Brs], [Brs], scale=-0.5)

            pcnt = 0
            pstats(0)
            for wi, w in enumerate(wins):
                t0 = FS + FW * w
                t1 = t0 + FW
                r0 = t0 - 16
                dpar = wi % 2
                rs, Brs = rstd2[wi % 2], B_rstd2[wi % 2]
                if wi + 1 < len(wins):
                    pstats(wi + 1)
                fin = {}

                def st0(ci):
                    c = CH_ORDER[ci]
                    bi4 = ci % NB4
                    hb, Bh = h1[bi4], B_h1[bi4]
                    STT(xn[bi4][:, :], X[:, c, r0:t1], Amod[:, 1, 0, c:c + 1], rs[:, :], ALU.mult, ALU.mult,
                        xb(c, r0, t1) + [B_Amod[1][0], Brs], [B_xn[bi4]])
                    ACT(hb[:, :], xn[bi4][:, :], AF.Identity, [B_xn[bi4], B_modv[1][0]], [Bh], bias=modcol(1, 0, c), scale=1.0)
                    if w == 0:
                        TT(hb[:, 0:48], hb[:, 0:48], PK[:, PK_VMASK + 208:PK_VMASK + 256], ALU.mult, [Bh, B_PK], [Bh])

                def st1(ci):
                    c = CH_ORDER[ci]
                    bi4 = ci % NB4
                    g = c // 2
                    aeng = "pool" if c >= 4 else "dve"
                    src, Bsrc = h1[bi4], B_h1[bi4]
                    bufs = [(sA[bi4], B_sA[bi4]), (sB[bi4], B_sB[bi4])]
                    for step in range(g + 1):
                        sh = 1 << step
                        lo = (1 << (step + 1)) - 1
                        dstb, Bd = bufs[step % 2]
                        TT(dstb[:, lo:PWd], src[:, lo:PWd], src[:, lo - sh:PWd - sh], ALU.add, [Bsrc], [Bd], eng=aeng)
                        src, Bsrc = dstb, Bd
                    fin[ci] = (src, Bsrc)

                def st2(ci):
                    c = CH_ORDER[ci]
                    bi4 = ci % NB4
                    g = c // 2
                    hb, Bh = h1[bi4], B_h1[bi4]
                    src, Bsrc = fin[ci]
                    wd = float(1 << (g + 1))
                    STT(dT[dpar][:, c, :], src[:, 16:PWd], 1.0 / wd, hb[:, 16:PWd], ALU.mult, ALU.subtract,
                        [Bsrc, Bh], [B_dT[dpar][c]])
                    if w == 0:
                        pf, Bp = pfix[c % 2], B_pfix[c % 2]
                        TT(pf[:, :], src[:, 48:64], PK[:, PK_PINV + c * 16:PK_PINV + (c + 1) * 16], ALU.mult, [Bsrc, B_PK], [Bp])
                        TT(dT[dpar][:, c, 32:48], pf[:, :], hb[:, 48:64], ALU.subtract, [Bp, Bh, B_dT[dpar][c]], [B_dT[dpar][c]])

                for it in range(NCH + 3):
                    if 0 <= it - 3 < NCH:
                        st2(it - 3)
                    if 0 <= it - 1 < NCH:
                        st1(it - 1)
                    if it < NCH:
                        st0(it)
                for g in range(4):
                    for eo in range(2):
                        pq = pcnt % 2
                        pcnt += 1
                        ch = 2 * g + eo
                        for ci in range(2):
                            MM(PS[pq][:, 0:FW], pw[:, g, ci, eo * 128:(eo + 1) * 128], dT[dpar][:, 2 * g + ci, :], ci == 0, ci == 1,
                               [B_pw, B_dT[dpar][2 * g + ci]], [PSB[pq]], inc=(ci == 1))
                        STT(X[:, ch, t0:t1], PS[pq][:, 0:FW], Gm1[:, ch:ch + 1], X[:, ch, t0:t1], ALU.mult, ALU.add,
                            [PSB[pq], B_Gm1] + xb(ch, t0, t1), xb(ch, t0, t1))
            S.barrier()
            if stop_after == "mix1":
                dump_and_finish(); done = True
        if not done:
            ffn(1, None, final=True)

        S.finish()
        S.emit()
    return nc


def _const_tables():
    ident = np.eye(128, dtype=np.float32)
    onesm = np.full((128, 128), 1.0 / 1024.0, np.float32)
    bd = np.zeros((128, 128), np.float32)
    bd[0:64, 0:64] = 1.0 / 64.0
    bd[64:128, 64:128] = 1.0 / 64.0
    perm = np.zeros((128, 128), np.float32)
    for m in range(128):
        d = m % 64
        if d < 8:
            perm[m + 8, m] = 1.0
        elif d < 16:
            perm[m - 8, m] = 1.0
    cmat = np.stack([ident, onesm, bd, perm], axis=1)
    j = np.arange(128)[:, None]
    q = np.arange(128)[None, :]
    NEG = -1.0e4
    cur = np.where(j <= q, 0.0, NEG).astype(np.float32)
    prev = np.where(j > q, 0.0, NEG).astype(np.float32)
    cur4 = np.tile(cur, (1, 4))
    prev4 = np.tile(prev, (1, 4))
    allneg = np.full((128, 512), NEG, np.float32)
    return np.ascontiguousarray(cmat), cur4, prev4, allneg


def _rope_table(pos0):
    half = 8
    inv_freq = (500000.0 ** (-np.arange(0, half, dtype=np.float32) * 2.0 / 16.0)).astype(np.float32)
    pos = (pos0 + np.arange(XT)).astype(np.float32)
    ang = pos[None, :] * inv_freq[:, None]
    cos = np.cos(ang).astype(np.float32)
    sin = np.sin(ang).astype(np.float32)
    cs = np.zeros((128, 2, XT), np.float32)
    cs[:, 0, :] = 1.0
    for p in range(128):
        d = p % 64
        if d < 8:
            cs[p, 0] = cos[d]
            cs[p, 1] = -sin[d]
        elif d < 16:
            cs[p, 0] = cos[d - 8]
            cs[p, 1] = sin[d - 8]
    return cs


_PROGRAM_CACHE = {}


def kernel(x, c, mod_w, mod_b, mix_norm_gain, ffn_norm_gain, w_qkv, q_gain, k_gain, sinks, w_o, pool_w,
           pool_scale, w_up, conv_w, conv_b, w_down, _stop_after=None):
    f32 = np.float32
    x = np.asarray(x, f32); c = np.asarray(c, f32)
    mod_w = np.asarray(mod_w, f32); mod_b = np.asarray(mod_b, f32)
    w_qkv = np.asarray(w_qkv, f32); w_o = np.asarray(w_o, f32)
    w_up = np.asarray(w_up, f32); w_down = np.asarray(w_down, f32)
    conv_w = np.asarray(conv_w, f32); conv_b = np.asarray(conv_b, f32)
    pool_w = np.asarray(pool_w, f32)

    modw_r = np.ascontiguousarray(mod_w.reshape(2, 8, 128, NPIECE, 256).transpose(0, 3, 2, 1, 4))
    wq = w_qkv[0]
    qcols = []
    for ci in range(8):
        hi, g = ci // 4, ci % 4
        lo_head = 8 * hi + g
        hi_head = 8 * hi + 4 + g
        qcols += list(range(lo_head * 64, lo_head * 64 + 64)) + list(range(hi_head * 64, hi_head * 64 + 64))
    wq_perm = np.concatenate([wq[:, qcols], wq[:, 1024:]], axis=1)
    wqkv_r = np.ascontiguousarray(wq_perm.reshape(8, 128, 1536).transpose(1, 0, 2))
    wo_r = np.ascontiguousarray(w_o[0].reshape(8, 128, 1024).transpose(1, 0, 2))
    poolw_r = np.ascontiguousarray(pool_w[0].reshape(4, 2, 128, 256).transpose(2, 0, 1, 3))
    wup_r = np.ascontiguousarray(w_up.reshape(2, 8, 128, 2, NPAIR, 128).transpose(0, 4, 2, 1, 3, 5))
    wdown_r = np.ascontiguousarray(w_down.reshape(2, NPAIR, 128, 1024).transpose(0, 2, 1, 3))
    cmat, cur4, prev4, allneg = _const_tables()
    qkrow = np.concatenate([np.asarray(q_gain, f32)[0], np.asarray(k_gain, f32)[0]])[None, :].astype(f32)

    convp = np.zeros((128, 2, 44, 4), f32)
    for l in range(2):
        convp[:, l, :, 0:3] = conv_w[l].reshape(3, 44, 128).transpose(2, 1, 0)
        convp[:, l, :, 3] = conv_b[l].reshape(44, 128).T
    gains = np.stack([np.asarray(mix_norm_gain, f32), np.asarray(ffn_norm_gain, f32)], axis=1)
    gains_r = gains.reshape(2, 2, 8, 128).transpose(3, 0, 1, 2).reshape(128, 32)
    modb_r = mod_b.reshape(2, 48, 128).transpose(2, 0, 1).reshape(128, 96)
    pidx = np.arange(128) % 64
    qkg = np.stack([np.asarray(q_gain, f32)[0][pidx], np.asarray(k_gain, f32)[0][pidx]], axis=1)
    sinks_bc = np.tile(np.asarray(sinks, f32)[0][None, :], (128, 1))
    psc = np.asarray(pool_scale, f32)[0].reshape(8, 128).T

    in_maps = []
    for core in range(8):
        b, q = core // 4, core % 4
        T0 = q * TOK
        xs = np.zeros((XT, D), f32)
        lo = T0 - HALO
        if lo >= 0:
            xs[:] = x[b, lo:T0 + TOK]
        else:
            xs[HALO:] = x[b, 0:TOK]
        pos = lo + np.arange(XT)
        pk = np.zeros((128, PK_N), f32)
        pk[:, PK_C:PK_C + 8] = c[b].reshape(8, 128).T
        pk[:, PK_MODB:PK_MODB + 96] = modb_r
        pk[:, PK_GAIN:PK_GAIN + 32] = gains_r
        pk[:, PK_QKG:PK_QKG + 2] = qkg
        pk[:, PK_SINK:PK_SINK + 16] = sinks_bc
        pk[:, PK_PSC:PK_PSC + 8] = psc
        pk[:, PK_CONV:PK_CONV + 352] = convp.reshape(128, 352)
        pk[:, PK_VMASK:PK_VMASK + 256] = (pos[0:256] >= 0).astype(f32)[None, :]
        pinv = np.zeros((8, 16), f32)
        for ch in range(8):
            wd = float(2 ** (ch // 2 + 1))
            pp = pos[256:272].astype(f32)
            pinv[ch] = 1.0 / np.minimum(pp + 1.0, wd)
        pk[:, PK_PINV:PK_PINV + 128] = pinv.reshape(1, 128)
        masks = np.stack([cur4, prev4, allneg if q == 0 else prev4], axis=1)
        in_maps.append({
            "xT": np.ascontiguousarray(xs.T),
            "pk": pk,
            "qkrow": qkrow,
            "cmat": cmat,
            "masks": np.ascontiguousarray(masks),
            "cs": _rope_table(lo),
            "modw": modw_r,
            "wqkv": wqkv_r,
            "wo": wo_r,
            "poolw": poolw_r,
            "wup": wup_r,
            "wdown": wdown_r,
        })

    key = _stop_after
    if key not in _PROGRAM_CACHE:
        _PROGRAM_CACHE[key] = build_program(_stop_after)
    nc = _PROGRAM_CACHE[key]
    res = run_bass_kernel_spmd(nc, in_maps, core_ids=list(range(8)))
    out = np.zeros((2, SEQ, D), f32)
    for core in range(8):
        b, q = core // 4, core % 4
        out[b, q * TOK:(q + 1) * TOK, :] = res.results[core]["yT"].T
    return out
```

```python
import contextlib
import math
import numpy as np
import concourse.bass as bass
import concourse.mybir as mybir
from concourse.bass_utils import run_bass_kernel_spmd

F32 = mybir.dt.float32
BF16 = mybir.dt.bfloat16
AF = mybir.ActivationFunctionType
ALU = mybir.AluOpType
AX = mybir.AxisListType

D = 1024
NCH = 8
SEQ = 8192
TOK = 2048
HALO = 256
XT = TOK + HALO
NBLK = XT // 128
FS = 224
FW = 416
NWIN = 5
DFF = 2816
NPAIR = 22
QUARTERS = [(0, 6), (6, 6), (12, 5), (17, 5)]
MIX_TILES = [(0, 1, False), (1, 3, True), (4, 3, True), (7, 3, True), (10, 3, True), (13, 3, True), (16, 2, True)]
MW = 384
NPIECE = 24
EPS = 1e-6

PK_C = 0
PK_MODB = 8
PK_GAIN = 104
PK_QKG = 136
PK_SINK = 138
PK_PSC = 154
PK_CONV = 162
PK_VMASK = 514
PK_PINV = 770
PK_N = 898

SAME_ENG_SYNC = True
NSLOT = 6


class Buf:
    __slots__ = ("name", "w", "rs")

    def __init__(self, name=""):
        self.name = name
        self.w = None
        self.rs = {}


class Sched:
    ENGS = ("pe", "act", "dve", "pool", "sp")

    def __init__(self, nc):
        self.nc = nc
        self.streams = {e: [] for e in self.ENGS}
        self.cnt = {e: 0 for e in self.ENGS}
        self.waited = {e: {} for e in self.ENGS}
        self.dma_cnt = {"sp": 0, "pool": 0}
        self.semval = {}
        for q in ("sp", "pool"):
            for s in range(NSLOT):
                self.semval["d%s%d" % (q, s)] = 0

    def _deps(self, eng, reads, writes):
        need = {}

        def add(ev, war):
            if ev is None:
                return
            k, v = ev
            if k == eng:
                if eng == "pe" or not SAME_ENG_SYNC:
                    return
            if k in self.cnt and v > self.cnt[k]:
                raise RuntimeError("forward dependency on pending %s event (deadlock): %s" % (k, eng))
            if v > need.get(k, 0):
                need[k] = v

        for b in reads:
            add(b.w, False)
        for b in writes:
            add(b.w, False)
            for k, v in b.rs.items():
                add((k, v), True)
        return need

    def _commit_waits(self, eng, need):
        wt = self.waited[eng]
        waits = []
        for k, v in need.items():
            if v > wt.get(k, 0):
                wt[k] = v
                waits.append((k, v))
        return waits

    def _mark(self, ev, reads, writes):
        k, v = ev
        for b in reads:
            if v > b.rs.get(k, 0):
                b.rs[k] = v
        for b in writes:
            b.w = ev
            b.rs = {}

    def op(self, eng, fn, reads=(), writes=(), inc=True):
        need = self._deps(eng, reads, writes)
        waits = self._commit_waits(eng, need)
        if inc:
            self.cnt[eng] += 1
            ev = (eng, self.cnt[eng])
        else:
            ev = (eng, self.cnt[eng] + 1)
        self._mark(ev, reads, writes)
        self.streams[eng].append((waits, fn, "inc" if inc else None))

    def dma(self, q, fn, reads=(), writes=()):
        i = self.dma_cnt[q]
        self.dma_cnt[q] += 1
        key = "d%s%d" % (q, i % NSLOT)
        need = self._deps(q, reads, writes)
        pv = self.semval[key]
        if pv > 0:
            need[key] = max(need.get(key, 0), pv)
        waits = self._commit_waits(q, need)
        self.semval[key] = pv + 16
        ev = (key, pv + 16)
        self._mark(ev, reads, writes)
        self.streams[q].append((waits, fn, key))

    def barrier(self):
        need = {e: self.cnt[e] for e in self.ENGS if self.cnt[e] > 0}
        for k, v in self.semval.items():
            if v > 0:
                need[k] = v
        for e in self.ENGS:
            n2 = {k: v for k, v in need.items() if k != e}
            waits = self._commit_waits(e, n2)
            if waits:
                self.streams[e].append((waits, None, None))

    def finish(self):
        need = {k: v for k, v in self.semval.items() if v > 0}
        waits = self._commit_waits("sp", need)
        if waits:
            self.streams["sp"].append((waits, None, None))

    def emit(self):
        nc = self.nc
        names = list(self.ENGS) + sorted(self.semval.keys())
        with contextlib.ExitStack() as st:
            sems = {n: st.enter_context(nc.semaphore("s_" + n)) for n in names}
            block = st.enter_context(nc.Block())

            def run(e, eng):
                for waits, fn, mode in self.streams[e]:
                    for k, v in waits:
                        eng.wait_ge(sems[k], v)
                    if fn is None:
                        continue
                    ins = fn(eng)
                    if mode == "inc":
                        ins.then_inc(sems[e], 1)
                    elif mode is not None:
                        ins.then_inc(sems[mode], 16)

            @block.tensor
            def _(eng):
                run("pe", eng)

            @block.scalar
            def _(eng):
                run("act", eng)

            @block.vector
            def _(eng):
                run("dve", eng)

            @block.gpsimd
            def _(eng):
                run("pool", eng)

            @block.sync
            def _(eng):
                run("sp", eng)


def build_program(stop_after=None):
    nc = bass.Bass("TRN2", target_bir_lowering=False)

    def dram(name, shape, kind="ExternalInput", dt=F32):
        return nc.dram_tensor(name, list(shape), dt, kind=kind).ap()

    xT_d = dram("xT", [D, XT])
    pk_d = dram("pk", [128, PK_N])
    qkrow_d = dram("qkrow", [1, 128])
    cmat_d = dram("cmat", [128, 4, 128])
    masks_d = dram("masks", [128, 3, 512])
    cs_d = dram("cs", [128, 2, XT])
    modw_d = dram("modw", [2, NPIECE, 128, 8, 256])
    wqkv_d = dram("wqkv", [128, 8, 1536])
    wo_d = dram("wo", [128, 8, 1024])
    poolw_d = dram("poolw", [128, 4, 2, 256])
    wup_d = dram("wup", [2, NPAIR, 128, 8, 2, 128])
    wdown_d = dram("wdown", [2, 128, NPAIR, 1024])
    yT_d = dram("yT", [D, TOK], kind="ExternalOutput")
    yT_v = yT_d.rearrange("(c p) t -> p c t", p=128)
    xT_v = xT_d.rearrange("(c p) t -> p c t", p=128)

    S = Sched(nc)
    st = contextlib.ExitStack()
    with st:
        def sb(name, shape, dt):
            return st.enter_context(nc.sbuf_tensor(name, list(shape), dt))

        X = sb("X", [128, NCH, XT], F32)
        PK = sb("PK", [128, PK_N], F32)
        CM = sb("CM", [128, 4, 128], BF16)
        MK = sb("MK", [128, 3, 512], BF16)
        cact = sb("cact", [128, 8], BF16)
        modv = sb("modv", [128, 2, 48], F32)
        Amod = sb("Amod", [128, 2, 2, 8], F32)
        Gm1 = sb("Gm1", [128, 8], F32)
        negc = sb("negc", [128, 1], F32)
        esink = sb("esink", [128, 16], F32)
        epsT = sb("epsT", [128, 1], F32)
        qkrow = sb("qkrow_s", [1, 128], F32)
        onesrow = sb("onesrow", [1, 128], BF16)
        c1 = sb("c1", [1, 2], F32)
        c1b = sb("c1b", [1, 2], BF16)
        den = [sb("den%d" % i, [128, 4], F32) for i in range(2)]
        pfix = [sb("pfix%d" % i, [128, 16], F32) for i in range(2)]
        mwbuf = [sb("mwbuf%d" % i, [128, 8, 256], BF16) for i in range(2)]
        AW = 26800
        ARENA = sb("ARENA", [128, AW], F32)
        PS = [st.enter_context(nc.psum_tensor("PS%d" % i, [128, 512], F32)) for i in range(8)]
        PSB = [Buf("PS%d" % i) for i in range(8)]

        XB = [[Buf("X%d_%d" % (c, b)) for b in range(NBLK)] for c in range(NCH)]

        def xb(c, t0, t1):
            return XB[c][t0 // 128:(t1 - 1) // 128 + 1]

        B_PK = Buf("PK"); B_CM = Buf("CM"); B_MK = Buf("MK"); B_cact = Buf("cact")
        B_modv = [[Buf("modv%d_%d" % (l, i)) for i in range(6)] for l in range(2)]
        B_Amod = [[Buf(), Buf()], [Buf(), Buf()]]
        B_Gm1 = Buf(); B_negc = Buf(); B_esink = Buf(); B_eps = Buf()
        B_misc = Buf("misc")
        B_den = [Buf(), Buf()]; B_pfix = [Buf(), Buf()]
        B_mw = [Buf(), Buf()]

        ident = CM[:, 0, :]
        onesm = CM[:, 1, :]
        bdm = CM[:, 2, :]
        permm = CM[:, 3, :]

        class Carver:
            def __init__(self):
                self.off = 0

            def take(self, shape, dt):
                n = 1
                for s in shape[1:]:
                    n *= s
                words = (n + 1) // 2 if dt == BF16 else n
                a = ARENA[:, self.off:self.off + words]
                self.off += words
                assert self.off <= AW, "arena overflow %d" % self.off
                if dt == BF16:
                    a = a.bitcast(BF16)[:, 0:n]
                if len(shape) == 3:
                    a = a.rearrange("p (a b) -> p a b", a=shape[1])
                elif len(shape) == 4:
                    a = a.rearrange("p (a b c) -> p a b c", a=shape[1], b=shape[2])
                return a

        def ACT(out, in_, func, reads, writes, bias=None, scale=None):
            kw = {}
            if bias is not None:
                kw["bias"] = bias
            if scale is not None:
                kw["scale"] = scale
            S.op("act", lambda e: e.activation(out=out, in_=in_, func=func, **kw), reads, writes)

        def STT(out, in0, scalar, in1, op0, op1, reads, writes, eng="dve"):
            S.op(eng, lambda e: e.scalar_tensor_tensor(out=out, in0=in0, scalar=scalar, in1=in1, op0=op0, op1=op1),
                 reads, writes)

        def TT(out, in0, in1, op, reads, writes, eng="dve"):
            S.op(eng, lambda e: e.tensor_tensor(out=out, in0=in0, in1=in1, op=op), reads, writes)

        def TS(out, in0, s1, s2, op0, op1, reads, writes, eng="dve"):
            if op1 is None:
                S.op(eng, lambda e: e.tensor_scalar(out=out, in0=in0, scalar1=s1, scalar2=None, op0=op0), reads, writes)
            else:
                S.op(eng, lambda e: e.tensor_scalar(out=out, in0=in0, scalar1=s1, scalar2=s2, op0=op0, op1=op1),
                     reads, writes)

        def MM(out, lhsT, rhs, start, stop, reads, writes, inc):
            S.op("pe", lambda e: e.matmul(out, lhsT=lhsT, rhs=rhs, start=start, stop=stop), reads, writes, inc=inc)

        def modcol(l, which, c):
            return modv[:, l, which * 8 + c:which * 8 + c + 1]

        S.dma("sp", lambda e: e.dma_start(out=PK[:], in_=pk_d), writes=[B_PK])
        S.dma("sp", lambda e: e.dma_start(out=qkrow[:], in_=qkrow_d), writes=[B_misc])
        S.dma("pool", lambda e: e.dma_start(out=CM[:], in_=cmat_d), writes=[B_CM])
        S.dma("pool", lambda e: e.dma_start(out=MK[:], in_=masks_d), writes=[B_MK])
        xtiles = [(0, 512), (512, 1024), (1024, 1536), (1536, 2048), (2048, 2304)]
        for (a, b) in xtiles[:1]:
            S.dma("sp", lambda e, a=a, b=b: e.dma_start(out=X[:, :, a:b], in_=xT_v[:, :, a:b]),
                  writes=[bb for c in range(NCH) for bb in xb(c, a, b)])

        S.op("dve", lambda e: e.memset(epsT[:], EPS), writes=[B_eps])
        S.op("dve", lambda e: e.memset(onesrow[:], 1.0), writes=[B_misc])
        ACT(cact[:], PK[:, PK_C:PK_C + 8], AF.Silu, [B_PK], [B_cact])

        def mod_job(l, PMOD):
            def issue(piece):
                S.dma("pool", lambda e, piece=piece: e.dma_start(out=mwbuf[piece % 2][:], in_=modw_d[l, piece]),
                      writes=[B_mw[piece % 2]])
            issue(0)
            for piece in range(NPIECE):
                if piece + 1 < NPIECE:
                    issue(piece + 1)
                mb = mwbuf[piece % 2]
                for m in range(2):
                    for k in range(8):
                        MM(PS[PMOD][:, m:m + 1], mb[:, k, m * 128:(m + 1) * 128], cact[:, k:k + 1],
                           k == 0, k == 7, [B_mw[piece % 2], B_cact], [PSB[PMOD]], inc=(m == 1 and k == 7))
                which = piece // 4
                TT(modv[:, l, 2 * piece:2 * piece + 2], PS[PMOD][:, 0:2],
                   PK[:, PK_MODB + l * 48 + 2 * piece:PK_MODB + l * 48 + 2 * piece + 2], ALU.add,
                   [PSB[PMOD], B_PK], [B_modv[l][which]])
                if piece in (7, 19):
                    mf = 0 if piece == 7 else 1
                    wsc = 1 if mf == 0 else 4
                    STT(Amod[:, l, mf, :], modv[:, l, wsc * 8:wsc * 8 + 8], 1.0,
                        PK[:, PK_GAIN + (l * 2 + mf) * 8:PK_GAIN + (l * 2 + mf) * 8 + 8], ALU.add, ALU.mult,
                        [B_modv[l][wsc], B_PK], [B_Amod[l][mf]])
                if piece == 11 and l == 1:
                    TT(Gm1[:], modv[:, 1, 16:24], PK[:, PK_PSC:PK_PSC + 8], ALU.mult, [B_modv[1][2], B_PK], [B_Gm1])
                yield piece

        def run_job(job, n):
            for _ in range(n):
                try:
                    next(job)
                except StopIteration:
                    return

        job0 = mod_job(0, 4)
        job1 = mod_job(1, 7)

        TT(qkrow[:], qkrow[:], qkrow[:], ALU.mult, [B_misc], [B_misc])
        S.op("dve", lambda e: e.tensor_reduce(out=c1[:, 0:1], in_=qkrow[:], axis=AX.X, op=ALU.max), [B_misc], [B_misc])
        TS(c1b[:, 0:1], c1[:, 0:1], -8.0, None, ALU.mult, None, [B_misc], [B_misc])
        MM(PS[2][:, 0:1], onesrow[:, :], c1b[:, 0:1], True, True, [B_misc], [PSB[2]], inc=True)
        S.op("dve", lambda e: e.tensor_copy(out=negc[:], in_=PS[2][:, 0:1]), [PSB[2]], [B_negc])
        ACT(esink[:], PK[:, PK_SINK:PK_SINK + 16], AF.Exp, [B_PK, B_negc], [B_esink], bias=negc[:, 0:1], scale=1.0)

        run_job(job0, 8)

        def norm_tile(l, mf, t0, t1, xsq, B_xsq, rstd, B_rstd, xn, B_xn, PST, dst_fn, dst_bufs_fn, post=None,
                      sq_eng="pool"):
            W = t1 - t0
            for c in range(NCH):
                if sq_eng == "act":
                    ACT(xsq[c % 2][:, 0:W], X[:, c, t0:t1], AF.Square, xb(c, t0, t1), [B_xsq[c % 2]])
                else:
                    TT(xsq[c % 2][:, 0:W], X[:, c, t0:t1], X[:, c, t0:t1], ALU.mult, xb(c, t0, t1), [B_xsq[c % 2]],
                       eng=sq_eng)
                MM(PS[PST][:, 0:W], onesm, xsq[c % 2][:, 0:W], c == 0, c == NCH - 1, [B_xsq[c % 2], B_CM],
                   [PSB[PST]], inc=True)
            ACT(rstd[:, 0:W], PS[PST][:, 0:W], AF.Ln, [PSB[PST], B_eps], [B_rstd], bias=epsT[:, 0:1], scale=1.0)
            ACT(rstd[:, 0:W], rstd[:, 0:W], AF.Exp, [B_rstd], [B_rstd], scale=-0.5)
            wsh = 0 if mf == 0 else 3
            for c in range(NCH):
                STT(xn[c % 2][:, 0:W], X[:, c, t0:t1], Amod[:, l, mf, c:c + 1], rstd[:, 0:W], ALU.mult, ALU.mult,
                    xb(c, t0, t1) + [B_Amod[l][mf], B_rstd], [B_xn[c % 2]])
                ACT(dst_fn(c), xn[c % 2][:, 0:W], AF.Identity, [B_xn[c % 2], B_modv[l][wsh]], dst_bufs_fn(c),
                    bias=modcol(l, wsh, c), scale=1.0)
                if post is not None:
                    post(c)

        cv = Carver()
        wqkv = cv.take([128, 8, 1536], BF16); B_wqkv = Buf()
        wo = cv.take([128, 8, 1024], BF16); B_wo = Buf()
        h0 = cv.take([128, 8, MW], BF16); B_h0 = [Buf() for _ in range(8)]
        qt = [cv.take([128, 8, MW], BF16) for _ in range(2)]; B_qt = [[Buf() for _ in range(8)] for _ in range(2)]
        kT = [cv.take([128, 2, 128 + MW], BF16) for _ in range(2)]; B_kT = [[Buf(), Buf()], [Buf(), Buf()]]
        Vt = [cv.take([128, 4, 4, 65], BF16) for _ in range(2)]; B_V = [[Buf() for _ in range(4)] for _ in range(2)]
        xsq = [cv.take([128, MW], BF16) for _ in range(2)]; B_xsq = [Buf(), Buf()]
        rstd = cv.take([128, MW], F32); B_rstd = Buf()
        xn = [cv.take([128, MW], F32) for _ in range(2)]; B_xn = [Buf(), Buf()]
        sq = [cv.take([128, MW], BF16) for _ in range(2)]; B_sq = [Buf(), Buf()]
        lnr = [cv.take([128, MW], F32) for _ in range(2)]; B_lnr = [Buf(), Buf()]
        qn = [cv.take([128, MW], BF16) for _ in range(2)]; B_qn = [Buf(), Buf()]
        t1b = [cv.take([128, MW], F32) for _ in range(2)]; B_t1 = [Buf(), Buf()]
        t2b = [cv.take([128, MW], F32) for _ in range(2)]; B_t2 = [Buf(), Buf()]
        csb = [cv.take([128, 2, MW], F32) for _ in range(2)]; B_cs = [Buf(), Buf()]
        PT = [cv.take([128, 2, 512], BF16) for _ in range(2)]; B_PT = [[Buf(), Buf()], [Buf(), Buf()]]
        otok = [cv.take([128, 1024], BF16) for _ in range(2)]; B_otok = [[Buf() for _ in range(4)] for _ in range(2)]
        oT = cv.take([128, 8, MW], BF16); B_oT = [Buf() for _ in range(3)]

        S.dma("pool", lambda e: e.dma_start(out=wqkv, in_=wqkv_d), writes=[B_wqkv])
        for (a, b) in xtiles[1:]:
            S.dma("sp", lambda e, a=a, b=b: e.dma_start(out=X[:, :, a:b], in_=xT_v[:, :, a:b]),
                  reads=[B_Amod[0][0]], writes=[bb for c in range(NCH) for bb in xb(c, a, b)])
        S.dma("pool", lambda e: e.dma_start(out=wo, in_=wo_d), writes=[B_wo])
        for par in range(2):
            S.op("dve", lambda e, par=par: e.memset(Vt[par], 1.0), writes=B_V[par])

        PTR = PS[7][:].bitcast(BF16).rearrange("p (c t) -> p c t", c=8)
        ABANK = [0, 1, 2]
        PSTAT = 3
        PPERM = 4
        RING = [5, 6]
        POUT = 7
        SQ_ENG = "act"
        NSQ_ENG = "pool"
        T1_ENG = "pool"
        ADD_ENG = "dve"

        def qk_stream(ti):
            b0, nb, has_q = MIX_TILES[ti]
            par = ti % 2
            t0 = b0 * 128
            W = nb * 128
            t1 = t0 + W
            if ti > 0:
                pnb = MIX_TILES[ti - 1][1]
                for kc in range(2):
                    S.op("dve", lambda e, kc=kc: e.tensor_copy(
                        out=kT[par][:, kc, 0:128], in_=kT[1 - par][:, kc, pnb * 128:pnb * 128 + 128]),
                        [B_kT[1 - par][kc]], [B_kT[par][kc]])
                S.op("dve", lambda e: e.tensor_copy(out=Vt[par][:, 0, :, 0:64], in_=Vt[1 - par][:, pnb, :, 0:64]),
                     [B_V[1 - par][pnb]], [B_V[par][0]])
            S.dma("sp", lambda e: e.dma_start(out=csb[par][:, :, 0:W], in_=cs_d[:, :, t0:t1]), writes=[B_cs[par]])
            fastn = ti <= 1
            for c in range(NCH):
                if NSQ_ENG == "act" or fastn:
                    ACT(xsq[c % 2][:, 0:W], X[:, c, t0:t1], AF.Square, xb(c, t0, t1), [B_xsq[c % 2]])
                else:
                    TT(xsq[c % 2][:, 0:W], X[:, c, t0:t1], X[:, c, t0:t1], ALU.mult, xb(c, t0, t1), [B_xsq[c % 2]], eng=NSQ_ENG)
                MM(PS[PSTAT][:, 0:W], onesm, xsq[c % 2][:, 0:W], c == 0, c == NCH - 1, [B_xsq[c % 2], B_CM],
                   [PSB[PSTAT]], inc=True)
                if c % 2 == 1:
                    yield
            ACT(rstd[:, 0:W], PS[PSTAT][:, 0:W], AF.Ln, [PSB[PSTAT], B_eps], [B_rstd], bias=epsT[:, 0:1], scale=1.0)
            ACT(rstd[:, 0:W], rstd[:, 0:W], AF.Exp, [B_rstd], [B_rstd], scale=-0.5)
            yield
            for c in range(NCH):
                if fastn:
                    STT(xn[c % 2][:, 0:W], X[:, c, t0:t1], Amod[:, 0, 0, c:c + 1], rstd[:, 0:W], ALU.mult, ALU.mult,
                        xb(c, t0, t1) + [B_Amod[0][0], B_rstd], [B_xn[c % 2]])
                    ACT(h0[:, c, 0:W], xn[c % 2][:, 0:W], AF.Identity, [B_xn[c % 2], B_modv[0][0]], [B_h0[c]],
                        bias=modcol(0, 0, c), scale=1.0)
                else:
                    TT(xn[c % 2][:, 0:W], X[:, c, t0:t1], rstd[:, 0:W], ALU.mult, xb(c, t0, t1) + [B_rstd], [B_xn[c % 2]],
                       eng="pool")
                    TS(h0[:, c, 0:W], xn[c % 2][:, 0:W], Amod[:, 0, 0, c:c + 1], modcol(0, 0, c), ALU.mult, ALU.add,
                       [B_xn[c % 2], B_Amod[0][0], B_modv[0][0]], [B_h0[c]], eng="pool")
                if c % 2 == 1:
                    yield
            for bi in range(nb):
                pq = ABANK[bi % 3]
                for k in range(8):
                    MM(PS[pq][:, 0:256], h0[:, k, bi * 128:(bi + 1) * 128], wqkv[:, k, 1280:1536], k == 0, k == 7,
                       [B_wqkv, B_h0[k]], [PSB[pq]], inc=(k == 7))
                S.op("dve", lambda e, pq=pq, bi=bi: e.tensor_copy(
                    out=Vt[par][:, 1 + bi, :, 0:64], in_=PS[pq][:, 0:256].rearrange("p (h d) -> p h d", h=4)),
                    [PSB[pq]], [B_V[par][1 + bi]])
                yield
            ocs = ([("q", i) for i in range(8)] if has_q else []) + [("k", 0), ("k", 1)]
            n = len(ocs)

            def A_(i):
                kind, oc = ocs[i]
                col = oc * 128 if kind == "q" else 1024 + oc * 128
                pq = ABANK[i % 3]
                for k in range(8):
                    MM(PS[pq][:, 0:W], wqkv[:, k, col:col + 128], h0[:, k, 0:W], k == 0, k == 7,
                       [B_wqkv, B_h0[k]], [PSB[pq]], inc=(k == 7))

            def SQ_(i):
                pq = ABANK[i % 3]
                if SQ_ENG == "act":
                    ACT(sq[i % 2][:, 0:W], PS[pq][:, 0:W], AF.Square, [PSB[pq]], [B_sq[i % 2]])
                else:
                    TT(sq[i % 2][:, 0:W], PS[pq][:, 0:W], PS[pq][:, 0:W], ALU.mult, [PSB[pq]], [B_sq[i % 2]], eng=SQ_ENG)

            def ST_(i):
                MM(PS[PSTAT][:, 0:W], bdm, sq[i % 2][:, 0:W], True, True, [B_CM, B_sq[i % 2]], [PSB[PSTAT]], inc=True)

            def LN_(i):
                ACT(lnr[i % 2][:, 0:W], PS[PSTAT][:, 0:W], AF.Ln, [PSB[PSTAT], B_eps], [B_lnr[i % 2]], bias=epsT[:, 0:1], scale=1.0)
                ACT(lnr[i % 2][:, 0:W], lnr[i % 2][:, 0:W], AF.Exp, [B_lnr[i % 2]], [B_lnr[i % 2]], scale=-0.5)

            def QN_(i):
                kind, oc = ocs[i]
                pq = ABANK[i % 3]
                gcol = PK_QKG if kind == "q" else PK_QKG + 1
                STT(qn[i % 2][:, 0:W], PS[pq][:, 0:W], PK[:, gcol:gcol + 1], lnr[i % 2][:, 0:W], ALU.mult, ALU.mult,
                    [PSB[pq], B_PK, B_lnr[i % 2]], [B_qn[i % 2]])

            def PM_(i):
                MM(PS[PPERM][:, 0:W], permm, qn[i % 2][:, 0:W], True, True, [B_CM, B_qn[i % 2]], [PSB[PPERM]], inc=True)

            def T1_(i):
                TT(t1b[i % 2][:, 0:W], qn[i % 2][:, 0:W], csb[par][:, 0, 0:W], ALU.mult, [B_qn[i % 2], B_cs[par]],
                   [B_t1[i % 2]], eng=T1_ENG)

            def T2_(i):
                TT(t2b[i % 2][:, 0:W], PS[PPERM][:, 0:W], csb[par][:, 1, 0:W], ALU.mult, [PSB[PPERM], B_cs[par]], [B_t2[i % 2]])

            def AD_(i):
                kind, oc = ocs[i]
                if kind == "q":
                    dst, dbuf = qt[par][:, oc, 0:W], B_qt[par][oc]
                else:
                    dst, dbuf = kT[par][:, oc, 128:128 + W], B_kT[par][oc]
                TT(dst, t1b[i % 2][:, 0:W], t2b[i % 2][:, 0:W], ALU.add, [B_t1[i % 2], B_t2[i % 2]], [dbuf], eng=ADD_ENG)

            for s in range(n + 4):
                if 0 <= s - 1 < n: SQ_(s - 1)
                if 0 <= s - 4 < n: T2_(s - 4)
                if 0 <= s - 4 < n: AD_(s - 4)
                if 0 <= s - 2 < n: LN_(s - 2)
                if 0 <= s - 2 < n: QN_(s - 2)
                if 0 <= s < n: A_(s)
                if 0 <= s - 1 < n: ST_(s - 1)
                if 0 <= s - 3 < n: PM_(s - 3)
                if 0 <= s - 3 < n: T1_(s - 3)
                yield

        PTH = [PT[0][:, 0, :], PT[0][:, 1, :], PT[1][:, 0, :], PT[1][:, 1, :]]
        B_PTH = [Buf() for _ in range(4)]
        B_POUT = [PSB[7]] * 3
        B_otk = [[Buf() for _ in range(8)] for _ in range(2)]

        def attn_stream(ti):
            b0, nb, has_q = MIX_TILES[ti]
            par = ti % 2
            t0 = b0 * 128
            W = nb * 128
            t1 = t0 + W
            units = [(bi, h, hf) for bi in range(nb) for h in range(4) for hf in range(2)]
            NV = len(units)

            def SC_(v):
                bi, h, hf = units[v]
                gb = b0 + bi
                mprev = MK[:, 2, 0:256] if gb == 2 else MK[:, 1, 0:256]
                mcur = MK[:, 0, 0:256]
                hp = (h % 2) * 64
                kc = h // 2
                bank = RING[v % 2]
                rqh = qt[par][hp:hp + 64, 4 * kc + 2 * hf:4 * kc + 2 * hf + 2, bi * 128:(bi + 1) * 128]
                qbufs = [B_qt[par][4 * kc + 2 * hf + g] for g in range(2)]
                MM(PS[bank][:, 0:256], kT[par][hp:hp + 64, kc, bi * 128:bi * 128 + 128], rqh, True, False,
                   [B_kT[par][kc]] + qbufs, [PSB[bank]], inc=False)
                MM(PS[bank][:, 0:256], ident, mprev, False, True, [B_CM, B_MK], [PSB[bank]], inc=False)
                MM(PS[bank][:, 256:512], kT[par][hp:hp + 64, kc, 128 + bi * 128:256 + bi * 128], rqh, True, False,
                   [B_kT[par][kc]] + qbufs, [PSB[bank]], inc=False)
                MM(PS[bank][:, 256:512], ident, mcur, False, True, [B_CM, B_MK], [PSB[bank]], inc=True)

            def EX_(v):
                bank = RING[v % 2]
                ACT(PTH[v % 4], PS[bank][:, :], AF.Exp, [PSB[bank], B_negc], [B_PTH[v % 4]], bias=negc[:, 0:1], scale=0.125)

            def PV_(v):
                bi, h, hf = units[v]
                r = v % 3
                pt = PTH[v % 4]
                for g in range(2):
                    o0 = r * 130 + g * 65
                    MM(PS[POUT][:, o0:o0 + 65], pt[:, g * 128:(g + 1) * 128], Vt[par][:, bi, h, :],
                       True, False, [B_PTH[v % 4], B_V[par][bi]], [B_POUT[r]], inc=False)
                    MM(PS[POUT][:, o0:o0 + 65], pt[:, 256 + g * 128:256 + (g + 1) * 128], Vt[par][:, bi + 1, h, :],
                       False, True, [B_PTH[v % 4], B_V[par][bi + 1]], [B_POUT[r]], inc=(g == 1))

            def NM_(v):
                bi, h, hf = units[v]
                r = v % 3
                oi = bi % 2
                ov = PS[POUT][:, r * 130:(r + 1) * 130].rearrange("p (g d) -> p g d", g=2)
                di = v % 2
                qh0 = 4 * h + 2 * hf
                TT(den[di][:, 0:2], ov[:, :, 64], esink[:, qh0:qh0 + 2], ALU.add, [B_POUT[r], B_esink], [B_den[di]])
                S.op("dve", lambda e: e.reciprocal(out=den[di][:, 0:2], in_=den[di][:, 0:2]), [B_den[di]], [B_den[di]])
                TT(otok[oi][:, 64 * qh0:64 * qh0 + 128].rearrange("p (g d) -> p g d", g=2), ov[:, :, 0:64],
                   den[di][:, 0:2].unsqueeze(2).to_broadcast([128, 2, 64]), ALU.mult, [B_POUT[r], B_den[di]],
                   [B_otk[oi][2 * h + hf]])
                if h == 3 and hf == 1:
                    for c in range(8):
                        S.op("pe", lambda e, c=c: e.transpose(out=PTR[:, c, :], in_=otok[oi][:, c * 128:(c + 1) * 128],
                                                              identity=ident),
                             [B_otk[oi][c], B_CM], B_POUT, inc=(c == 7))
                    ACT(oT[:, :, bi * 128:(bi + 1) * 128], PTR, AF.Copy, B_POUT, [B_oT[bi]])

            SC_(0)
            for v in range(NV):
                if v + 1 < NV:
                    SC_(v + 1)
                EX_(v)
                PV_(v)
                NM_(v)
                if v % 2 == 1:
                    yield
            for e_ in range(8):
                pq = RING[e_ % 2]
                for f in range(8):
                    MM(PS[pq][:, 0:W], wo[:, f, e_ * 128:(e_ + 1) * 128], oT[:, f, 0:W], f == 0, f == 7,
                       [B_wo] + B_oT[0:nb], [PSB[pq]], inc=(f == 7))
                STT(X[:, e_, t0:t1], PS[pq][:, 0:W], modcol(0, 2, e_), X[:, e_, t0:t1], ALU.mult, ALU.add,
                    [PSB[pq], B_modv[0][2]] + xb(e_, t0, t1), xb(e_, t0, t1))
                yield

        def drain(g):
            for _ in g:
                pass

        def interleave(gens):
            gens = [g for g in gens if g is not None]
            while gens:
                for g in list(gens):
                    try:
                        next(g)
                    except StopIteration:
                        gens.remove(g)

        def job_slices(job, n):
            for _ in range(n):
                run_job(job, 1)
                yield
                yield
                yield

        drain(qk_stream(0))
        run_job(job0, 4)
        drain(qk_stream(1))
        NT_ = len(MIX_TILES)
        for ti in range(1, NT_):
            interleave([qk_stream(ti + 1) if ti + 1 < NT_ else None, attn_stream(ti), job_slices(job0, 2)])
        run_job(job0, NPIECE)
        S.barrier()

        def ffn(l, job, final):
            cv = Carver()
            hF = cv.take([128, 8, NWIN * FW + 2], BF16)
            B_hF = [[Buf() for _ in range(NWIN)] for _ in range(8)]
            abuf = cv.take([128, 6, NWIN * FW], BF16)
            B_a = [[Buf() for _ in range(NWIN)] for _ in range(6)]
            wub = [cv.take([128, 8, 2, 128], BF16) for _ in range(3)]; B_wub = [Buf() for _ in range(3)]
            wdb = cv.take([128, 6, 1024], BF16); B_wdb = Buf()
            tg = [cv.take([128, FW], F32) for _ in range(2)]; B_tg = [Buf(), Buf()]
            tv = [cv.take([128, FW], F32) for _ in range(2)]; B_tv = [Buf(), Buf()]
            sg = [cv.take([128, FW], F32) for _ in range(2)]; B_sg = [Buf(), Buf()]
            xsq = [cv.take([128, FW + 2], BF16) for _ in range(2)]; B_xsq = [Buf(), Buf()]
            rstd = cv.take([128, FW + 2], F32); B_rstd = Buf()
            xn = [cv.take([128, FW + 2], F32) for _ in range(2)]; B_xn = [Buf(), Buf()]

            def issue_wup(j, cnt):
                slot = cnt % 3
                S.dma("pool", lambda e, j=j, slot=slot: e.dma_start(out=wub[slot], in_=wup_d[l, j]), writes=[B_wub[slot]])

            issue_wup(0, 0)
            issue_wup(1, 1)
            def norm_win(w):
                t0 = FS + FW * w - (2 if w == 0 else 0)
                t1 = FS + FW * (w + 1)
                h0c = t0 - (FS - 2)
                norm_tile(l, 1, t0, t1, xsq, B_xsq, rstd, B_rstd, xn, B_xn, 6,
                          lambda c, h0c=h0c, t0=t0, t1=t1: hF[:, c, h0c:h0c + (t1 - t0)], lambda c, w=w: [B_hF[c][w]])
                if w == 0:
                    for c in range(8):
                        TT(hF[:, c, 0:34], hF[:, c, 0:34], PK[:, PK_VMASK + 222:PK_VMASK + 256], ALU.mult,
                           [B_hF[c][0], B_PK], [B_hF[c][0]])
            norm_win(0)
            cnt = 0
            ev = 0
            for qi, (j0, nq) in enumerate(QUARTERS):
                S.dma("pool", lambda e, j0=j0, nq=nq: e.dma_start(out=wdb[:, 0:nq, :], in_=wdown_d[l, :, j0:j0 + nq, :]),
                      writes=[B_wdb])
                for jl in range(nq):
                    j = j0 + jl
                    if j + 2 < NPAIR:
                        issue_wup(j + 2, cnt + 2)
                    slot = cnt % 3
                    cnt += 1
                    for w in range(NWIN):
                        if j == 0 and w + 1 < NWIN:
                            norm_win(w + 1)
                        pg = ev % 2
                        pv = 2 + ev % 2
                        ev += 1
                        hreads = lambda k: [B_hF[k][w]] + ([B_hF[k][w - 1]] if w > 0 else [])
                        for k in range(8):
                            MM(PS[pg][:, 0:FW + 2], wub[slot][:, k, 0, :], hF[:, k, FW * w:FW * w + FW + 2], k == 0, k == 7,
                               [B_wub[slot]] + hreads(k), [PSB[pg]], inc=(k == 7))
                        for k in range(8):
                            MM(PS[pv][:, 0:FW + 2], wub[slot][:, k, 1, :], hF[:, k, FW * w:FW * w + FW + 2], k == 0, k == 7,
                               [B_wub[slot]] + hreads(k), [PSB[pv]], inc=(k == 7))
                        ti = ev % 2

                        def cp(cc, i):
                            o = PK_CONV + (l * 44 + cc) * 4 + i
                            return PK[:, o:o + 1]
                        for (pp, tb, Bt, cc) in ((pg, tg[ti], B_tg[ti], j), (pv, tv[ti], B_tv[ti], 22 + j)):
                            ACT(tb[:, :], PS[pp][:, 2:FW + 2], AF.Identity, [PSB[pp], B_PK], [Bt], bias=cp(cc, 3), scale=cp(cc, 2))
                            STT(tb[:, :], PS[pp][:, 1:FW + 1], cp(cc, 1), tb[:, :], ALU.mult, ALU.add, [PSB[pp], B_PK, Bt], [Bt])
                            STT(tb[:, :], PS[pp][:, 0:FW], cp(cc, 0), tb[:, :], ALU.mult, ALU.add, [PSB[pp], B_PK, Bt], [Bt])
                        ACT(sg[ti][:, :], tg[ti][:, :], AF.Silu, [B_tg[ti]], [B_sg[ti]])
                        TT(abuf[:, jl, FW * w:FW * (w + 1)], sg[ti][:, :], tv[ti][:, :], ALU.mult, [B_sg[ti], B_tv[ti]],
                           [B_a[jl][w]])
                    if job is not None:
                        run_job(job, 2 if j < 2 else 1)
                dcnt = 0
                for w in range(NWIN):
                    a0 = FS + FW * w
                    a1 = a0 + FW
                    for e_ in range(8):
                        pd = 4 + dcnt % 2
                        dcnt += 1
                        for jl in range(nq):
                            MM(PS[pd][:, 0:FW], wdb[:, jl, e_ * 128:(e_ + 1) * 128], abuf[:, jl, FW * w:FW * (w + 1)],
                               jl == 0, jl == nq - 1, [B_wdb, B_a[jl][w]], [PSB[pd]], inc=(jl == nq - 1))
                        STT(X[:, e_, a0:a1], PS[pd][:, 0:FW], modcol(l, 5, e_), X[:, e_, a0:a1], ALU.mult, ALU.add,
                            [PSB[pd], B_modv[l][5]] + xb(e_, a0, a1), xb(e_, a0, a1))
                    if final and qi == len(QUARTERS) - 1:
                        r0 = max(a0, HALO)
                        S.dma("sp", lambda e, r0=r0, a1=a1: e.dma_start(out=yT_v[:, :, r0 - HALO:a1 - HALO], in_=X[:, :, r0:a1]),
                              reads=[bb for c in range(NCH) for bb in xb(c, r0, a1)])
            if job is not None:
                run_job(job, NPIECE)

        def dump_and_finish():
            for (a, b) in ((256, 768), (768, 1280), (1280, 1792), (1792, 2304)):
                S.dma("sp", lambda e, a=a, b=b: e.dma_start(out=yT_v[:, :, a - HALO:b - HALO], in_=X[:, :, a:b]),
                      reads=[bb for c in range(NCH) for bb in xb(c, a, b)])

        done = False
        if stop_after == "mix0":
            dump_and_finish(); done = True
        if not done:
            ffn(0, job1, final=False)
            S.barrier()
            if stop_after == "ffn0":
                dump_and_finish(); done = True

        if not done:
            cv = Carver()
            PWd = FW + 16
            NB4 = 4
            pw = cv.take([128, 4, 2, 256], BF16); B_pw = Buf()
            xsq = [cv.take([128, PWd], BF16) for _ in range(2)]; B_xsq = [Buf(), Buf()]
            rstd2 = [cv.take([128, PWd], F32) for _ in range(2)]; B_rstd2 = [Buf(), Buf()]
            xn = [cv.take([128, PWd], F32) for _ in range(NB4)]; B_xn = [Buf() for _ in range(NB4)]
            h1 = [cv.take([128, PWd], F32) for _ in range(NB4)]; B_h1 = [Buf() for _ in range(NB4)]
            sA = [cv.take([128, PWd], F32) for _ in range(NB4)]; B_sA = [Buf() for _ in range(NB4)]
            sB = [cv.take([128, PWd], F32) for _ in range(NB4)]; B_sB = [Buf() for _ in range(NB4)]
            dT = [cv.take([128, 8, FW], BF16) for _ in range(2)]; B_dT = [[Buf() for _ in range(8)] for _ in range(2)]
            S.dma("pool", lambda e: e.dma_start(out=pw, in_=poolw_d), writes=[B_pw])
            CH_ORDER = [0, 4, 1, 5, 2, 6, 3, 7]
            wins = list(range(NWIN - 1, -1, -1))

            def pstats(wi):
                w = wins[wi]
                r0 = FS + FW * w - 16
                t1 = FS + FW * (w + 1)
                pst = 6 + wi % 2
                for c in range(NCH):
                    ACT(xsq[c % 2][:, 0:PWd], X[:, c, r0:t1], AF.Square, xb(c, r0, t1), [B_xsq[c % 2]])
                    MM(PS[pst][:, 0:PWd], onesm, xsq[c % 2][:, 0:PWd], c == 0, c == NCH - 1, [B_xsq[c % 2], B_CM],
                       [PSB[pst]], inc=True)
                rs, Brs = rstd2[wi % 2], B_rstd2[wi % 2]
                ACT(rs[:, :], PS[pst][:, 0:PWd], AF.Ln, [PSB[pst], B_eps], [Brs], bias=epsT[:, 0:1], scale=1.0)
                ACT(rs[:, :], rs[:, :], AF.Exp, [Brs], [Brs], scale=-0.5)

            pcnt = 0
            pstats(0)
            for wi, w in enumerate(wins):
                t0 = FS + FW * w
                t1 = t0 + FW
                r0 = t0 - 16
                dpar = wi % 2
                rs, Brs = rstd2[wi % 2], B_rstd2[wi % 2]
                if wi + 1 < len(wins):
                    pstats(wi + 1)
                fin = {}

                def st0(ci):
                    c = CH_ORDER[ci]
                    bi4 = ci % NB4
                    hb, Bh = h1[bi4], B_h1[bi4]
                    STT(xn[bi4][:, :], X[:, c, r0:t1], Amod[:, 1, 0, c:c + 1], rs[:, :], ALU.mult, ALU.mult,
                        xb(c, r0, t1) + [B_Amod[1][0], Brs], [B_xn[bi4]])
                    ACT(hb[:, :], xn[bi4][:, :], AF.Identity, [B_xn[bi4], B_modv[1][0]], [Bh], bias=modcol(1, 0, c), scale=1.0)
                    if w == 0:
                        TT(hb[:, 0:48], hb[:, 0:48], PK[:, PK_VMASK + 208:PK_VMASK + 256], ALU.mult, [Bh, B_PK], [Bh])

                def st1(ci):
                    c = CH_ORDER[ci]
                    bi4 = ci % NB4
                    g = c // 2
                    aeng = "pool" if c >= 3 else "dve"
                    src, Bsrc = h1[bi4], B_h1[bi4]
                    bufs = [(sA[bi4], B_sA[bi4]), (sB[bi4], B_sB[bi4])]
                    for step in range(g + 1):
                        sh = 1 << step
                        lo = (1 << (step + 1)) - 1
                        dstb, Bd = bufs[step % 2]
                        TT(dstb[:, lo:PWd], src[:, lo:PWd], src[:, lo - sh:PWd - sh], ALU.add, [Bsrc], [Bd], eng=aeng)
                        src, Bsrc = dstb, Bd
                    fin[ci] = (src, Bsrc)

                def st2(ci):
                    c = CH_ORDER[ci]
                    bi4 = ci % NB4
                    g = c // 2
                    hb, Bh = h1[bi4], B_h1[bi4]
                    src, Bsrc = fin[ci]
                    wd = float(1 << (g + 1))
                    STT(dT[dpar][:, c, :], src[:, 16:PWd], 1.0 / wd, hb[:, 16:PWd], ALU.mult, ALU.subtract,
                        [Bsrc, Bh], [B_dT[dpar][c]])
                    if w == 0:
                        pf, Bp = pfix[c % 2], B_pfix[c % 2]
                        TT(pf[:, :], src[:, 48:64], PK[:, PK_PINV + c * 16:PK_PINV + (c + 1) * 16], ALU.mult, [Bsrc, B_PK], [Bp])
                        TT(dT[dpar][:, c, 32:48], pf[:, :], hb[:, 48:64], ALU.subtract, [Bp, Bh, B_dT[dpar][c]], [B_dT[dpar][c]])

                for it in range(NCH + 3):
                    if 0 <= it - 3 < NCH:
                        st2(it - 3)
                    if 0 <= it - 1 < NCH:
                        st1(it - 1)
                    if it < NCH:
                        st0(it)
                for g in range(4):
                    for eo in range(2):
                        pq = pcnt % 2
                        pcnt += 1
                        ch = 2 * g + eo
                        for ci in range(2):
                            MM(PS[pq][:, 0:FW], pw[:, g, ci, eo * 128:(eo + 1) * 128], dT[dpar][:, 2 * g + ci, :], ci == 0, ci == 1,
                               [B_pw, B_dT[dpar][2 * g + ci]], [PSB[pq]], inc=(ci == 1))
                        STT(X[:, ch, t0:t1], PS[pq][:, 0:FW], Gm1[:, ch:ch + 1], X[:, ch, t0:t1], ALU.mult, ALU.add,
                            [PSB[pq], B_Gm1] + xb(ch, t0, t1), xb(ch, t0, t1))
            S.barrier()
            if stop_after == "mix1":
                dump_and_finish(); done = True
        if not done:
            ffn(1, None, final=True)

        S.finish()
        S.emit()
    return nc


def _const_tables():
    ident = np.eye(128, dtype=np.float32)
    onesm = np.full((128, 128), 1.0 / 1024.0, np.float32)
    bd = np.zeros((128, 128), np.float32)
    bd[0:64, 0:64] = 1.0 / 64.0
    bd[64:128, 64:128] = 1.0 / 64.0
    perm = np.zeros((128, 128), np.float32)
    for m in range(128):
        d = m % 64
        if d < 8:
            perm[m + 8, m] = 1.0
        elif d < 16:
            perm[m - 8, m] = 1.0
    cmat = np.stack([ident, onesm, bd, perm], axis=1)
    j = np.arange(128)[:, None]
    q = np.arange(128)[None, :]
    NEG = -1.0e4
    cur = np.where(j <= q, 0.0, NEG).astype(np.float32)
    prev = np.where(j > q, 0.0, NEG).astype(np.float32)
    cur4 = np.tile(cur, (1, 4))
    prev4 = np.tile(prev, (1, 4))
    allneg = np.full((128, 512), NEG, np.float32)
    return np.ascontiguousarray(cmat), cur4, prev4, allneg


def _rope_table(pos0):
    half = 8
    inv_freq = (500000.0 ** (-np.arange(0, half, dtype=np.float32) * 2.0 / 16.0)).astype(np.float32)
    pos = (pos0 + np.arange(XT)).astype(np.float32)
    ang = pos[None, :] * inv_freq[:, None]
    cos = np.cos(ang).astype(np.float32)
    sin = np.sin(ang).astype(np.float32)
    cs = np.zeros((128, 2, XT), np.float32)
    cs[:, 0, :] = 1.0
    for p in range(128):
        d = p % 64
        if d < 8:
            cs[p, 0] = cos[d]
            cs[p, 1] = -sin[d]
        elif d < 16:
            cs[p, 0] = cos[d - 8]
            cs[p, 1] = sin[d - 8]
    return cs


_PROGRAM_CACHE = {}


def kernel(x, c, mod_w, mod_b, mix_norm_gain, ffn_norm_gain, w_qkv, q_gain, k_gain, sinks, w_o, pool_w,
           pool_scale, w_up, conv_w, conv_b, w_down, _stop_after=None):
    f32 = np.float32
    x = np.asarray(x, f32); c = np.asarray(c, f32)
    mod_w = np.asarray(mod_w, f32); mod_b = np.asarray(mod_b, f32)
    w_qkv = np.asarray(w_qkv, f32); w_o = np.asarray(w_o, f32)
    w_up = np.asarray(w_up, f32); w_down = np.asarray(w_down, f32)
    conv_w = np.asarray(conv_w, f32); conv_b = np.asarray(conv_b, f32)
    pool_w = np.asarray(pool_w, f32)

    modw_r = np.ascontiguousarray(mod_w.reshape(2, 8, 128, NPIECE, 256).transpose(0, 3, 2, 1, 4))
    wq = w_qkv[0]
    qcols = []
    for ci in range(8):
        hi, g = ci // 4, ci % 4
        lo_head = 8 * hi + g
        hi_head = 8 * hi + 4 + g
        qcols += list(range(lo_head * 64, lo_head * 64 + 64)) + list(range(hi_head * 64, hi_head * 64 + 64))
    wq_perm = np.concatenate([wq[:, qcols], wq[:, 1024:]], axis=1)
    wqkv_r = np.ascontiguousarray(wq_perm.reshape(8, 128, 1536).transpose(1, 0, 2))
    wo_r = np.ascontiguousarray(w_o[0].reshape(8, 128, 1024).transpose(1, 0, 2))
    poolw_r = np.ascontiguousarray(pool_w[0].reshape(4, 2, 128, 256).transpose(2, 0, 1, 3))
    wup_r = np.ascontiguousarray(w_up.reshape(2, 8, 128, 2, NPAIR, 128).transpose(0, 4, 2, 1, 3, 5))
    wdown_r = np.ascontiguousarray(w_down.reshape(2, NPAIR, 128, 1024).transpose(0, 2, 1, 3))
    cmat, cur4, prev4, allneg = _const_tables()
    qkrow = np.concatenate([np.asarray(q_gain, f32)[0], np.asarray(k_gain, f32)[0]])[None, :].astype(f32)

    convp = np.zeros((128, 2, 44, 4), f32)
    for l in range(2):
        convp[:, l, :, 0:3] = conv_w[l].reshape(3, 44, 128).transpose(2, 1, 0)
        convp[:, l, :, 3] = conv_b[l].reshape(44, 128).T
    gains = np.stack([np.asarray(mix_norm_gain, f32), np.asarray(ffn_norm_gain, f32)], axis=1)
    gains_r = gains.reshape(2, 2, 8, 128).transpose(3, 0, 1, 2).reshape(128, 32)
    modb_r = mod_b.reshape(2, 48, 128).transpose(2, 0, 1).reshape(128, 96)
    pidx = np.arange(128) % 64
    qkg = np.stack([np.asarray(q_gain, f32)[0][pidx], np.asarray(k_gain, f32)[0][pidx]], axis=1)
    sinks_bc = np.tile(np.asarray(sinks, f32)[0][None, :], (128, 1))
    psc = np.asarray(pool_scale, f32)[0].reshape(8, 128).T

    in_maps = []
    for core in range(8):
        b, q = core // 4, core % 4
        T0 = q * TOK
        xs = np.zeros((XT, D), f32)
        lo = T0 - HALO
        if lo >= 0:
            xs[:] = x[b, lo:T0 + TOK]
        else:
            xs[HALO:] = x[b, 0:TOK]
        pos = lo + np.arange(XT)
        pk = np.zeros((128, PK_N), f32)
        pk[:, PK_C:PK_C + 8] = c[b].reshape(8, 128).T
        pk[:, PK_MODB:PK_MODB + 96] = modb_r
        pk[:, PK_GAIN:PK_GAIN + 32] = gains_r
        pk[:, PK_QKG:PK_QKG + 2] = qkg
        pk[:, PK_SINK:PK_SINK + 16] = sinks_bc
        pk[:, PK_PSC:PK_PSC + 8] = psc
        pk[:, PK_CONV:PK_CONV + 352] = convp.reshape(128, 352)
        pk[:, PK_VMASK:PK_VMASK + 256] = (pos[0:256] >= 0).astype(f32)[None, :]
        pinv = np.zeros((8, 16), f32)
        for ch in range(8):
            wd = float(2 ** (ch // 2 + 1))
            pp = pos[256:272].astype(f32)
            pinv[ch] = 1.0 / np.minimum(pp + 1.0, wd)
        pk[:, PK_PINV:PK_PINV + 128] = pinv.reshape(1, 128)
        masks = np.stack([cur4, prev4, allneg if q == 0 else prev4], axis=1)
        in_maps.append({
            "xT": np.ascontiguousarray(xs.T),
            "pk": pk,
            "qkrow": qkrow,
            "cmat": cmat,
            "masks": np.ascontiguousarray(masks),
            "cs": _rope_table(lo),
            "modw": modw_r,
            "wqkv": wqkv_r,
            "wo": wo_r,
            "poolw": poolw_r,
            "wup": wup_r,
            "wdown": wdown_r,
        })

    key = _stop_after
    if key not in _PROGRAM_CACHE:
        _PROGRAM_CACHE[key] = build_program(_stop_after)
    nc = _PROGRAM_CACHE[key]
    res = run_bass_kernel_spmd(nc, in_maps, core_ids=list(range(8)))
    out = np.zeros((2, SEQ, D), f32)
    for core in range(8):
        b, q = core // 4, core % 4
        out[b, q * TOK:(q + 1) * TOK, :] = res.results[core]["yT"].T
    return out
```

```python
import contextlib
import math
import numpy as np
import concourse.bass as bass
import concourse.mybir as mybir
from concourse.bass_utils import run_bass_kernel_spmd

F32 = mybir.dt.float32
BF16 = mybir.dt.bfloat16
AF = mybir.ActivationFunctionType
ALU = mybir.AluOpType
AX = mybir.AxisListType

D = 1024
NCH = 8
SEQ = 8192
TOK = 2048
HALO = 256
XT = TOK + HALO
NBLK = XT // 128
FS = 224
FW = 416
NWIN = 5
DFF = 2816
NPAIR = 22
QUARTERS = [(0, 6), (6, 6), (12, 5), (17, 5)]
MIX_TILES = [(0, 1, False), (1, 3, True), (4, 3, True), (7, 3, True), (10, 3, True), (13, 3, True), (16, 2, True)]
MW = 384
NPIECE = 24
EPS = 1e-6

PK_C = 0
PK_MODB = 8
PK_GAIN = 104
PK_QKG = 136
PK_SINK = 138
PK_PSC = 154
PK_CONV = 162
PK_VMASK = 514
PK_PINV = 770
PK_N = 898

SAME_ENG_SYNC = True
NSLOT = 6


class Buf:
    __slots__ = ("name", "w", "rs")

    def __init__(self, name=""):
        self.name = name
        self.w = None
        self.rs = {}


class Sched:
    ENGS = ("pe", "act", "dve", "pool", "sp")

    def __init__(self, nc):
        self.nc = nc
        self.streams = {e: [] for e in self.ENGS}
        self.cnt = {e: 0 for e in self.ENGS}
        self.waited = {e: {} for e in self.ENGS}
        self.dma_cnt = {"sp": 0, "pool": 0}
        self.semval = {}
        for q in ("sp", "pool"):
            for s in range(NSLOT):
                self.semval["d%s%d" % (q, s)] = 0

    def _deps(self, eng, reads, writes):
        need = {}

        def add(ev, war):
            if ev is None:
                return
            k, v = ev
            if k == eng:
                if eng == "pe" or not SAME_ENG_SYNC:
                    return
            if k in self.cnt and v > self.cnt[k]:
                raise RuntimeError("forward dependency on pending %s event (deadlock): %s" % (k, eng))
            if v > need.get(k, 0):
                need[k] = v

        for b in reads:
            add(b.w, False)
        for b in writes:
            add(b.w, False)
            for k, v in b.rs.items():
                add((k, v), True)
        return need

    def _commit_waits(self, eng, need):
        wt = self.waited[eng]
        waits = []
        for k, v in need.items():
            if v > wt.get(k, 0):
                wt[k] = v
                waits.append((k, v))
        return waits

    def _mark(self, ev, reads, writes):
        k, v = ev
        for b in reads:
            if v > b.rs.get(k, 0):
                b.rs[k] = v
        for b in writes:
            b.w = ev
            b.rs = {}

    def op(self, eng, fn, reads=(), writes=(), inc=True):
        need = self._deps(eng, reads, writes)
        waits = self._commit_waits(eng, need)
        if inc:
            self.cnt[eng] += 1
            ev = (eng, self.cnt[eng])
        else:
            ev = (eng, self.cnt[eng] + 1)
        self._mark(ev, reads, writes)
        self.streams[eng].append((waits, fn, "inc" if inc else None))

    def dma(self, q, fn, reads=(), writes=()):
        i = self.dma_cnt[q]
        self.dma_cnt[q] += 1
        key = "d%s%d" % (q, i % NSLOT)
        need = self._deps(q, reads, writes)
        pv = self.semval[key]
        if pv > 0:
            need[key] = max(need.get(key, 0), pv)
        waits = self._commit_waits(q, need)
        self.semval[key] = pv + 16
        ev = (key, pv + 16)
        self._mark(ev, reads, writes)
        self.streams[q].append((waits, fn, key))

    def barrier(self):
        need = {e: self.cnt[e] for e in self.ENGS if self.cnt[e] > 0}
        for k, v in self.semval.items():
            if v > 0:
                need[k] = v
        for e in self.ENGS:
            n2 = {k: v for k, v in need.items() if k != e}
            waits = self._commit_waits(e, n2)
            if waits:
                self.streams[e].append((waits, None, None))

    def finish(self):
        need = {k: v for k, v in self.semval.items() if v > 0}
        waits = self._commit_waits("sp", need)
        if waits:
            self.streams["sp"].append((waits, None, None))

    def emit(self):
        nc = self.nc
        names = list(self.ENGS) + sorted(self.semval.keys())
        with contextlib.ExitStack() as st:
            sems = {n: st.enter_context(nc.semaphore("s_" + n)) for n in names}
            block = st.enter_context(nc.Block())

            def run(e, eng):
                for waits, fn, mode in self.streams[e]:
                    for k, v in waits:
                        eng.wait_ge(sems[k], v)
                    if fn is None:
                        continue
                    ins = fn(eng)
                    if mode == "inc":
                        ins.then_inc(sems[e], 1)
                    elif mode is not None:
                        ins.then_inc(sems[mode], 16)

            @block.tensor
            def _(eng):
                run("pe", eng)

            @block.scalar
            def _(eng):
                run("act", eng)

            @block.vector
            def _(eng):
                run("dve", eng)

            @block.gpsimd
            def _(eng):
                run("pool", eng)

            @block.sync
            def _(eng):
                run("sp", eng)


def build_program(stop_after=None):
    nc = bass.Bass("TRN2", target_bir_lowering=False)

    def dram(name, shape, kind="ExternalInput", dt=F32):
        return nc.dram_tensor(name, list(shape), dt, kind=kind).ap()

    xT_d = dram("xT", [D, XT])
    pk_d = dram("pk", [128, PK_N])
    qkrow_d = dram("qkrow", [1, 128])
    cmat_d = dram("cmat", [128, 4, 128])
    masks_d = dram("masks", [128, 3, 512])
    cs_d = dram("cs", [128, 2, XT])
    modw_d = dram("modw", [2, NPIECE, 128, 8, 256])
    wqkv_d = dram("wqkv", [128, 8, 1536])
    wo_d = dram("wo", [128, 8, 1024])
    poolw_d = dram("poolw", [128, 4, 2, 256])
    wup_d = dram("wup", [2, NPAIR, 128, 8, 2, 128])
    wdown_d = dram("wdown", [2, 128, NPAIR, 1024])
    yT_d = dram("yT", [D, TOK], kind="ExternalOutput")
    yT_v = yT_d.rearrange("(c p) t -> p c t", p=128)
    xT_v = xT_d.rearrange("(c p) t -> p c t", p=128)

    S = Sched(nc)
    st = contextlib.ExitStack()
    with st:
        def sb(name, shape, dt):
            return st.enter_context(nc.sbuf_tensor(name, list(shape), dt))

        X = sb("X", [128, NCH, XT], F32)
        PK = sb("PK", [128, PK_N], F32)
        CM = sb("CM", [128, 4, 128], BF16)
        MK = sb("MK", [128, 3, 512], BF16)
        cact = sb("cact", [128, 8], BF16)
        modv = sb("modv", [128, 2, 48], F32)
        Amod = sb("Amod", [128, 2, 2, 8], F32)
        Gm1 = sb("Gm1", [128, 8], F32)
        negc = sb("negc", [128, 1], F32)
        esink = sb("esink", [128, 16], F32)
        epsT = sb("epsT", [128, 1], F32)
        qkrow = sb("qkrow_s", [1, 128], F32)
        onesrow = sb("onesrow", [1, 128], BF16)
        c1 = sb("c1", [1, 2], F32)
        c1b = sb("c1b", [1, 2], BF16)
        den = [sb("den%d" % i, [128, 4], F32) for i in range(2)]
        pfix = [sb("pfix%d" % i, [128, 16], F32) for i in range(2)]
        NMW = 4
        mwbuf = [sb("mwbuf%d" % i, [128, 8, 256], BF16) for i in range(NMW)]
        AW = 26800
        ARENA = sb("ARENA", [128, AW], F32)
        PS = [st.enter_context(nc.psum_tensor("PS%d" % i, [128, 512], F32)) for i in range(8)]
        PSB = [Buf("PS%d" % i) for i in range(8)]

        XB = [[Buf("X%d_%d" % (c, b)) for b in range(NBLK)] for c in range(NCH)]

        def xb(c, t0, t1):
            return XB[c][t0 // 128:(t1 - 1) // 128 + 1]

        B_PK = Buf("PK"); B_CM = Buf("CM"); B_MK = Buf("MK"); B_cact = Buf("cact")
        B_modv = [[Buf("modv%d_%d" % (l, i)) for i in range(6)] for l in range(2)]
        B_Amod = [[Buf(), Buf()], [Buf(), Buf()]]
        B_Gm1 = Buf(); B_negc = Buf(); B_esink = Buf(); B_eps = Buf()
        B_misc = Buf("misc")
        B_den = [Buf(), Buf()]; B_pfix = [Buf(), Buf()]
        B_mw = [Buf() for _ in range(4)]

        ident = CM[:, 0, :]
        onesm = CM[:, 1, :]
        bdm = CM[:, 2, :]
        permm = CM[:, 3, :]

        class Carver:
            def __init__(self):
                self.off = 0

            def take(self, shape, dt):
                n = 1
                for s in shape[1:]:
                    n *= s
                words = (n + 1) // 2 if dt == BF16 else n
                a = ARENA[:, self.off:self.off + words]
                self.off += words
                assert self.off <= AW, "arena overflow %d" % self.off
                if dt == BF16:
                    a = a.bitcast(BF16)[:, 0:n]
                if len(shape) == 3:
                    a = a.rearrange("p (a b) -> p a b", a=shape[1])
                elif len(shape) == 4:
                    a = a.rearrange("p (a b c) -> p a b c", a=shape[1], b=shape[2])
                return a

        def ACT(out, in_, func, reads, writes, bias=None, scale=None):
            kw = {}
            if bias is not None:
                kw["bias"] = bias
            if scale is not None:
                kw["scale"] = scale
            S.op("act", lambda e: e.activation(out=out, in_=in_, func=func, **kw), reads, writes)

        def STT(out, in0, scalar, in1, op0, op1, reads, writes, eng="dve"):
            S.op(eng, lambda e: e.scalar_tensor_tensor(out=out, in0=in0, scalar=scalar, in1=in1, op0=op0, op1=op1),
                 reads, writes)

        def TT(out, in0, in1, op, reads, writes, eng="dve"):
            S.op(eng, lambda e: e.tensor_tensor(out=out, in0=in0, in1=in1, op=op), reads, writes)

        def TS(out, in0, s1, s2, op0, op1, reads, writes, eng="dve"):
            if op1 is None:
                S.op(eng, lambda e: e.tensor_scalar(out=out, in0=in0, scalar1=s1, scalar2=None, op0=op0), reads, writes)
            else:
                S.op(eng, lambda e: e.tensor_scalar(out=out, in0=in0, scalar1=s1, scalar2=s2, op0=op0, op1=op1),
                     reads, writes)

        def MM(out, lhsT, rhs, start, stop, reads, writes, inc):
            S.op("pe", lambda e: e.matmul(out, lhsT=lhsT, rhs=rhs, start=start, stop=stop), reads, writes, inc=inc)

        def modcol(l, which, c):
            return modv[:, l, which * 8 + c:which * 8 + c + 1]

        S.dma("sp", lambda e: e.dma_start(out=PK[:], in_=pk_d), writes=[B_PK])
        S.dma("sp", lambda e: e.dma_start(out=qkrow[:], in_=qkrow_d), writes=[B_misc])
        S.dma("pool", lambda e: e.dma_start(out=CM[:], in_=cmat_d), writes=[B_CM])
        S.dma("pool", lambda e: e.dma_start(out=MK[:], in_=masks_d), writes=[B_MK])
        xtiles = [(0, 512), (512, 1024), (1024, 1536), (1536, 2048), (2048, 2304)]
        for (a, b) in xtiles[:1]:
            S.dma("sp", lambda e, a=a, b=b: e.dma_start(out=X[:, :, a:b], in_=xT_v[:, :, a:b]),
                  writes=[bb for c in range(NCH) for bb in xb(c, a, b)])

        S.op("dve", lambda e: e.memset(epsT[:], EPS), writes=[B_eps])
        S.op("dve", lambda e: e.memset(onesrow[:], 1.0), writes=[B_misc])
        ACT(cact[:], PK[:, PK_C:PK_C + 8], AF.Silu, [B_PK], [B_cact])

        def mod_job(l, PMOD):
            def issue(piece):
                S.dma("pool", lambda e, piece=piece: e.dma_start(out=mwbuf[piece % NMW][:], in_=modw_d[l, piece]),
                      writes=[B_mw[piece % NMW]])
            for p0 in range(NMW - 1):
                issue(p0)
            for piece in range(NPIECE):
                if piece + NMW - 1 < NPIECE:
                    issue(piece + NMW - 1)
                mb = mwbuf[piece % NMW]
                for m in range(2):
                    for k in range(8):
                        MM(PS[PMOD][:, m:m + 1], mb[:, k, m * 128:(m + 1) * 128], cact[:, k:k + 1],
                           k == 0, k == 7, [B_mw[piece % NMW], B_cact], [PSB[PMOD]], inc=(m == 1 and k == 7))
                which = piece // 4
                TT(modv[:, l, 2 * piece:2 * piece + 2], PS[PMOD][:, 0:2],
                   PK[:, PK_MODB + l * 48 + 2 * piece:PK_MODB + l * 48 + 2 * piece + 2], ALU.add,
                   [PSB[PMOD], B_PK], [B_modv[l][which]])
                if piece in (7, 19):
                    mf = 0 if piece == 7 else 1
                    wsc = 1 if mf == 0 else 4
                    STT(Amod[:, l, mf, :], modv[:, l, wsc * 8:wsc * 8 + 8], 1.0,
                        PK[:, PK_GAIN + (l * 2 + mf) * 8:PK_GAIN + (l * 2 + mf) * 8 + 8], ALU.add, ALU.mult,
                        [B_modv[l][wsc], B_PK], [B_Amod[l][mf]])
                if piece == 11 and l == 1:
                    TT(Gm1[:], modv[:, 1, 16:24], PK[:, PK_PSC:PK_PSC + 8], ALU.mult, [B_modv[1][2], B_PK], [B_Gm1])
                yield piece

        def run_job(job, n):
            for _ in range(n):
                try:
                    next(job)
                except StopIteration:
                    return

        job0 = mod_job(0, 4)
        job1 = mod_job(1, 7)

        TT(qkrow[:], qkrow[:], qkrow[:], ALU.mult, [B_misc], [B_misc])
        S.op("dve", lambda e: e.tensor_reduce(out=c1[:, 0:1], in_=qkrow[:], axis=AX.X, op=ALU.max), [B_misc], [B_misc])
        TS(c1b[:, 0:1], c1[:, 0:1], -8.0, None, ALU.mult, None, [B_misc], [B_misc])
        MM(PS[2][:, 0:1], onesrow[:, :], c1b[:, 0:1], True, True, [B_misc], [PSB[2]], inc=True)
        S.op("dve", lambda e: e.tensor_copy(out=negc[:], in_=PS[2][:, 0:1]), [PSB[2]], [B_negc])
        ACT(esink[:], PK[:, PK_SINK:PK_SINK + 16], AF.Exp, [B_PK, B_negc], [B_esink], bias=negc[:, 0:1], scale=1.0)

        run_job(job0, 8)

        def norm_tile(l, mf, t0, t1, xsq, B_xsq, rstd, B_rstd, xn, B_xn, PST, dst_fn, dst_bufs_fn, post=None,
                      sq_eng="pool"):
            W = t1 - t0
            for c in range(NCH):
                if sq_eng == "act":
                    ACT(xsq[c % 2][:, 0:W], X[:, c, t0:t1], AF.Square, xb(c, t0, t1), [B_xsq[c % 2]])
                else:
                    TT(xsq[c % 2][:, 0:W], X[:, c, t0:t1], X[:, c, t0:t1], ALU.mult, xb(c, t0, t1), [B_xsq[c % 2]],
                       eng=sq_eng)
                MM(PS[PST][:, 0:W], onesm, xsq[c % 2][:, 0:W], c == 0, c == NCH - 1, [B_xsq[c % 2], B_CM],
                   [PSB[PST]], inc=True)
            ACT(rstd[:, 0:W], PS[PST][:, 0:W], AF.Ln, [PSB[PST], B_eps], [B_rstd], bias=epsT[:, 0:1], scale=1.0)
            ACT(rstd[:, 0:W], rstd[:, 0:W], AF.Exp, [B_rstd], [B_rstd], scale=-0.5)
            wsh = 0 if mf == 0 else 3
            for c in range(NCH):
                STT(xn[c % 2][:, 0:W], X[:, c, t0:t1], Amod[:, l, mf, c:c + 1], rstd[:, 0:W], ALU.mult, ALU.mult,
                    xb(c, t0, t1) + [B_Amod[l][mf], B_rstd], [B_xn[c % 2]])
                ACT(dst_fn(c), xn[c % 2][:, 0:W], AF.Identity, [B_xn[c % 2], B_modv[l][wsh]], dst_bufs_fn(c),
                    bias=modcol(l, wsh, c), scale=1.0)
                if post is not None:
                    post(c)

        cv = Carver()
        wqkv = cv.take([128, 8, 1536], BF16); B_wqkv = Buf()
        wo = cv.take([128, 8, 1024], BF16); B_wo = Buf()
        h0 = cv.take([128, 8, MW], BF16); B_h0 = [Buf() for _ in range(8)]
        qt = [cv.take([128, 8, MW], BF16) for _ in range(2)]; B_qt = [[Buf() for _ in range(8)] for _ in range(2)]
        kT = [cv.take([128, 2, 128 + MW], BF16) for _ in range(2)]; B_kT = [[Buf(), Buf()], [Buf(), Buf()]]
        Vt = [cv.take([128, 4, 4, 65], BF16) for _ in range(2)]; B_V = [[Buf() for _ in range(4)] for _ in range(2)]
        xsq = [cv.take([128, MW], BF16) for _ in range(2)]; B_xsq = [Buf(), Buf()]
        rstd = cv.take([128, MW], F32); B_rstd = Buf()
        xn = [cv.take([128, MW], F32) for _ in range(2)]; B_xn = [Buf(), Buf()]
        sq = [cv.take([128, MW], BF16) for _ in range(2)]; B_sq = [Buf(), Buf()]
        lnr = [cv.take([128, MW], F32) for _ in range(2)]; B_lnr = [Buf(), Buf()]
        qn = [cv.take([128, MW], BF16) for _ in range(2)]; B_qn = [Buf(), Buf()]
        t1b = [cv.take([128, MW], F32) for _ in range(2)]; B_t1 = [Buf(), Buf()]
        t2b = [cv.take([128, MW], F32) for _ in range(2)]; B_t2 = [Buf(), Buf()]
        csb = [cv.take([128, 2, MW], F32) for _ in range(2)]; B_cs = [Buf(), Buf()]
        PT = [cv.take([128, 2, 512], BF16) for _ in range(2)]; B_PT = [[Buf(), Buf()], [Buf(), Buf()]]
        otok = [cv.take([128, 1024], BF16) for _ in range(2)]; B_otok = [[Buf() for _ in range(4)] for _ in range(2)]
        oT = cv.take([128, 8, MW], BF16); B_oT = [Buf() for _ in range(3)]

        S.dma("pool", lambda e: e.dma_start(out=wqkv, in_=wqkv_d), writes=[B_wqkv])
        for (a, b) in xtiles[1:]:
            S.dma("sp", lambda e, a=a, b=b: e.dma_start(out=X[:, :, a:b], in_=xT_v[:, :, a:b]),
                  reads=[B_Amod[0][0]], writes=[bb for c in range(NCH) for bb in xb(c, a, b)])
        S.dma("pool", lambda e: e.dma_start(out=wo, in_=wo_d), writes=[B_wo])
        for par in range(2):
            S.op("dve", lambda e, par=par: e.memset(Vt[par], 1.0), writes=B_V[par])

        PTR = PS[7][:].bitcast(BF16).rearrange("p (c t) -> p c t", c=8)
        ABANK = [0, 1, 2]
        PSTAT = 3
        PPERM = 4
        RING = [5, 6]
        POUT = 7
        SQ_ENG = "act"
        NSQ_ENG = "pool"
        T1_ENG = "pool"
        ADD_ENG = "dve"

        def qk_stream(ti):
            b0, nb, has_q = MIX_TILES[ti]
            par = ti % 2
            t0 = b0 * 128
            W = nb * 128
            t1 = t0 + W
            if ti > 0:
                pnb = MIX_TILES[ti - 1][1]
                for kc in range(2):
                    S.op("dve", lambda e, kc=kc: e.tensor_copy(
                        out=kT[par][:, kc, 0:128], in_=kT[1 - par][:, kc, pnb * 128:pnb * 128 + 128]),
                        [B_kT[1 - par][kc]], [B_kT[par][kc]])
                S.op("dve", lambda e: e.tensor_copy(out=Vt[par][:, 0, :, 0:64], in_=Vt[1 - par][:, pnb, :, 0:64]),
                     [B_V[1 - par][pnb]], [B_V[par][0]])
            S.dma("sp", lambda e: e.dma_start(out=csb[par][:, :, 0:W], in_=cs_d[:, :, t0:t1]), writes=[B_cs[par]])
            fastn = ti <= 1
            for c in range(NCH):
                if NSQ_ENG == "act" or fastn:
                    ACT(xsq[c % 2][:, 0:W], X[:, c, t0:t1], AF.Square, xb(c, t0, t1), [B_xsq[c % 2]])
                else:
                    TT(xsq[c % 2][:, 0:W], X[:, c, t0:t1], X[:, c, t0:t1], ALU.mult, xb(c, t0, t1), [B_xsq[c % 2]], eng=NSQ_ENG)
                MM(PS[PSTAT][:, 0:W], onesm, xsq[c % 2][:, 0:W], c == 0, c == NCH - 1, [B_xsq[c % 2], B_CM],
                   [PSB[PSTAT]], inc=True)
                if c % 2 == 1:
                    yield
            ACT(rstd[:, 0:W], PS[PSTAT][:, 0:W], AF.Ln, [PSB[PSTAT], B_eps], [B_rstd], bias=epsT[:, 0:1], scale=1.0)
            ACT(rstd[:, 0:W], rstd[:, 0:W], AF.Exp, [B_rstd], [B_rstd], scale=-0.5)
            yield
            for c in range(NCH):
                if fastn:
                    STT(xn[c % 2][:, 0:W], X[:, c, t0:t1], Amod[:, 0, 0, c:c + 1], rstd[:, 0:W], ALU.mult, ALU.mult,
                        xb(c, t0, t1) + [B_Amod[0][0], B_rstd], [B_xn[c % 2]])
                    ACT(h0[:, c, 0:W], xn[c % 2][:, 0:W], AF.Identity, [B_xn[c % 2], B_modv[0][0]], [B_h0[c]],
                        bias=modcol(0, 0, c), scale=1.0)
                else:
                    TT(xn[c % 2][:, 0:W], X[:, c, t0:t1], rstd[:, 0:W], ALU.mult, xb(c, t0, t1) + [B_rstd], [B_xn[c % 2]],
                       eng="pool")
                    TS(h0[:, c, 0:W], xn[c % 2][:, 0:W], Amod[:, 0, 0, c:c + 1], modcol(0, 0, c), ALU.mult, ALU.add,
                       [B_xn[c % 2], B_Amod[0][0], B_modv[0][0]], [B_h0[c]], eng="pool")
                if c % 2 == 1:
                    yield
            for bi in range(nb):
                pq = ABANK[bi % 3]
                for k in range(8):
                    MM(PS[pq][:, 0:256], h0[:, k, bi * 128:(bi + 1) * 128], wqkv[:, k, 1280:1536], k == 0, k == 7,
                       [B_wqkv, B_h0[k]], [PSB[pq]], inc=(k == 7))
                S.op("dve", lambda e, pq=pq, bi=bi: e.tensor_copy(
                    out=Vt[par][:, 1 + bi, :, 0:64], in_=PS[pq][:, 0:256].rearrange("p (h d) -> p h d", h=4)),
                    [PSB[pq]], [B_V[par][1 + bi]])
                yield
            ocs = ([("q", i) for i in range(8)] if has_q else []) + [("k", 0), ("k", 1)]
            n = len(ocs)

            def A_(i):
                kind, oc = ocs[i]
                col = oc * 128 if kind == "q" else 1024 + oc * 128
                pq = ABANK[i % 3]
                for k in range(8):
                    MM(PS[pq][:, 0:W], wqkv[:, k, col:col + 128], h0[:, k, 0:W], k == 0, k == 7,
                       [B_wqkv, B_h0[k]], [PSB[pq]], inc=(k == 7))

            def SQ_(i):
                pq = ABANK[i % 3]
                if SQ_ENG == "act":
                    ACT(sq[i % 2][:, 0:W], PS[pq][:, 0:W], AF.Square, [PSB[pq]], [B_sq[i % 2]])
                else:
                    TT(sq[i % 2][:, 0:W], PS[pq][:, 0:W], PS[pq][:, 0:W], ALU.mult, [PSB[pq]], [B_sq[i % 2]], eng=SQ_ENG)

            def ST_(i):
                MM(PS[PSTAT][:, 0:W], bdm, sq[i % 2][:, 0:W], True, True, [B_CM, B_sq[i % 2]], [PSB[PSTAT]], inc=True)

            def LN_(i):
                ACT(lnr[i % 2][:, 0:W], PS[PSTAT][:, 0:W], AF.Ln, [PSB[PSTAT], B_eps], [B_lnr[i % 2]], bias=epsT[:, 0:1], scale=1.0)
                ACT(lnr[i % 2][:, 0:W], lnr[i % 2][:, 0:W], AF.Exp, [B_lnr[i % 2]], [B_lnr[i % 2]], scale=-0.5)

            def QN_(i):
                kind, oc = ocs[i]
                pq = ABANK[i % 3]
                gcol = PK_QKG if kind == "q" else PK_QKG + 1
                STT(qn[i % 2][:, 0:W], PS[pq][:, 0:W], PK[:, gcol:gcol + 1], lnr[i % 2][:, 0:W], ALU.mult, ALU.mult,
                    [PSB[pq], B_PK, B_lnr[i % 2]], [B_qn[i % 2]])

            def PM_(i):
                MM(PS[PPERM][:, 0:W], permm, qn[i % 2][:, 0:W], True, True, [B_CM, B_qn[i % 2]], [PSB[PPERM]], inc=True)

            def T1_(i):
                TT(t1b[i % 2][:, 0:W], qn[i % 2][:, 0:W], csb[par][:, 0, 0:W], ALU.mult, [B_qn[i % 2], B_cs[par]],
                   [B_t1[i % 2]], eng=T1_ENG)

            def T2_(i):
                TT(t2b[i % 2][:, 0:W], PS[PPERM][:, 0:W], csb[par][:, 1, 0:W], ALU.mult, [PSB[PPERM], B_cs[par]], [B_t2[i % 2]])

            def AD_(i):
                kind, oc = ocs[i]
                if kind == "q":
                    dst, dbuf = qt[par][:, oc, 0:W], B_qt[par][oc]
                else:
                    dst, dbuf = kT[par][:, oc, 128:128 + W], B_kT[par][oc]
                TT(dst, t1b[i % 2][:, 0:W], t2b[i % 2][:, 0:W], ALU.add, [B_t1[i % 2], B_t2[i % 2]], [dbuf], eng=ADD_ENG)

            for s in range(n + 4):
                if 0 <= s - 1 < n: SQ_(s - 1)
                if 0 <= s - 4 < n: T2_(s - 4)
                if 0 <= s - 4 < n: AD_(s - 4)
                if 0 <= s - 2 < n: LN_(s - 2)
                if 0 <= s - 2 < n: QN_(s - 2)
                if 0 <= s < n: A_(s)
                if 0 <= s - 1 < n: ST_(s - 1)
                if 0 <= s - 3 < n: PM_(s - 3)
                if 0 <= s - 3 < n: T1_(s - 3)
                yield

        PTH = [PT[0][:, 0, :], PT[0][:, 1, :], PT[1][:, 0, :], PT[1][:, 1, :]]
        B_PTH = [Buf() for _ in range(4)]
        B_POUT = [PSB[7]] * 3
        B_otk = [[Buf() for _ in range(8)] for _ in range(2)]

        def attn_stream(ti):
            b0, nb, has_q = MIX_TILES[ti]
            par = ti % 2
            t0 = b0 * 128
            W = nb * 128
            t1 = t0 + W
            units = [(bi, h, hf) for bi in range(nb) for h in range(4) for hf in range(2)]
            NV = len(units)

            def SC_(v):
                bi, h, hf = units[v]
                gb = b0 + bi
                mprev = MK[:, 2, 0:256] if gb == 2 else MK[:, 1, 0:256]
                mcur = MK[:, 0, 0:256]
                hp = (h % 2) * 64
                kc = h // 2
                bank = RING[v % 2]
                rqh = qt[par][hp:hp + 64, 4 * kc + 2 * hf:4 * kc + 2 * hf + 2, bi * 128:(bi + 1) * 128]
                qbufs = [B_qt[par][4 * kc + 2 * hf + g] for g in range(2)]
                MM(PS[bank][:, 0:256], kT[par][hp:hp + 64, kc, bi * 128:bi * 128 + 128], rqh, True, False,
                   [B_kT[par][kc]] + qbufs, [PSB[bank]], inc=False)
                MM(PS[bank][:, 0:256], ident, mprev, False, True, [B_CM, B_MK], [PSB[bank]], inc=False)
                MM(PS[bank][:, 256:512], kT[par][hp:hp + 64, kc, 128 + bi * 128:256 + bi * 128], rqh, True, False,
                   [B_kT[par][kc]] + qbufs, [PSB[bank]], inc=False)
                MM(PS[bank][:, 256:512], ident, mcur, False, True, [B_CM, B_MK], [PSB[bank]], inc=True)

            def EX_(v):
                bank = RING[v % 2]
                ACT(PTH[v % 4], PS[bank][:, :], AF.Exp, [PSB[bank], B_negc], [B_PTH[v % 4]], bias=negc[:, 0:1], scale=0.125)

            def PV_(v):
                bi, h, hf = units[v]
                r = v % 3
                pt = PTH[v % 4]
                for g in range(2):
                    o0 = r * 130 + g * 65
                    MM(PS[POUT][:, o0:o0 + 65], pt[:, g * 128:(g + 1) * 128], Vt[par][:, bi, h, :],
                       True, False, [B_PTH[v % 4], B_V[par][bi]], [B_POUT[r]], inc=False)
                    MM(PS[POUT][:, o0:o0 + 65], pt[:, 256 + g * 128:256 + (g + 1) * 128], Vt[par][:, bi + 1, h, :],
                       False, True, [B_PTH[v % 4], B_V[par][bi + 1]], [B_POUT[r]], inc=(g == 1))

            def NM_(v):
                bi, h, hf = units[v]
                r = v % 3
                oi = bi % 2
                ov = PS[POUT][:, r * 130:(r + 1) * 130].rearrange("p (g d) -> p g d", g=2)
                di = v % 2
                qh0 = 4 * h + 2 * hf
                TT(den[di][:, 0:2], ov[:, :, 64], esink[:, qh0:qh0 + 2], ALU.add, [B_POUT[r], B_esink], [B_den[di]])
                S.op("dve", lambda e: e.reciprocal(out=den[di][:, 0:2], in_=den[di][:, 0:2]), [B_den[di]], [B_den[di]])
                TT(otok[oi][:, 64 * qh0:64 * qh0 + 128].rearrange("p (g d) -> p g d", g=2), ov[:, :, 0:64],
                   den[di][:, 0:2].unsqueeze(2).to_broadcast([128, 2, 64]), ALU.mult, [B_POUT[r], B_den[di]],
                   [B_otk[oi][2 * h + hf]])
                if h == 3 and hf == 1:
                    for c in range(8):
                        S.op("pe", lambda e, c=c: e.transpose(out=PTR[:, c, :], in_=otok[oi][:, c * 128:(c + 1) * 128],
                                                              identity=ident),
                             [B_otk[oi][c], B_CM], B_POUT, inc=(c == 7))
                    ACT(oT[:, :, bi * 128:(bi + 1) * 128], PTR, AF.Copy, B_POUT, [B_oT[bi]])

            SC_(0)
            for v in range(NV):
                if v + 1 < NV:
                    SC_(v + 1)
                EX_(v)
                PV_(v)
                NM_(v)
                if v % 2 == 1:
                    yield
            for e_ in range(8):
                pq = RING[e_ % 2]
                for f in range(8):
                    MM(PS[pq][:, 0:W], wo[:, f, e_ * 128:(e_ + 1) * 128], oT[:, f, 0:W], f == 0, f == 7,
                       [B_wo] + B_oT[0:nb], [PSB[pq]], inc=(f == 7))
                STT(X[:, e_, t0:t1], PS[pq][:, 0:W], modcol(0, 2, e_), X[:, e_, t0:t1], ALU.mult, ALU.add,
                    [PSB[pq], B_modv[0][2]] + xb(e_, t0, t1), xb(e_, t0, t1))
                yield

        def drain(g):
            for _ in g:
                pass

        def interleave(gens):
            gens = [g for g in gens if g is not None]
            while gens:
                for g in list(gens):
                    try:
                        next(g)
                    except StopIteration:
                        gens.remove(g)

        def job_slices(job, n):
            for _ in range(n):
                run_job(job, 1)
                yield
                yield
                yield

        drain(qk_stream(0))
        run_job(job0, 4)
        drain(qk_stream(1))
        NT_ = len(MIX_TILES)
        for ti in range(1, NT_):
            interleave([qk_stream(ti + 1) if ti + 1 < NT_ else None, attn_stream(ti), job_slices(job0, 2)])
        run_job(job0, NPIECE)
        S.barrier()

        def ffn(l, job, final):
            cv = Carver()
            hF = cv.take([128, 8, NWIN * FW + 2], BF16)
            B_hF = [[Buf() for _ in range(NWIN)] for _ in range(8)]
            abuf = cv.take([128, 6, NWIN * FW], BF16)
            B_a = [[Buf() for _ in range(NWIN)] for _ in range(6)]
            wub = [cv.take([128, 8, 2, 128], BF16) for _ in range(3)]; B_wub = [Buf() for _ in range(3)]
            wdb = cv.take([128, 6, 1024], BF16); B_wdb = Buf()
            tg = [cv.take([128, FW], F32) for _ in range(2)]; B_tg = [Buf(), Buf()]
            tv = [cv.take([128, FW], F32) for _ in range(2)]; B_tv = [Buf(), Buf()]
            sg = [cv.take([128, FW], F32) for _ in range(2)]; B_sg = [Buf(), Buf()]
            xsq = [cv.take([128, FW + 2], BF16) for _ in range(2)]; B_xsq = [Buf(), Buf()]
            rstd = cv.take([128, FW + 2], F32); B_rstd = Buf()
            xn = [cv.take([128, FW + 2], F32) for _ in range(2)]; B_xn = [Buf(), Buf()]

            def issue_wup(j, cnt):
                slot = cnt % 3
                S.dma("pool", lambda e, j=j, slot=slot: e.dma_start(out=wub[slot], in_=wup_d[l, j]), writes=[B_wub[slot]])

            issue_wup(0, 0)
            issue_wup(1, 1)
            def norm_win(w):
                t0 = FS + FW * w - (2 if w == 0 else 0)
                t1 = FS + FW * (w + 1)
                h0c = t0 - (FS - 2)
                norm_tile(l, 1, t0, t1, xsq, B_xsq, rstd, B_rstd, xn, B_xn, 6,
                          lambda c, h0c=h0c, t0=t0, t1=t1: hF[:, c, h0c:h0c + (t1 - t0)], lambda c, w=w: [B_hF[c][w]])
                if w == 0:
                    for c in range(8):
                        TT(hF[:, c, 0:34], hF[:, c, 0:34], PK[:, PK_VMASK + 222:PK_VMASK + 256], ALU.mult,
                           [B_hF[c][0], B_PK], [B_hF[c][0]])
            norm_win(0)
            cnt = 0
            ev = 0
            for qi, (j0, nq) in enumerate(QUARTERS):
                S.dma("pool", lambda e, j0=j0, nq=nq: e.dma_start(out=wdb[:, 0:nq, :], in_=wdown_d[l, :, j0:j0 + nq, :]),
                      writes=[B_wdb])
                for jl in range(nq):
                    j = j0 + jl
                    if j + 2 < NPAIR:
                        issue_wup(j + 2, cnt + 2)
                    slot = cnt % 3
                    cnt += 1
                    for w in range(NWIN):
                        if j == 0 and w + 1 < NWIN:
                            norm_win(w + 1)
                        pg = ev % 2
                        pv = 2 + ev % 2
                        ev += 1
                        hreads = lambda k: [B_hF[k][w]] + ([B_hF[k][w - 1]] if w > 0 else [])
                        for k in range(8):
                            MM(PS[pg][:, 0:FW + 2], wub[slot][:, k, 0, :], hF[:, k, FW * w:FW * w + FW + 2], k == 0, k == 7,
                               [B_wub[slot]] + hreads(k), [PSB[pg]], inc=(k == 7))
                        for k in range(8):
                            MM(PS[pv][:, 0:FW + 2], wub[slot][:, k, 1, :], hF[:, k, FW * w:FW * w + FW + 2], k == 0, k == 7,
                               [B_wub[slot]] + hreads(k), [PSB[pv]], inc=(k == 7))
                        ti = ev % 2

                        def cp(cc, i):
                            o = PK_CONV + (l * 44 + cc) * 4 + i
                            return PK[:, o:o + 1]
                        for (pp, tb, Bt, cc) in ((pg, tg[ti], B_tg[ti], j), (pv, tv[ti], B_tv[ti], 22 + j)):
                            ACT(tb[:, :], PS[pp][:, 2:FW + 2], AF.Identity, [PSB[pp], B_PK], [Bt], bias=cp(cc, 3), scale=cp(cc, 2))
                            STT(tb[:, :], PS[pp][:, 1:FW + 1], cp(cc, 1), tb[:, :], ALU.mult, ALU.add, [PSB[pp], B_PK, Bt], [Bt])
                            STT(tb[:, :], PS[pp][:, 0:FW], cp(cc, 0), tb[:, :], ALU.mult, ALU.add, [PSB[pp], B_PK, Bt], [Bt])
                        ACT(sg[ti][:, :], tg[ti][:, :], AF.Silu, [B_tg[ti]], [B_sg[ti]])
                        TT(abuf[:, jl, FW * w:FW * (w + 1)], sg[ti][:, :], tv[ti][:, :], ALU.mult, [B_sg[ti], B_tv[ti]],
                           [B_a[jl][w]], eng="pool")
                    if job is not None:
                        run_job(job, 2 if j < 2 else 1)
                dcnt = 0
                for w in range(NWIN):
                    a0 = FS + FW * w
                    a1 = a0 + FW
                    for e_ in range(8):
                        pd = 4 + dcnt % 2
                        dcnt += 1
                        for jl in range(nq):
                            MM(PS[pd][:, 0:FW], wdb[:, jl, e_ * 128:(e_ + 1) * 128], abuf[:, jl, FW * w:FW * (w + 1)],
                               jl == 0, jl == nq - 1, [B_wdb, B_a[jl][w]], [PSB[pd]], inc=(jl == nq - 1))
                        STT(X[:, e_, a0:a1], PS[pd][:, 0:FW], modcol(l, 5, e_), X[:, e_, a0:a1], ALU.mult, ALU.add,
                            [PSB[pd], B_modv[l][5]] + xb(e_, a0, a1), xb(e_, a0, a1))
                    if final and qi == len(QUARTERS) - 1:
                        r0 = max(a0, HALO)
                        S.dma("sp", lambda e, r0=r0, a1=a1: e.dma_start(out=yT_v[:, :, r0 - HALO:a1 - HALO], in_=X[:, :, r0:a1]),
                              reads=[bb for c in range(NCH) for bb in xb(c, r0, a1)])
            if job is not None:
                run_job(job, NPIECE)

        def dump_and_finish():
            for (a, b) in ((256, 768), (768, 1280), (1280, 1792), (1792, 2304)):
                S.dma("sp", lambda e, a=a, b=b: e.dma_start(out=yT_v[:, :, a - HALO:b - HALO], in_=X[:, :, a:b]),
                      reads=[bb for c in range(NCH) for bb in xb(c, a, b)])

        done = False
        if stop_after == "mix0":
            dump_and_finish(); done = True
        if not done:
            ffn(0, job1, final=False)
            S.barrier()
            if stop_after == "ffn0":
                dump_and_finish(); done = True

        if not done:
            cv = Carver()
            PWd = FW + 16
            NB4 = 4
            pw = cv.take([128, 4, 2, 256], BF16); B_pw = Buf()
            xsq = [cv.take([128, PWd], BF16) for _ in range(2)]; B_xsq = [Buf(), Buf()]
            rstd2 = [cv.take([128, PWd], F32) for _ in range(2)]; B_rstd2 = [Buf(), Buf()]
            xn = [cv.take([128, PWd], F32) for _ in range(NB4)]; B_xn = [Buf() for _ in range(NB4)]
            h1 = [cv.take([128, PWd], F32) for _ in range(NB4)]; B_h1 = [Buf() for _ in range(NB4)]
            sA = [cv.take([128, PWd], F32) for _ in range(NB4)]; B_sA = [Buf() for _ in range(NB4)]
            sB = [cv.take([128, PWd], F32) for _ in range(NB4)]; B_sB = [Buf() for _ in range(NB4)]
            dT = [cv.take([128, 8, FW], BF16) for _ in range(2)]; B_dT = [[Buf() for _ in range(8)] for _ in range(2)]
            S.dma("pool", lambda e: e.dma_start(out=pw, in_=poolw_d), writes=[B_pw])
            CH_ORDER = [0, 4, 1, 5, 2, 6, 3, 7]
            wins = list(range(NWIN - 1, -1, -1))

            def pstats(wi):
                w = wins[wi]
                r0 = FS + FW * w - 16
                t1 = FS + FW * (w + 1)
                pst = 6 + wi % 2
                for c in range(NCH):
                    ACT(xsq[c % 2][:, 0:PWd], X[:, c, r0:t1], AF.Square, xb(c, r0, t1), [B_xsq[c % 2]])
                    MM(PS[pst][:, 0:PWd], onesm, xsq[c % 2][:, 0:PWd], c == 0, c == NCH - 1, [B_xsq[c % 2], B_CM],
                       [PSB[pst]], inc=True)
                rs, Brs = rstd2[wi % 2], B_rstd2[wi % 2]
                ACT(rs[:, :], PS[pst][:, 0:PWd], AF.Ln, [PSB[pst], B_eps], [Brs], bias=epsT[:, 0:1], scale=1.0)
                ACT(rs[:, :], rs[:, :], AF.Exp, [Brs], [Brs], scale=-0.5)

            pending = []
            pstats(0)
            for wi, w in enumerate(wins):
                t0 = FS + FW * w
                t1 = t0 + FW
                r0 = t0 - 16
                dpar = wi % 2
                rs, Brs = rstd2[wi % 2], B_rstd2[wi % 2]
                if wi + 1 < len(wins):
                    pstats(wi + 1)
                fin = {}

                def st0(ci):
                    c = CH_ORDER[ci]
                    bi4 = ci % NB4
                    hb, Bh = h1[bi4], B_h1[bi4]
                    STT(xn[bi4][:, :], X[:, c, r0:t1], Amod[:, 1, 0, c:c + 1], rs[:, :], ALU.mult, ALU.mult,
                        xb(c, r0, t1) + [B_Amod[1][0], Brs], [B_xn[bi4]])
                    ACT(hb[:, :], xn[bi4][:, :], AF.Identity, [B_xn[bi4], B_modv[1][0]], [Bh], bias=modcol(1, 0, c), scale=1.0)
                    if w == 0:
                        TT(hb[:, 0:48], hb[:, 0:48], PK[:, PK_VMASK + 208:PK_VMASK + 256], ALU.mult, [Bh, B_PK], [Bh])

                def st1(ci):
                    c = CH_ORDER[ci]
                    bi4 = ci % NB4
                    g = c // 2
                    aeng = "pool" if c >= 3 else "dve"
                    src, Bsrc = h1[bi4], B_h1[bi4]
                    bufs = [(sA[bi4], B_sA[bi4]), (sB[bi4], B_sB[bi4])]
                    for step in range(g + 1):
                        sh = 1 << step
                        lo = (1 << (step + 1)) - 1
                        dstb, Bd = bufs[step % 2]
                        TT(dstb[:, lo:PWd], src[:, lo:PWd], src[:, lo - sh:PWd - sh], ALU.add, [Bsrc], [Bd], eng=aeng)
                        src, Bsrc = dstb, Bd
                    fin[ci] = (src, Bsrc)

                def st2(ci):
                    c = CH_ORDER[ci]
                    bi4 = ci % NB4
                    g = c // 2
                    hb, Bh = h1[bi4], B_h1[bi4]
                    src, Bsrc = fin[ci]
                    wd = float(1 << (g + 1))
                    STT(dT[dpar][:, c, :], src[:, 16:PWd], 1.0 / wd, hb[:, 16:PWd], ALU.mult, ALU.subtract,
                        [Bsrc, Bh], [B_dT[dpar][c]])
                    if w == 0:
                        pf, Bp = pfix[c % 2], B_pfix[c % 2]
                        TT(pf[:, :], src[:, 48:64], PK[:, PK_PINV + c * 16:PK_PINV + (c + 1) * 16], ALU.mult, [Bsrc, B_PK], [Bp])
                        TT(dT[dpar][:, c, 32:48], pf[:, :], hb[:, 48:64], ALU.subtract, [Bp, Bh, B_dT[dpar][c]], [B_dT[dpar][c]])

                for it in range(NCH + 3):
                    if it == 4 and pending:
                        pending.pop()()
                    if 0 <= it - 3 < NCH:
                        st2(it - 3)
                    if 0 <= it - 1 < NCH:
                        st1(it - 1)
                    if it < NCH:
                        st0(it)

                def mm_resid(dpar=dpar, t0=t0, t1=t1):
                    for g in range(4):
                        for eo in range(2):
                            pq = g % 2
                            ch = 2 * g + eo
                            for ci in range(2):
                                MM(PS[pq][:, 0:FW], pw[:, g, ci, eo * 128:(eo + 1) * 128], dT[dpar][:, 2 * g + ci, :], ci == 0, ci == 1,
                                   [B_pw, B_dT[dpar][2 * g + ci]], [PSB[pq]], inc=(ci == 1))
                            STT(X[:, ch, t0:t1], PS[pq][:, 0:FW], Gm1[:, ch:ch + 1], X[:, ch, t0:t1], ALU.mult, ALU.add,
                                [PSB[pq], B_Gm1] + xb(ch, t0, t1), xb(ch, t0, t1))
                pending.append(mm_resid)
            while pending:
                pending.pop()()
            S.barrier()
            if stop_after == "mix1":
                dump_and_finish(); done = True
        if not done:
            ffn(1, None, final=True)

        S.finish()
        S.emit()
    return nc


def _const_tables():
    ident = np.eye(128, dtype=np.float32)
    onesm = np.full((128, 128), 1.0 / 1024.0, np.float32)
    bd = np.zeros((128, 128), np.float32)
    bd[0:64, 0:64] = 1.0 / 64.0
    bd[64:128, 64:128] = 1.0 / 64.0
    perm = np.zeros((128, 128), np.float32)
    for m in range(128):
        d = m % 64
        if d < 8:
            perm[m + 8, m] = 1.0
        elif d < 16:
            perm[m - 8, m] = 1.0
    cmat = np.stack([ident, onesm, bd, perm], axis=1)
    j = np.arange(128)[:, None]
    q = np.arange(128)[None, :]
    NEG = -1.0e4
    cur = np.where(j <= q, 0.0, NEG).astype(np.float32)
    prev = np.where(j > q, 0.0, NEG).astype(np.float32)
    cur4 = np.tile(cur, (1, 4))
    prev4 = np.tile(prev, (1, 4))
    allneg = np.full((128, 512), NEG, np.float32)
    return np.ascontiguousarray(cmat), cur4, prev4, allneg


def _rope_table(pos0):
    half = 8
    inv_freq = (500000.0 ** (-np.arange(0, half, dtype=np.float32) * 2.0 / 16.0)).astype(np.float32)
    pos = (pos0 + np.arange(XT)).astype(np.float32)
    ang = pos[None, :] * inv_freq[:, None]
    cos = np.cos(ang).astype(np.float32)
    sin = np.sin(ang).astype(np.float32)
    cs = np.zeros((128, 2, XT), np.float32)
    cs[:, 0, :] = 1.0
    for p in range(128):
        d = p % 64
        if d < 8:
            cs[p, 0] = cos[d]
            cs[p, 1] = -sin[d]
        elif d < 16:
            cs[p, 0] = cos[d - 8]
            cs[p, 1] = sin[d - 8]
    return cs


_PROGRAM_CACHE = {}


def kernel(x, c, mod_w, mod_b, mix_norm_gain, ffn_norm_gain, w_qkv, q_gain, k_gain, sinks, w_o, pool_w,
           pool_scale, w_up, conv_w, conv_b, w_down, _stop_after=None):
    f32 = np.float32
    x = np.asarray(x, f32); c = np.asarray(c, f32)
    mod_w = np.asarray(mod_w, f32); mod_b = np.asarray(mod_b, f32)
    w_qkv = np.asarray(w_qkv, f32); w_o = np.asarray(w_o, f32)
    w_up = np.asarray(w_up, f32); w_down = np.asarray(w_down, f32)
    conv_w = np.asarray(conv_w, f32); conv_b = np.asarray(conv_b, f32)
    pool_w = np.asarray(pool_w, f32)

    modw_r = np.ascontiguousarray(mod_w.reshape(2, 8, 128, NPIECE, 256).transpose(0, 3, 2, 1, 4))
    wq = w_qkv[0]
    qcols = []
    for ci in range(8):
        hi, g = ci // 4, ci % 4
        lo_head = 8 * hi + g
        hi_head = 8 * hi + 4 + g
        qcols += list(range(lo_head * 64, lo_head * 64 + 64)) + list(range(hi_head * 64, hi_head * 64 + 64))
    wq_perm = np.concatenate([wq[:, qcols], wq[:, 1024:]], axis=1)
    wqkv_r = np.ascontiguousarray(wq_perm.reshape(8, 128, 1536).transpose(1, 0, 2))
    wo_r = np.ascontiguousarray(w_o[0].reshape(8, 128, 1024).transpose(1, 0, 2))
    poolw_r = np.ascontiguousarray(pool_w[0].reshape(4, 2, 128, 256).transpose(2, 0, 1, 3))
    wup_r = np.ascontiguousarray(w_up.reshape(2, 8, 128, 2, NPAIR, 128).transpose(0, 4, 2, 1, 3, 5))
    wdown_r = np.ascontiguousarray(w_down.reshape(2, NPAIR, 128, 1024).transpose(0, 2, 1, 3))
    cmat, cur4, prev4, allneg = _const_tables()
    qkrow = np.concatenate([np.asarray(q_gain, f32)[0], np.asarray(k_gain, f32)[0]])[None, :].astype(f32)

    convp = np.zeros((128, 2, 44, 4), f32)
    for l in range(2):
        convp[:, l, :, 0:3] = conv_w[l].reshape(3, 44, 128).transpose(2, 1, 0)
        convp[:, l, :, 3] = conv_b[l].reshape(44, 128).T
    gains = np.stack([np.asarray(mix_norm_gain, f32), np.asarray(ffn_norm_gain, f32)], axis=1)
    gains_r = gains.reshape(2, 2, 8, 128).transpose(3, 0, 1, 2).reshape(128, 32)
    modb_r = mod_b.reshape(2, 48, 128).transpose(2, 0, 1).reshape(128, 96)
    pidx = np.arange(128) % 64
    qkg = np.stack([np.asarray(q_gain, f32)[0][pidx], np.asarray(k_gain, f32)[0][pidx]], axis=1)
    sinks_bc = np.tile(np.asarray(sinks, f32)[0][None, :], (128, 1))
    psc = np.asarray(pool_scale, f32)[0].reshape(8, 128).T

    in_maps = []
    for core in range(8):
        b, q = core // 4, core % 4
        T0 = q * TOK
        xs = np.zeros((XT, D), f32)
        lo = T0 - HALO
        if lo >= 0:
            xs[:] = x[b, lo:T0 + TOK]
        else:
            xs[HALO:] = x[b, 0:TOK]
        pos = lo + np.arange(XT)
        pk = np.zeros((128, PK_N), f32)
        pk[:, PK_C:PK_C + 8] = c[b].reshape(8, 128).T
        pk[:, PK_MODB:PK_MODB + 96] = modb_r
        pk[:, PK_GAIN:PK_GAIN + 32] = gains_r
        pk[:, PK_QKG:PK_QKG + 2] = qkg
        pk[:, PK_SINK:PK_SINK + 16] = sinks_bc
        pk[:, PK_PSC:PK_PSC + 8] = psc
        pk[:, PK_CONV:PK_CONV + 352] = convp.reshape(128, 352)
        pk[:, PK_VMASK:PK_VMASK + 256] = (pos[0:256] >= 0).astype(f32)[None, :]
        pinv = np.zeros((8, 16), f32)
        for ch in range(8):
            wd = float(2 ** (ch // 2 + 1))
            pp = pos[256:272].astype(f32)
            pinv[ch] = 1.0 / np.minimum(pp + 1.0, wd)
        pk[:, PK_PINV:PK_PINV + 128] = pinv.reshape(1, 128)
        masks = np.stack([cur4, prev4, allneg if q == 0 else prev4], axis=1)
        in_maps.append({
            "xT": np.ascontiguousarray(xs.T),
            "pk": pk,
            "qkrow": qkrow,
            "cmat": cmat,
            "masks": np.ascontiguousarray(masks),
            "cs": _rope_table(lo),
            "modw": modw_r,
            "wqkv": wqkv_r,
            "wo": wo_r,
            "poolw": poolw_r,
            "wup": wup_r,
            "wdown": wdown_r,
        })

    key = _stop_after
    if key not in _PROGRAM_CACHE:
        _PROGRAM_CACHE[key] = build_program(_stop_after)
    nc = _PROGRAM_CACHE[key]
    res = run_bass_kernel_spmd(nc, in_maps, core_ids=list(range(8)))
    out = np.zeros((2, SEQ, D), f32)
    for core in range(8):
        b, q = core // 4, core % 4
        out[b, q * TOK:(q + 1) * TOK, :] = res.results[core]["yT"].T
    return out
```
